# Optimizing a Trainium2 kernel written in Bass

```python
import jax
import jax.numpy as jnp
from jax import lax
import numpy as np

D_MODEL = 1024
BATCH = 16
SEQ = 256
DEPTH = 4
DEC_BATCH = 8
DEC_SEQ = 1024
PAST_LEN = 512

GRID_W = 64
D_HEAD = 64
ML_HEADS = D_MODEL // (4 * D_HEAD)
SWA_HEADS = (3 * D_MODEL) // (8 * D_HEAD)
SWA_KV_HEADS = SWA_HEADS // 3
NAT_HEADS = (3 * D_MODEL) // (8 * D_HEAD)
ML_W = ML_HEADS * D_HEAD
SWA_W = SWA_HEADS * D_HEAD
SWA_KV_W = SWA_KV_HEADS * D_HEAD
NAT_W = NAT_HEADS * D_HEAD
MIX_W = ML_W + SWA_W + NAT_W
N_GATES = 4 * ML_HEADS
IN_SIZES = (ML_W, ML_W, ML_W, ML_W, N_GATES, SWA_W, SWA_KV_W, SWA_KV_W, NAT_W, NAT_W, NAT_W)
IN_DIM = sum(IN_SIZES)
D_FF = 4 * D_MODEL
ML_CHUNK = 64
SWA_WINDOW = 128
SWA_BLOCK = 128
NAT_KH = 8
NAT_KW = 16
NAT_QB = 16
NAT_KC = NAT_QB + NAT_KW
ROPE_BASE = 10000.0
LN_EPS = 1e-5
DN_ALPHA = (2 * DEPTH) ** 0.25
DN_BETA = (8 * DEPTH) ** -0.25

kernel_name = 'hybrid_mlstm_swa_natten_flow_step'


def _layer_norm(x, g, b):
    xf = x.astype(jnp.float32)
    mu = jnp.mean(xf, axis=-1, keepdims=True)
    var = jnp.mean(jnp.square(xf - mu), axis=-1, keepdims=True)
    y = (xf - mu) * lax.rsqrt(var + LN_EPS) * g.astype(jnp.float32) + b.astype(jnp.float32)
    return y.astype(x.dtype)


def _heads(a, n_heads):
    bsz, t, _ = a.shape
    return a.reshape(bsz, t, n_heads, D_HEAD).transpose(0, 2, 1, 3)


def _merge(a):
    bsz, h, t, d = a.shape
    return a.transpose(0, 2, 1, 3).reshape(bsz, t, h * d)


def _rope_axis(x, pos):
    half = x.shape[-1] // 2
    freqs = ROPE_BASE ** (-jnp.arange(half, dtype=jnp.float32) / half)
    ang = pos.astype(jnp.float32)[:, None] * freqs[None, :]
    cos = jnp.cos(ang).astype(x.dtype)
    sin = jnp.sin(ang).astype(x.dtype)
    x1, x2 = x[..., :half], x[..., half:]
    return jnp.concatenate([x1 * cos - x2 * sin, x1 * sin + x2 * cos], axis=-1)


def _rope_2d(x):
    t = jnp.arange(x.shape[-2])
    half = x.shape[-1] // 2
    return jnp.concatenate([_rope_axis(x[..., :half], t // GRID_W),
                            _rope_axis(x[..., half:], t % GRID_W)], axis=-1)


def _softmax_with_sink(s, sink):
    s = s.astype(jnp.float32)
    sk = jnp.broadcast_to(sink.astype(jnp.float32), s.shape[:-1] + (1,))
    p = jax.nn.softmax(jnp.concatenate([s, sk], axis=-1), axis=-1)
    return p[..., :-1]


def _modulation(cvec, w_ada, b_ada):
    return jnp.split(jax.nn.silu(cvec) @ w_ada + b_ada, 6, axis=-1)


def _split_in(z):
    idx = [int(i) for i in np.cumsum(IN_SIZES)[:-1]]
    return jnp.split(z, idx, axis=-1)


def _mlstm_scan(q, k, v, li, lf, c0, n0, m0):
    bsz, nh, t, d = q.shape
    nc = t // ML_CHUNK

    def chunks(a):
        return jnp.moveaxis(a.reshape((bsz, nh, nc, ML_CHUNK) + a.shape[3:]), 2, 0)

    causal = jnp.tril(jnp.ones((ML_CHUNK, ML_CHUNK), dtype=bool))

    def step(carry, xs):
        c_st, n_st, m_st = carry
        qc, kc, vc, lic, lfc = xs
        b = jnp.cumsum(lfc, axis=-1)
        dmat = jnp.where(causal, b[..., :, None] - b[..., None, :] + lic[..., None, :], -jnp.inf)
        prev = b + m_st[..., None]
        mt = jnp.maximum(prev, jnp.max(dmat, axis=-1))
        wprev = jnp.exp(prev - mt)
        s = jnp.einsum('bhtd,bhsd->bhts', qc, kc) * jnp.exp(dmat - mt[..., None])
        num = s @ vc + wprev[..., None] * jnp.einsum('bhde,bhte->bhtd', c_st, qc)
        den = jnp.sum(s, axis=-1) + wprev * jnp.einsum('bhd,bhtd->bht', n_st, qc)
        h = num / jnp.maximum(jnp.abs(den), jnp.exp(-mt))[..., None]
        m_new = mt[..., -1]
        ws = jnp.exp(b[..., -1:] - b + lic - m_new[..., None])
        wc = wprev[..., -1]
        c_new = wc[..., None, None] * c_st + jnp.einsum('bhs,bhsd,bhse->bhde', ws, vc, kc)
        n_new = wc[..., None] * n_st + jnp.einsum('bhs,bhsd->bhd', ws, kc)
        return (c_new, n_new, m_new), h

    (c_f, n_f, m_f), hs = lax.scan(step, (c0, n0, m0), tuple(chunks(a) for a in (q, k, v, li, lf)))
    h = jnp.moveaxis(hs, 0, 2).reshape(bsz, nh, t, d)
    return h, c_f, n_f, m_f


def _mlstm_mixer(mq, mk, mv, mo, mg, fbias, norm_g, c0, n0, m0):
    out_dtype = mq.dtype
    bsz, t, _ = mq.shape
    f32 = jnp.float32
    q = _heads(mq.astype(f32), ML_HEADS)
    k = _heads(mk.astype(f32), ML_HEADS) * (D_HEAD ** -0.5)
    v = _heads(mv.astype(f32), ML_HEADS)
    g = mg.astype(f32).reshape(bsz, t, 4, ML_HEADS).transpose(2, 0, 3, 1)
    fb = fbias.astype(f32)
    li_f, lf_f = g[0], jax.nn.log_sigmoid(g[1] + fb[0][:, None])
    li_b, lf_b = g[2], jax.nn.log_sigmoid(g[3] + fb[1][:, None])
    c0, n0, m0 = c0.astype(f32), n0.astype(f32), m0.astype(f32)
    h_f, c_f, n_f, m_f = _mlstm_scan(q, k, v, li_f, lf_f, c0[:, 0], n0[:, 0], m0[:, 0])
    flip = lambda a: jnp.flip(a, axis=2)
    h_b, c_b, n_b, m_b = _mlstm_scan(flip(q), flip(k), flip(v), flip(li_b), flip(lf_b),
                                     c0[:, 1], n0[:, 1], m0[:, 1])
    h = h_f + flip(h_b)
    mu = jnp.mean(h, axis=-1, keepdims=True)
    var = jnp.mean(jnp.square(h - mu), axis=-1, keepdims=True)
    h = _merge((h - mu) * lax.rsqrt(var + LN_EPS))
    h = h * norm_g.astype(f32) * jax.nn.sigmoid(mo.astype(f32))
    return (h.astype(out_dtype), jnp.stack([c_f, c_b], axis=1),
            jnp.stack([n_f, n_b], axis=1), jnp.stack([m_f, m_b], axis=1))


def _swa_context(q, k, v, sink):
    bsz, hq, lq, d = q.shape
    grp = hq // SWA_KV_HEADS
    qg = q.reshape(bsz, SWA_KV_HEADS, grp, lq, d)
    s = jnp.einsum('bkgqd,bksd->bkgqs', qg, k) * (d ** -0.5)
    p = _softmax_with_sink(s, sink.reshape(1, SWA_KV_HEADS, grp, 1, 1)).astype(v.dtype)
    return jnp.einsum('bkgqs,bksd->bkgqd', p, v).reshape(bsz, hq, lq, d)


def _swa_latent(q, k, v, ck, cv, sink):
    bsz, hq, t, d = q.shape
    kv = SWA_KV_HEADS
    grp = hq // kv
    bl = SWA_BLOCK
    nb = t // bl
    scale = d ** -0.5
    qb = q.reshape(bsz, kv, grp, nb, bl, d)

    def band(a):
        ap = jnp.pad(a, ((0, 0), (0, 0), (bl, bl), (0, 0))).reshape(bsz, kv, nb + 2, bl, d)
        return jnp.concatenate([ap[:, :, 0:nb], ap[:, :, 1:nb + 1], ap[:, :, 2:nb + 2]], axis=3)

    kb, vb = band(k), band(v)
    qpos = np.arange(nb)[:, None, None] * bl + np.arange(bl)[None, :, None]
    kpos = np.arange(nb)[:, None, None] * bl - bl + np.arange(3 * bl)[None, None, :]
    valid = (np.abs(qpos - kpos) <= SWA_WINDOW) & (kpos >= 0) & (kpos < t)
    s_band = jnp.einsum('bkgnqd,bknsd->bkgnqs', qb, kb).astype(jnp.float32) * scale
    s_band = jnp.where(valid, s_band, -jnp.inf)
    s_ctx = jnp.einsum('bkgnqd,bkcd->bkgnqc', qb, ck).astype(jnp.float32) * scale
    p = _softmax_with_sink(jnp.concatenate([s_band, s_ctx], axis=-1),
                           sink.reshape(1, kv, grp, 1, 1, 1)).astype(v.dtype)
    o = (jnp.einsum('bkgnqs,bknsd->bkgnqd', p[..., :3 * bl], vb)
         + jnp.einsum('bkgnqc,bkcd->bkgnqd', p[..., 3 * bl:], cv))
    return o.reshape(bsz, hq, t, d)


def _dense_attention(q, k, v):
    s = jnp.einsum('bhqd,bhsd->bhqs', q, k) * (q.shape[-1] ** -0.5)
    p = jax.nn.softmax(s.astype(jnp.float32), axis=-1).astype(v.dtype)
    return jnp.einsum('bhqs,bhsd->bhqd', p, v)


def _nat_latent(q, k, v, ck, cv, rpb):
    bsz, nh, t, d = q.shape
    rows = t // GRID_W
    kh = min(NAT_KH, rows)
    ncb = GRID_W // NAT_QB
    scale = d ** -0.5
    r = np.arange(rows)
    rows_idx = np.clip(r - kh // 2, 0, rows - kh)[:, None] + np.arange(kh)[None, :]
    cb = np.arange(ncb)
    cols_idx = (np.clip(cb * NAT_QB - NAT_KW // 2, 0, GRID_W - NAT_KC)[:, None]
                + np.arange(NAT_KC)[None, :])
    qcol = cb[:, None] * NAT_QB + np.arange(NAT_QB)[None, :]
    cs = np.clip(qcol - NAT_KW // 2, 0, GRID_W - NAT_KW)
    kc3 = cols_idx[:, None, :]
    col_ok = (kc3 >= cs[..., None]) & (kc3 < cs[..., None] + NAT_KW)
    roff = rows_idx - r[:, None] + NAT_KH - 1
    coff = np.clip(kc3 - qcol[..., None] + NAT_KW - 1, 0, 2 * NAT_KW - 2)
    bias = rpb[:, roff[:, None, None, :, None], coff[None, :, :, None, :]]
    bias = jnp.where(col_ok[None, None, :, :, None, :], bias.astype(jnp.float32), -jnp.inf)
    qg = q.reshape(bsz, nh, rows, ncb, NAT_QB, d)
    ri = rows_idx[:, None, :, None]
    ci = cols_idx[None, :, None, :]
    kg = k.reshape(bsz, nh, rows, GRID_W, d)[:, :, ri, ci]
    vg = v.reshape(bsz, nh, rows, GRID_W, d)[:, :, ri, ci]
    s_nb = jnp.einsum('bhrcqd,bhrckwd->bhrcqkw', qg, kg).astype(jnp.float32) * scale + bias[None]
    n_nb = kh * NAT_KC
    s_nb = s_nb.reshape(bsz, nh, rows, ncb, NAT_QB, n_nb)
    s_ctx = jnp.einsum('bhrcqd,bhsd->bhrcqs', qg, ck).astype(jnp.float32) * scale
    p = jax.nn.softmax(jnp.concatenate([s_nb, s_ctx], axis=-1), axis=-1).astype(v.dtype)
    p_nb = p[..., :n_nb].reshape(bsz, nh, rows, ncb, NAT_QB, kh, NAT_KC)
    o = (jnp.einsum('bhrcqkw,bhrckwd->bhrcqd', p_nb, vg)
         + jnp.einsum('bhrcqs,bhsd->bhrcqd', p[..., n_nb:], cv))
    return o.reshape(bsz, nh, t, d)


def _residual_tail(x, mix_heads, g_a, sh_m, sc_m, g_m, w_out, ln1_g, ln1_b, w_mlp1, w_mlp2, ln2_g, ln2_b):
    mix = jnp.concatenate(mix_heads, axis=-1) @ w_out
    x = _layer_norm(DN_ALPHA * x + g_a * mix, ln1_g, ln1_b)
    h = x * (1.0 + sc_m) + sh_m
    f = jnp.square(jax.nn.relu(h @ w_mlp1)) @ w_mlp2
    return _layer_norm(DN_ALPHA * x + g_m * f, ln2_g, ln2_b)


def _context_layer(x, c_ctx, w_ada, b_ada, w_in, b_in, fbias, norm_g, sink,
                   w_out, ln1_g, ln1_b, w_mlp1, w_mlp2, ln2_g, ln2_b):
    sh_a, sc_a, g_a, sh_m, sc_m, g_m = _modulation(c_ctx, w_ada, b_ada)
    bsz = x.shape[0]
    h = x * (1.0 + sc_a) + sh_a
    mq, mk, mv, mo, mg, sq, sk, sv, nq, nk, nv = _split_in(h @ w_in + b_in)
    zc = jnp.zeros((bsz, 2, ML_HEADS, D_HEAD, D_HEAD), jnp.float32)
    zn = jnp.zeros((bsz, 2, ML_HEADS, D_HEAD), jnp.float32)
    zm = jnp.zeros((bsz, 2, ML_HEADS), jnp.float32)
    y_ml, st_c, st_n, st_m = _mlstm_mixer(mq, mk, mv, mo, mg, fbias, norm_g, zc, zn, zm)
    k_swa, v_swa = _heads(sk, SWA_KV_HEADS), _heads(sv, SWA_KV_HEADS)
    y_swa = _merge(_swa_context(_heads(sq, SWA_HEADS), k_swa, v_swa, sink))
    k_nat, v_nat = _heads(nk, NAT_HEADS), _heads(nv, NAT_HEADS)
    y_nat = _merge(_dense_attention(_heads(nq, NAT_HEADS), k_nat, v_nat))
    x = _residual_tail(x, [y_ml, y_swa, y_nat], g_a, sh_m, sc_m, g_m,
                       w_out, ln1_g, ln1_b, w_mlp1, w_mlp2, ln2_g, ln2_b)
    return x, (k_swa, v_swa, k_nat, v_nat, st_c, st_n, st_m)


def _latent_layer(x, c, ck_swa, cv_swa, ck_nat, cv_nat, st_c, st_n, st_m,
                  w_ada, b_ada, w_in, b_in, fbias, norm_g, sink, rpb,
                  w_out, ln1_g, ln1_b, w_mlp1, w_mlp2, ln2_g, ln2_b):
    mods = _modulation(c, w_ada, b_ada)
    sh_a, sc_a, g_a, sh_m, sc_m, g_m = [m[:, None, :] for m in mods]
    h = x * (1.0 + sc_a) + sh_a
    mq, mk, mv, mo, mg, sq, sk, sv, nq, nk, nv = _split_in(h @ w_in + b_in)
    y_ml, _, _, _ = _mlstm_mixer(mq, mk, mv, mo, mg, fbias, norm_g, st_c, st_n, st_m)
    y_swa = _merge(_swa_latent(_rope_2d(_heads(sq, SWA_HEADS)), _rope_2d(_heads(sk, SWA_KV_HEADS)),
                               _heads(sv, SWA_KV_HEADS), ck_swa, cv_swa, sink))
    y_nat = _merge(_nat_latent(_heads(nq, NAT_HEADS), _heads(nk, NAT_HEADS), _heads(nv, NAT_HEADS),
                               ck_nat, cv_nat, rpb))
    return _residual_tail(x, [y_ml, y_swa, y_nat], g_a, sh_m, sc_m, g_m,
                          w_out, ln1_g, ln1_b, w_mlp1, w_mlp2, ln2_g, ln2_b)


def setup_inputs(seed: int = 0) -> dict:
    key = jax.random.key(seed)
    ks = jax.random.split(key, 26)

    def nrm(k, shape, scale=1.0):
        return scale * jax.random.normal(k, shape, jnp.float32)

    d = D_MODEL
    return {
        'x_prompt': nrm(ks[0], (BATCH, SEQ, d)),
        'x_sample': nrm(ks[1], (DEC_BATCH, DEC_SEQ, d)),
        'cache_swa_k': nrm(ks[2], (DEC_BATCH, DEPTH, SWA_KV_HEADS, PAST_LEN, D_HEAD)),
        'cache_swa_v': nrm(ks[3], (DEC_BATCH, DEPTH, SWA_KV_HEADS, PAST_LEN, D_HEAD)),
        'cache_nat_k': nrm(ks[4], (DEC_BATCH, DEPTH, NAT_HEADS, PAST_LEN, D_HEAD)),
        'cache_nat_v': nrm(ks[5], (DEC_BATCH, DEPTH, NAT_HEADS, PAST_LEN, D_HEAD)),
        'state_mlstm_C': nrm(ks[6], (DEC_BATCH, DEPTH, 2, ML_HEADS, D_HEAD, D_HEAD), 0.1),
        'state_mlstm_n': nrm(ks[7], (DEC_BATCH, DEPTH, 2, ML_HEADS, D_HEAD), 0.5),
        'state_mlstm_m': nrm(ks[8], (DEC_BATCH, DEPTH, 2, ML_HEADS)),
        'c': nrm(ks[9], (DEC_BATCH, d)),
        'c_ctx': nrm(ks[10], (d,)),
        'w_ada': nrm(ks[11], (DEPTH, d, 6 * d), 0.5 * d ** -0.5),
        'b_ada': nrm(ks[12], (DEPTH, 6 * d), 0.02),
        'w_in': nrm(ks[13], (DEPTH, d, IN_DIM), d ** -0.5),
        'b_in': nrm(ks[14], (DEPTH, IN_DIM), 0.02),
        'mlstm_fbias': jnp.linspace(3.0, 6.0, ML_HEADS, dtype=jnp.float32) + nrm(ks[15], (DEPTH, 2, ML_HEADS), 0.1),
        'mlstm_norm_g': 1.0 + nrm(ks[16], (DEPTH, ML_W), 0.02),
        'swa_sink': nrm(ks[17], (DEPTH, SWA_HEADS), 0.5),
        'nat_rpb': nrm(ks[18], (DEPTH, NAT_HEADS, 2 * NAT_KH - 1, 2 * NAT_KW - 1), 0.1),
        'w_out': nrm(ks[19], (DEPTH, MIX_W, d), DN_BETA * MIX_W ** -0.5),
        'ln1_g': 1.0 + nrm(ks[20], (DEPTH, d), 0.02),
        'ln1_b': nrm(ks[21], (DEPTH, d), 0.02),
        'w_mlp1': nrm(ks[22], (DEPTH, d, D_FF), d ** -0.5),
        'w_mlp2': nrm(ks[23], (DEPTH, D_FF, d), DN_BETA * D_FF ** -0.5),
        'ln2_g': 1.0 + nrm(ks[24], (DEPTH, d), 0.02),
        'ln2_b': nrm(ks[25], (DEPTH, d), 0.02),
    }


def reference(x_prompt, x_sample, cache_swa_k, cache_swa_v, cache_nat_k, cache_nat_v,
              state_mlstm_C, state_mlstm_n, state_mlstm_m, c, c_ctx,
              w_ada, b_ada, w_in, b_in, mlstm_fbias, mlstm_norm_g, swa_sink, nat_rpb,
              w_out, ln1_g, ln1_b, w_mlp1, w_mlp2, ln2_g, ln2_b):
    xp = x_prompt
    xs = x_sample
    ks_l, vs_l, kn_l, vn_l, cs_l, ns_l, ms_l = [], [], [], [], [], [], []
    for l in range(DEPTH):
        xp, (k_swa, v_swa, k_nat, v_nat, st_c, st_n, st_m) = _context_layer(
            xp, c_ctx, w_ada[l], b_ada[l], w_in[l], b_in[l], mlstm_fbias[l], mlstm_norm_g[l],
            swa_sink[l], w_out[l], ln1_g[l], ln1_b[l], w_mlp1[l], w_mlp2[l], ln2_g[l], ln2_b[l])
        ks_l.append(k_swa)
        vs_l.append(v_swa)
        kn_l.append(k_nat)
        vn_l.append(v_nat)
        cs_l.append(st_c)
        ns_l.append(st_n)
        ms_l.append(st_m)
        xs = _latent_layer(
            xs, c, cache_swa_k[:, l], cache_swa_v[:, l], cache_nat_k[:, l], cache_nat_v[:, l],
            state_mlstm_C[:, l], state_mlstm_n[:, l], state_mlstm_m[:, l],
            w_ada[l], b_ada[l], w_in[l], b_in[l], mlstm_fbias[l], mlstm_norm_g[l], swa_sink[l], nat_rpb[l],
            w_out[l], ln1_g[l], ln1_b[l], w_mlp1[l], w_mlp2[l], ln2_g[l], ln2_b[l])
    new_swa_k = jnp.stack(ks_l, axis=1)
    new_swa_v = jnp.stack(vs_l, axis=1)
    new_nat_k = jnp.stack(kn_l, axis=1)
    new_nat_v = jnp.stack(vn_l, axis=1)
    new_mlstm_C = jnp.stack(cs_l, axis=1)
    new_mlstm_n = jnp.stack(ns_l, axis=1)
    new_mlstm_m = jnp.stack(ms_l, axis=1)
    return (xp, xs, new_swa_k, new_swa_v, new_nat_k, new_nat_v, new_mlstm_C, new_mlstm_n, new_mlstm_m)
```

```python
import contextlib
import numpy as np
import concourse.bass as bass
import concourse.mybir as mybir
from concourse.bass_utils import run_bass_kernel_spmd

F32 = mybir.dt.float32
BF16 = mybir.dt.bfloat16
ALU = mybir.AluOpType
AF = mybir.ActivationFunctionType
AX = mybir.AxisListType
AP = bass.AP

NL = 4
D = 1024
KC = 8
IN_DIM = 2832
DFF = 4096
ALPHA = (2 * NL) ** 0.25
EPS = 1e-5
NEG = -10000.0
SLOTW = 576
NSLOT = 4
ARENA_BYTES = 85 * 1024

C_ID, C_MF, C_MB, C_PM, C_MSH, C_MN = 0, 128, 256, 384, 512, 704
NCONST = 768


def make_consts():
    c = np.zeros((128, NCONST), np.float32)
    c[:, C_ID:C_ID + 128] = np.eye(128, dtype=np.float32)
    s = np.arange(128)[:, None]
    t = np.arange(128)[None, :]
    c[:, C_MF:C_MF + 128] = (t >= s)
    c[:, C_MB:C_MB + 128] = (t <= s)
    pm = np.zeros((128, 128), np.float32)
    for m in range(128):
        d = m % 64
        partner = d + 16 if (d % 32) < 16 else d - 16
        pm[(m // 64) * 64 + partner, m] = 1.0
    c[:, C_PM:C_PM + 128] = pm
    for j in range(31):
        c[j, C_MSH + j + 48] = 1.0
        c[j, C_MSH + j + 112] = 1.0
    kc = np.arange(64)[:, None]
    cc = np.arange(64)[None, :]
    cs = np.clip(cc - 8, 0, 48)
    ok = (kc >= cs) & (kc < cs + 16)
    mn = np.where(ok, 0.0, NEG * 8.0).astype(np.float32)
    c[0:64, C_MN:C_MN + 64] = mn
    c[64:128, C_MN:C_MN + 64] = mn
    tt = np.arange(1024)
    rope = np.zeros((128, 2048), np.float32)
    fr = 10000.0 ** (-np.arange(16, dtype=np.float32) / 16.0)
    for p in range(128):
        d = p % 64
        pos = (tt // 64) if d < 32 else (tt % 64)
        i = d % 16
        ang = pos.astype(np.float32) * fr[i]
        rope[p, :1024] = np.cos(ang)
        sn = np.sin(ang)
        rope[p, 1024:] = -sn if (d % 32) < 16 else sn
    return c, rope


class DSem:
    def __init__(self, sem):
        self.sem = sem
        self.count = 0


class Buf:
    __slots__ = ("name", "w", "rs", "dsem")

    def __init__(self, name, dsem=None, init=None):
        self.name = name
        self.w = None
        self.rs = dict(init) if init else {}
        self.dsem = dsem


class EngQ:
    def __init__(self, name):
        self.name = name
        self.items = []
        self.sem = None
        self.count = 0
        self.waited = {}


class Prog:
    def __init__(self, nc, stack):
        self.nc = nc
        self.stack = stack
        self.q = {n: EngQ(n) for n in ("pe", "act", "dve", "pool", "sp")}
        self.nsem = 0
        for n in ("pe", "act", "dve"):
            self.q[n].sem = self.new_sem("e_" + n)
        self.stores = []
        self.all_bufs = []

    def new_sem(self, name):
        self.nsem += 1
        return self.stack.enter_context(self.nc.semaphore(name + "_%d" % self.nsem))

    def new_dsem(self, name):
        return DSem(self.new_sem("d_" + name))

    def buf(self, name, dsem=None, init=None):
        b = Buf(name, dsem, init)
        self.all_bufs.append(b)
        return b

    def _deps(self, reads, writes, own_sem=None):
        deps = {}

        def add(tok):
            s, v = tok
            if deps.get(s, 0) < v:
                deps[s] = v
        for b in reads:
            if b.w is not None:
                add(b.w)
        rset = set(id(b) for b in reads)
        for b in writes:
            if b.w is not None and not (own_sem is not None and b.w[0] is own_sem and id(b) not in rset):
                add(b.w)
            for s, v in b.rs.items():
                add((s, v))
        return deps

    def _wait(self, q, deps):
        for s, v in deps.items():
            if q.waited.get(s, 0) < v:
                q.waited[s] = v
                q.items.append(("wait", s, v))

    def _commit(self, tok, reads, writes):
        s, v = tok
        for b in reads:
            if b.rs.get(s, 0) < v:
                b.rs[s] = v
        for b in writes:
            b.w = tok
            b.rs = {}

    def op(self, e, fn, reads=(), writes=()):
        q = self.q[e]
        self._wait(q, self._deps(reads, writes, own_sem=q.sem))
        if q.count >= 60000:
            q.sem = self.new_sem("e_" + e)
            q.count = 0
        q.count += 1
        tok = (q.sem, q.count)
        q.items.append(("op", fn, q.sem, 1))
        self._commit(tok, reads, writes)
        return tok

    def dma(self, e, fns, reads=(), writes=(), ds=None, store=False):
        q = self.q[e]
        self._wait(q, self._deps(reads, writes))
        if not isinstance(fns, (list, tuple)):
            fns = [fns]
        for fn in fns:
            ds.count += 16
            q.items.append(("op", fn, ds.sem, 16))
        tok = (ds.sem, ds.count)
        self._commit(tok, reads, writes)
        if store:
            self.stores.append(tok)
        return tok

    def retire(self, bufs):
        deps = {}
        for b in bufs:
            toks = list(b.rs.items())
            if b.w is not None:
                toks.append(b.w)
            for s, v in toks:
                if deps.get(s, 0) < v:
                    deps[s] = v
        return deps

    def replay(self, name, eng):
        for it in self.q[name].items:
            if it[0] == "wait":
                eng.wait_ge(it[1], it[2])
            else:
                ins = it[1](eng)
                ins.then_inc(it[2], it[3])


class Arena:
    def __init__(self, P, tile):
        self.P = P
        self.tile = tile
        self.off = 0
        self.cur = []
        self.cur_rng = []
        self.old = []

    def new_phase(self):
        known = set(id(b) for _, _, bl in self.cur_rng for b in bl)
        extra = [b for b in self.cur if id(b) not in known]
        if extra and self.cur_rng:
            lo = min(o for o, _, _ in self.cur_rng)
            hi = max(o + n for o, n, _ in self.cur_rng)
            self.cur_rng.append((lo, hi - lo, extra))
        self.old = self.old + self.cur_rng
        self.cur_rng = []
        self.cur = []
        self.off = 0

    def alloc(self, name, shape, dt, dsem=None):
        n = int(np.prod(shape))
        esz = 4 if dt == F32 else 2
        nb = (n * esz + 31) // 32 * 32
        assert self.off + nb <= ARENA_BYTES, (name, self.off, nb)
        a = self.tile[:, self.off // 2:(self.off + n * esz) // 2]
        if dt == F32:
            a = a.bitcast(F32)
        if len(shape) > 1:
            dims, stp = [], 1
            for sz in reversed(shape):
                dims.insert(0, [stp, sz])
                stp *= sz
            a = AP(a.tensor, a.offset, [list(a.ap[0])] + dims)
        lo, hi = self.off, self.off + nb
        ov = [bl for (o, n_, bl) in self.old if o < hi and o + n_ > lo]
        init = self.P.retire([b for bl in ov for b in bl])
        self.old = [(o, n_, bl) for (o, n_, bl) in self.old if not (o >= lo and o + n_ <= hi)]
        self.off += nb
        b = self.P.buf(name, dsem, init=init)
        self.cur.append(b)
        self.cur_rng.append((lo, nb, [b]))
        return a, b


def apx(a, dims, off=0):
    return AP(a.tensor, a.offset + off, [list(a.ap[0])] + [list(d) for d in dims])


IN_SPECS = [
    ("xp", [512, D]), ("xs", [1024, D]),
    ("ck_swa", [NL, 2, 512, 64]), ("cv_swa", [NL, 2, 512, 64]),
    ("ck_nat", [NL, 6, 512, 64]), ("cv_nat", [NL, 6, 512, 64]),
    ("stC", [NL, 2, 4, 64, 64]), ("stn", [NL, 2, 4, 64]), ("stm", [NL, 2, 4]),
    ("cvec", [2, D]),
    ("w_ada", [NL, D, 6 * D]), ("b_ada", [NL, 6 * D]), ("w_in", [NL, D, IN_DIM]), ("b_in", [NL, IN_DIM]),
    ("fbias", [NL, 2, 4]), ("norm_g", [NL, 256]), ("sink", [NL, 6]), ("rpb", [NL, 6, 15, 31]),
    ("w_out", [NL, D, D]), ("ln1_g", [NL, D]), ("ln1_b", [NL, D]),
    ("w_mlp1", [NL, D, DFF]), ("w_mlp2", [NL, DFF, D]), ("ln2_g", [NL, D]), ("ln2_b", [NL, D]),
    ("consts", [128, NCONST]), ("rope", [128, 2048]),
]
OUT_SPECS = [
    ("y_p", [512, D]), ("y_s", [1024, D]),
    ("o_swa_k", [2, NL, 2, 256, 64]), ("o_swa_v", [2, NL, 2, 256, 64]),
    ("o_nat_k", [2, NL, 6, 256, 64]), ("o_nat_v", [2, NL, 6, 256, 64]),
    ("o_C", [2, NL, 2, 4, 64, 64]), ("o_n", [2, NL, 2, 4, 64]), ("o_m", [2, NL, 2, 4]),
]


class _Stop(Exception):
    pass


def build_program(nlayers=NL, dbg=None, stop=None):
    nc = bass.Bass("TRN2", target_bir_lowering=False)
    I = {n: nc.dram_tensor(n, s, F32, kind="ExternalInput").ap() for n, s in IN_SPECS}
    O = {n: nc.dram_tensor(n, s, F32, kind="ExternalOutput").ap() for n, s in OUT_SPECS}
    if dbg:
        for n, s in dbg.items():
            O[n] = nc.dram_tensor(n, s, F32, kind="ExternalOutput").ap()
    with contextlib.ExitStack() as st:
        P = Prog(nc, st)

        def SB(name, shape, dt):
            return st.enter_context(nc.sbuf_tensor(name, shape, dt))

        c32 = SB("c32", [128, NCONST], F32); b_c32 = P.buf("c32", P.new_dsem("c32"))
        cb = SB("cb", [128, 3, 128], BF16); b_cb = P.buf("cb")
        onesb = SB("onesb", [128, 128], BF16); b_ones = P.buf("ones")
        ones32 = SB("ones32", [128, 128], F32)
        xa = SB("xa", [128, 12, D], F32)
        b_xa = [P.buf("xa%d" % i, P.new_dsem("xa%d" % i)) for i in range(12)]
        ring = SB("ring", [128, NSLOT, KC, 512], BF16)
        b_ring = [P.buf("ring%d" % i, P.new_dsem("ring%d" % i)) for i in range(NSLOT)]
        hT = SB("hT", [128, KC, 1024], BF16)
        b_hT = [P.buf("hT%d" % i) for i in range(8)]
        mixT = SB("mixT", [128, KC, 1024], BF16)
        b_mix = [P.buf("mix%d" % i) for i in range(8)]
        modT = SB("modT", [128, NL, 6, KC, 2], F32)
        b_modl = [P.buf("modT%d" % i) for i in range(NL)]
        badaT = SB("badaT", [128, NL, 48], F32); b_badaT = P.buf("badaT", P.new_dsem("badaT"))
        pv = SB("pv", [128, 6, KC, 2], F32); b_pv = P.buf("pv")
        siluT = SB("siluT", [128, KC, 2], BF16); b_silu = P.buf("silu")
        gw = SB("gw", [128, KC, 128], BF16); b_gw = P.buf("gw", P.new_dsem("gw"))
        small = SB("small", [128, 64], F32); b_small = P.buf("small", P.new_dsem("small"))
        binT = SB("binT", [128, 16], F32); b_binT = P.buf("binT", P.new_dsem("binT"))
        arena_t = SB("arena", [128, ARENA_BYTES // 2], BF16)
        AR = Arena(P, arena_t)
        psum = [st.enter_context(nc.psum_tensor("ps%d" % i, [128, 512], F32)) for i in range(8)]
        b_ps = [P.buf("ps%d" % i) for i in range(8)]
        ps_rr = [0]

        def PS():
            i = ps_rr[0] % 8
            ps_rr[0] += 1
            return psum[i], b_ps[i]

        ident32 = c32[:, C_ID:C_ID + 128]
        identb = cb[:, 0, :]
        maskFb = cb[:, 1, :]
        maskBb = cb[:, 2, :]

        ring_rr = [0]

        def ring_load(src_ap, ncols, nk=KC):
            i = ring_rr[0] % NSLOT
            ring_rr[0] += 1
            dst = ring[:, i, 0:nk, 0:ncols]
            P.dma("pool", lambda e: e.dma_start(out=dst, in_=src_ap), writes=[b_ring[i]], ds=b_ring[i].dsem)
            return ring[:, i], b_ring[i]

        def wpiece(w, l, c0, ncols, r0=0, nk=KC):
            return w[l, r0:r0 + nk * 128, c0:c0 + ncols].rearrange("(k p) n -> p k n", p=128)

        P.dma("sp", lambda e: e.dma_start(out=c32[:], in_=I["consts"][:, :]), writes=[b_c32], ds=b_c32.dsem)

        def setup_consts(e):
            e.tensor_copy(out=cb[:, 0, :], in_=c32[:, C_ID:C_ID + 128])
            e.tensor_copy(out=cb[:, 1, :], in_=c32[:, C_MF:C_MF + 128])
            e.tensor_copy(out=cb[:, 2, :], in_=c32[:, C_MB:C_MB + 128])
            e.memset(onesb[:], 1.0)
            e.memset(ones32[:], 1.0)
            e.memset(small[:], EPS)
            return e.memset(gw[:], 0.0)
        P.op("dve", setup_consts, reads=[b_c32], writes=[b_cb, b_ones, b_gw, b_small])

        for i in range(12):
            src = I["xp"][i * 128:(i + 1) * 128, :] if i < 4 else I["xs"][(i - 4) * 128:(i - 3) * 128, :]
            P.dma("sp", (lambda i, src: lambda e: e.dma_start(out=xa[:, i, :], in_=src))(i, src),
                  writes=[b_xa[i]], ds=b_xa[i].dsem)
            P.op("act", (lambda i: lambda e: e.activation(out=xa[:, i, :], in_=xa[:, i, :], func=AF.Copy, scale=ALPHA))(i),
                 reads=[b_xa[i]], writes=[b_xa[i]])

        AR.new_phase()
        cT, b_cT = AR.alloc("cT", [KC, 2], F32, P.new_dsem("cT"))
        sg, b_sg = AR.alloc("sg", [KC, 2], F32)
        with nc.allow_non_contiguous_dma(reason="tiny transposed vector load"):
            P.dma("sp", [(lambda v_: lambda e: e.dma_start(out=cT[:, :, v_], in_=I["cvec"][v_, :].rearrange("(k p) -> p k", p=128)))(v_) for v_ in range(2)],
                  writes=[b_cT], ds=b_cT.dsem)
        P.op("act", lambda e: e.activation(out=sg, in_=cT, func=AF.Sigmoid), reads=[b_cT], writes=[b_sg])
        P.op("dve", lambda e: e.tensor_tensor(out=siluT[:], in0=cT, in1=sg, op=ALU.mult), reads=[b_cT, b_sg], writes=[b_silu])

        with nc.allow_non_contiguous_dma(reason="per-partition bias layout"):
            P.dma("sp", [(lambda l_: lambda e: e.dma_start(out=badaT[:, l_, :], in_=I["b_ada"][l_, :].rearrange("(c p) -> p c", p=128)))(l_)
                         for l_ in range(nlayers)], writes=[b_badaT], ds=b_badaT.dsem)

        def ada_piece(l, j, rows2):
            slot, bsl = ring_load(wpiece(I["w_ada"], l, j * 512, 512), 512)
            ps, bp = PS()

            def ada_mm(e):
                for k in range(KC):
                    r = e.matmul(ps[0:2, :], lhsT=siluT[:, k, :], rhs=slot[:, k, :], start=(k == 0), stop=(k == KC - 1))
                return r
            P.op("pe", ada_mm, reads=[b_silu, bsl], writes=[bp])
            rw, brw = rows2[j % 2]
            P.op("act", lambda e: e.activation(out=rw[0:2, :], in_=ps[0:2, :], func=AF.Copy), reads=[bp], writes=[brw])
            ps2, bp2 = PS()

            def ada_tr(e):
                for qd in range(4):
                    r = e.matmul(ps2[:, qd * 2:qd * 2 + 2], lhsT=rw[0:2, qd * 128:(qd + 1) * 128],
                                 rhs=ident32[0:2, 0:2], start=True, stop=True)
                return r
            P.op("pe", ada_tr, reads=[brw, b_c32], writes=[bp2])
            k0 = (j % 2) * 4
            dstm = modT[:, l, j // 2, k0:k0 + 4, :]
            bsrc = apx(badaT[:, l, (j // 2) * 8 + k0:(j // 2) * 8 + k0 + 1], [[1, 4], [0, 2]])
            P.op("dve", lambda e: e.tensor_tensor(out=dstm, in0=ps2[:, 0:8].rearrange("p (a b) -> p a b", a=4), in1=bsrc, op=ALU.add),
                 reads=[bp2, b_badaT], writes=[b_modl[l]])

        def alloc_rows(tag):
            r_, b_r = AR.alloc("rows" + tag, [2, 512], F32)
            b_r1 = P.buf("rows1" + tag)
            AR.cur.append(b_r1)
            return [(r_[:, 0, :], b_r), (r_[:, 1, :], b_r1)]

        rows_up = alloc_rows("u")
        for j in range(12):
            ada_piece(0, j, rows_up)

        def chk(name):
            if stop == name:
                raise _Stop()

        dsems = {}

        def DS(name):
            if name not in dsems:
                dsems[name] = P.new_dsem(name)
            return dsems[name]

        def dram_bc(src1d, n):
            return AP(src1d.tensor, src1d.offset, [[0, 128], [1, n]])

        def transposes_to_hT(l, s_idx, b_idx, xchunks, v, j_off=0):
            for j0 in range(0, len(xchunks), 4):
                grp = xchunks[j0:j0 + 4]
                for k in range(KC):
                    ps, bp = PS()

                    def tr(e, ps=ps, grp=grp, k=k):
                        for jj, tc in enumerate(grp):
                            r = e.transpose(ps[:, jj * 128:(jj + 1) * 128], in_=xa[:, tc, k * 128:(k + 1) * 128], identity=ident32)
                        return r
                    P.op("pe", tr, reads=[b_xa[tc] for tc in grp] + [b_c32], writes=[bp])
                    n = len(grp) * 128
                    dst = hT[:, k, (j_off + j0) * 128:(j_off + j0) * 128 + n]
                    P.op("act", (lambda dst, ps, n, k: lambda e: e.activation(
                        out=dst, in_=ps[:, 0:n], func=AF.Identity,
                        scale=pv[:, s_idx, k, v:v + 1], bias=modT[:, l, b_idx, k, v:v + 1]))(dst, ps, n, k),
                        reads=[bp, b_pv, b_modl[l]], writes=b_hT[j_off + j0:j_off + j0 + len(grp)])

        def fm_proj(slot, bsl, c0, T, evac):
            for nt in range(T // 512):
                ps, bp = PS()

                def mm(e, ps=ps, nt=nt):
                    for k in range(KC):
                        r = e.matmul(ps[:, :], lhsT=slot[:, k, c0:c0 + 128], rhs=hT[:, k, nt * 512:(nt + 1) * 512],
                                     start=(k == 0), stop=(k == KC - 1))
                    return r
                P.op("pe", mm, reads=[bsl] + b_hT[nt * 4:(nt + 1) * 4], writes=[bp])
                evac(ps, bp, nt)

        def tm_proj(slot, bsl, c0, ncols, tcl, evac):
            ps, bp = PS()

            def mm(e, ps=ps):
                for k in range(KC):
                    r = e.matmul(ps[:, 0:ncols], lhsT=hT[:, k, tcl * 128:(tcl + 1) * 128], rhs=slot[:, k, c0:c0 + ncols],
                                 start=(k == 0), stop=(k == KC - 1))
                return r
            P.op("pe", mm, reads=[bsl, b_hT[tcl]], writes=[bp])
            evac(ps, bp)

        def load_bias_bc(name, l, c0, n):
            t, b = AR.alloc(name, [n], F32, DS(name))
            P.dma("sp", lambda e: e.dma_start(out=t, in_=dram_bc(I["b_in"][l, c0:c0 + n], n)), writes=[b], ds=b.dsem)
            return t, b

        def attn_job(heads, nq, tiles, reads, pt_tiles):
            nh = len(heads)
            W = nh * nq
            pts = []
            for ti, tl in enumerate(tiles):
                nk = tl["nk"]
                ps, bp = PS()

                def qk(e, ps=ps, tl=tl, nk=nk):
                    for hi, h in enumerate(heads):
                        r = e.matmul(ps[0:nk, hi * nq:(hi + 1) * nq], lhsT=tl["kT"][hi], rhs=h["q"], start=True, stop=True)
                    return r
                P.op("pe", qk, reads=reads, writes=[bp])
                pt, bpt = pt_tiles[ti]
                src = ps[0:nk, 0:W]
                if tl.get("bias") is not None:
                    P.op("dve", (lambda src, tl: lambda e: e.tensor_tensor(
                        out=src.rearrange("p (h q) -> p h q", h=nh), in0=src.rearrange("p (h q) -> p h q", h=nh),
                        in1=tl["bias"], op=ALU.add))(src, tl),
                        reads=[bp] + reads, writes=[bp])
                    P.op("act", (lambda pt, src, nk: lambda e: e.activation(out=pt[0:nk, 0:W], in_=src, func=AF.Exp, scale=0.125))(pt, src, nk),
                         reads=[bp], writes=[bpt])
                else:
                    P.op("act", (lambda pt, src, nk: lambda e: e.activation(out=pt[0:nk, 0:W], in_=src, func=AF.Exp, scale=0.125))(pt, src, nk),
                         reads=[bp], writes=[bpt])
                if tl.get("mask") is not None:
                    pv3 = pt[0:nk, 0:W].rearrange("p (h q) -> p h q", h=nh)
                    mk = tl["mask"]
                    mk3 = apx(mk, [[0, nh], [1, nq]])
                    P.op("dve", (lambda pv3, mk3: lambda e: e.tensor_tensor(out=pv3, in0=pv3, in1=mk3, op=ALU.mult))(pv3, mk3),
                         reads=[bpt, b_cb], writes=[bpt])
                pts.append((pt, bpt))
            yield
            pso, bpo = PS()
            psd, bpd = PS()

            def pvmm(e):
                for hi, h in enumerate(heads):
                    for ti, tl in enumerate(tiles):
                        r0, r1 = tl.get("vrows", (0, tl["nk"]))
                        v = tl["v"][hi][r0:r1, :]
                        m = v.shape[-1]
                        e.matmul(pso[0:m, hi * nq:(hi + 1) * nq], lhsT=v, rhs=pts[ti][0][r0:r1, hi * nq:(hi + 1) * nq],
                                 start=(ti == 0), stop=(ti == len(tiles) - 1))
                for ti, tl in enumerate(tiles):
                    r0, r1 = tl.get("vrows", (0, tl["nk"]))
                    r = e.matmul(psd[:, 0:W], lhsT=onesb[r0:r1, :], rhs=pts[ti][0][r0:r1, 0:W],
                                 start=(ti == 0), stop=(ti == len(tiles) - 1))
                return r
            P.op("pe", pvmm, reads=[b for _, b in pts] + reads + [b_ones], writes=[bpo, bpd])
            rd, brd = pt_tiles[-2]
            rdv = rd.bitcast(F32)[:, 0:W]
            if heads[0].get("sink") is not None:
                sk = heads[0]["sink"]
                sk3 = apx(sk, [[1, nh], [0, nq]])
                P.op("dve", lambda e: e.tensor_tensor(out=rdv.rearrange("p (h q) -> p h q", h=nh),
                                                      in0=psd[:, 0:W].rearrange("p (h q) -> p h q", h=nh), in1=sk3, op=ALU.add),
                     reads=[bpd, b_small], writes=[brd])
                P.op("act", lambda e: e.activation(out=rdv, in_=rdv, func=AF.Ln), reads=[brd], writes=[brd])
            else:
                P.op("act", lambda e: e.activation(out=rdv, in_=psd[:, 0:W], func=AF.Ln), reads=[bpd], writes=[brd])
            P.op("act", lambda e: e.activation(out=rdv, in_=rdv, func=AF.Exp, scale=-1.0), reads=[brd], writes=[brd])
            if dbg and "aj_pt" in dbg and heads[0].get("dbgflag"):
                P.dma("pool", lambda e: e.dma_start(out=O["aj_pt"][:, :], in_=pts[0][0][:, 0:384]), reads=[pts[0][1]], ds=DS("dbgaj1"), store=True)
                P.dma("pool", lambda e: e.dma_start(out=O["aj_pt4"][:, :], in_=pts[4][0][:, 0:384]), reads=[pts[4][1]], ds=DS("dbgaj3"), store=True)
                P.dma("sp", lambda e: e.dma_start(out=O["aj_rd"][:, :], in_=rdv), reads=[brd], ds=DS("dbgaj2"), store=True)
                osb, bosb = pt_tiles[-1]
                osbv = osb.bitcast(F32)[:, 0:W]
                P.op("act", lambda e: e.activation(out=osbv, in_=pso[:, 0:W], func=AF.Copy), reads=[bpo], writes=[bosb])
                P.dma("sp", lambda e: e.dma_start(out=O["aj_o"][:, :], in_=osbv), reads=[bosb], ds=DS("dbgaj4"), store=True)
            for hf in range(2):
                his = [hi for hi, h in enumerate(heads) if h["half"] == hf]
                if not his:
                    continue
                assert all(b - a == 2 for a, b in zip(his, his[1:]))
                p0 = hf * 64
                n_ = len(his)
                d0 = heads[his[0]]["dst"]
                o_ = apx(d0, [[1024, n_], [1, nq]])
                i0_ = apx(pso[p0:p0 + 64, his[0] * nq:his[0] * nq + 1], [[2 * nq, n_], [1, nq]])
                i1_ = apx(rdv[p0:p0 + 64, his[0] * nq:his[0] * nq + 1], [[2 * nq, n_], [1, nq]])
                P.op("dve", (lambda o_, i0_, i1_: lambda e: e.tensor_tensor(out=o_, in0=i0_, in1=i1_, op=ALU.mult))(o_, i0_, i1_),
                     reads=[bpo, brd], writes=[heads[his[0]]["dstb"]])

        def run_pipelined(jobs):
            prev = None
            for j in jobs:
                next(j)
                if prev is not None:
                    for _ in prev:
                        pass
                prev = j
            if prev is not None:
                for _ in prev:
                    pass

        def alloc_pt(n, tag):
            out = []
            for i in range(n):
                out.append(AR.alloc("pt%s%d" % (tag, i), [384], BF16))
            out.append(AR.alloc("ptrd" + tag, [768], BF16))
            out.append(AR.alloc("pttmp" + tag, [768], BF16))
            return out

        def mlstm_phase(l, grp):
            isP = grp == "P"
            T = 512 if isP else 1024
            nch = T // 128
            seqs = [(0, 256), (256, 256)] if isP else [(0, 1024)]
            xbase = 0 if isP else 4
            AR.new_phase()
            qT, b_qT = AR.alloc("m_qT", [2, T], BF16)
            kT, b_kT = AR.alloc("m_kT", [4, T], BF16)
            P.op("dve", lambda e: e.memset(apx(kT[64:128, 0, 0:1], [[2 * kT.shape[-1], kT.shape[1] // 2], [1, kT.shape[-1]]]), 0.0), writes=[b_kT])
            P.op("dve", lambda e: e.memset(apx(kT[0:64, 1, 0:1], [[2 * kT.shape[-1], kT.shape[1] // 2], [1, kT.shape[-1]]]), 0.0), writes=[b_kT])
            ktok, b_ktok = AR.alloc("m_ktok", [nch, 256], BF16)
            vtok, b_vtok = AR.alloc("m_vtok", [nch, 256], BF16)
            ogtok, b_og = AR.alloc("m_og", [nch, 256], BF16)
            Hf, b_Hf = AR.alloc("m_Hf", [nch, 256], BF16)
            g = []
            for i in range(5):
                g.append(AR.alloc("m_g%d" % i, [T], F32))
            tks, b_tks = AR.alloc("m_tks", [nch, 32], F32)
            bbc, b_bbc = load_bias_bc("m_bbc", l, 256, 768)
            ngbc, b_ngbc = AR.alloc("m_ngbc", [256], F32, DS("m_ngbc"))
            P.dma("sp", lambda e: e.dma_start(out=ngbc, in_=dram_bc(I["norm_g"][l, :], 256)), writes=[b_ngbc], ds=b_ngbc.dsem)
            gb, b_gb = AR.alloc("m_gb", [4], F32, DS("m_gb"))
            DEC, b_DEC = AR.alloc("m_DEC", [8], F32)
            dgt, b_dgt = AR.alloc("m_dgt", [8, 4], F32)
            decsb, b_decsb = AR.alloc("m_decsb", [2, 8, 4], F32)
            nchains = 4 if isP else 2
            CT = [AR.alloc("m_CT%d" % d, [2, 65], F32, DS("m_CT%d" % d)) for d in range(nchains)]
            CTb = [AR.alloc("m_CTb%d" % d, [4, 65], BF16) for d in range(nchains)]
            Hb, b_Hb = AR.alloc("m_Hb", [nch, 256], BF16)
            W = []
            for i in range(4):
                W.append(dict(
                    Pt=AR.alloc("m_Pt%d" % i, [4, 128], BF16), Vp=AR.alloc("m_Vp%d" % i, [4, 65], BF16),
                    A=AR.alloc("m_A%d" % i, [4, 65], F32), Bq=AR.alloc("m_Bq%d" % i, [4, 65], F32),
                    sm=AR.alloc("m_sm%d" % i, [32], F32)))
            FS = []
            for i in range(4 if isP else 2):
                FS.append(dict(sm=AR.alloc("m_fsm%d" % i, [32], F32), Hs=AR.alloc("m_Hs%d" % i, [4, 64], F32),
                               sq=AR.alloc("m_sq%d" % i, [4, 64], F32), yml=AR.alloc("m_yml%d" % i, [256], BF16)))
            ogts = [AR.alloc("m_ogt%d" % i, [256], F32) for i in range(2)]
            if isP:
                Cst = [AR.alloc("m_Cst%d" % i, [2, 4, 64], F32, DS("m_Cst%d" % i)) for i in range(2)]
                mfin, b_mfin = AR.alloc("m_mfin", [2], F32, DS("m_mfin"))
            else:
                C0sb, b_C0 = AR.alloc("m_C0", [2, 4, 64], F32, DS("m_C0"))

            b_ginit = P.buf("m_g4dec_init", init=dict(g[4][1].rs))
            AR.cur.append(b_ginit)
            P.op("dve", lambda e: e.memset(g[4][0][0:36, :], 0.0), writes=[b_ginit])
            P.op("dve", lambda e: e.memset(DEC[0:36, :], 0.0), writes=[b_ginit])
            P.op("dve", lambda e: e.memset(gb[:, :], 0.0), writes=[b_gb])
            with nc.allow_non_contiguous_dma(reason="tiny vector loads"):
                fl = []
                for (r0, col, src) in ((0, 0, I["b_in"][l, 1024:1028]), (32, 0, I["b_in"][l, 1032:1036]),
                                       (0, 1, I["b_in"][l, 1028:1032]), (32, 1, I["b_in"][l, 1036:1040]),
                                       (0, 2, I["fbias"][l, 0, :]), (32, 2, I["fbias"][l, 1, :])) + \
                        (() if isP else ((0, 3, I["stm"][l, 0, :]), (32, 3, I["stm"][l, 1, :]))):
                    srcp = AP(src.tensor, src.offset, [[1, 4], [1, 1]])
                    fl.append((lambda r0, col, srcp: lambda e: e.dma_start(out=gb[r0:r0 + 4, col:col + 1], in_=srcp))(r0, col, srcp))
                P.dma("sp", fl, writes=[b_gb], ds=b_gb.dsem)
            P.op("dve", lambda e: e.tensor_tensor(out=gb[0:36, 1:2], in0=gb[0:36, 1:2], in1=gb[0:36, 2:3], op=ALU.add),
                 reads=[b_gb], writes=[b_gb])

            chk('ml_a' + grp)
            slot, bsl = ring_load(wpiece(I["w_in"], l, 0, 512), 512)
            for c in range(2):
                fm_proj(slot, bsl, c * 128, T, lambda ps, bp, nt, c=c: P.op(
                    "act", lambda e: e.activation(out=qT[:, c, nt * 512:(nt + 1) * 512], in_=ps[:, :], func=AF.Identity,
                                                  bias=binT[:, c:c + 1]), reads=[bp, b_binT], writes=[b_qT]))
                def evk_(ps, bp, nt, c=c):
                    for hh in range(2):
                        pr = slice(hh * 64, hh * 64 + 64)
                        P.op("act", (lambda pr, hh: lambda e: e.activation(
                            out=kT[pr, 2 * c + hh, nt * 512:(nt + 1) * 512], in_=ps[pr, :], func=AF.Identity,
                            bias=binT[pr, 2 + c:3 + c], scale=0.125))(pr, hh), reads=[bp, b_binT], writes=[b_kT])
                fm_proj(slot, bsl, 256 + c * 128, T, evk_)
            for tcl in range(nch):
                tm_proj(slot, bsl, 256, 256, tcl, lambda ps, bp, tcl=tcl: P.op(
                    "dve", lambda e: e.scalar_tensor_tensor(out=ktok[:, tcl, :], in0=ps[:, 0:256], scalar=1.0, in1=bbc[:, 0:256],
                                                            op0=ALU.mult, op1=ALU.add), reads=[bp, b_bbc], writes=[b_ktok]))
            P.op("act", lambda e: e.activation(out=ktok[:, :, :], in_=ktok[:, :, :], func=AF.Copy, scale=0.125),
                 reads=[b_ktok], writes=[b_ktok])
            slot, bsl = ring_load(wpiece(I["w_in"], l, 512, 512), 512)
            for tcl in range(nch):
                def ev(ps, bp, tcl=tcl):
                    ogt, b_ogt = ogts[tcl % 2]
                    P.op("dve", lambda e: e.tensor_tensor(out=vtok[:, tcl, :], in0=ps[:, 0:256], in1=bbc[:, 256:512], op=ALU.add),
                         reads=[bp, b_bbc], writes=[b_vtok])
                    P.op("dve", lambda e: e.tensor_tensor(out=ogt, in0=ps[:, 256:512], in1=bbc[:, 512:768], op=ALU.add),
                         reads=[bp, b_bbc], writes=[b_ogt])
                    P.op("act", lambda e: e.activation(out=ogtok[:, tcl, :], in_=ogt, func=AF.Sigmoid), reads=[b_ogt], writes=[b_og])
                tm_proj(slot, bsl, 0, 512, tcl, ev)
            chk('ml_b' + grp)
            (g0, bg0), (g1, bg1), (g2, bg2), (g3, bg3), (g4, bg4) = g
            for (c0, dst, bdst, bcol) in ((0, g3, bg3, 0), (64, g0, bg0, 1)):
                for nt in range(T // 512):
                    ps, bp = PS()

                    def mm(e, ps=ps, nt=nt, c0=c0):
                        for k in range(KC):
                            r = e.matmul(ps[0:64, :], lhsT=gw[:, k, c0:c0 + 64], rhs=hT[:, k, nt * 512:(nt + 1) * 512],
                                         start=(k == 0), stop=(k == KC - 1))
                        return r
                    P.op("pe", mm, reads=[b_gw] + b_hT[nt * 4:(nt + 1) * 4], writes=[bp])
                    P.op("act", (lambda ps, nt, dst, bcol: lambda e: e.activation(
                        out=dst[0:36, nt * 512:(nt + 1) * 512], in_=ps[0:36, :], func=AF.Identity, bias=gb[0:36, bcol:bcol + 1]))(ps, nt, dst, bcol),
                        reads=[bp, b_gb], writes=[bdst])
            chk('ml_c' + grp)
            R36 = slice(0, 36)
            P.op("act", lambda e: e.activation(out=g1[R36, :], in_=g0[R36, :], func=AF.Abs), reads=[bg0], writes=[bg1])
            P.op("act", lambda e: e.activation(out=g1[R36, :], in_=g1[R36, :], func=AF.Exp, scale=-1.0), reads=[bg1], writes=[bg1])
            P.op("act", lambda e: e.activation(out=g1[R36, :], in_=g1[R36, :], func=AF.Ln, bias=1.0), reads=[bg1], writes=[bg1])
            P.op("dve", lambda e: e.tensor_single_scalar(out=g0[R36, :], in_=g0[R36, :], scalar=0.0, op=ALU.min), reads=[bg0], writes=[bg0])
            P.op("dve", lambda e: e.tensor_tensor(out=g0[R36, :], in0=g0[R36, :], in1=g1[R36, :], op=ALU.subtract),
                 reads=[bg0, bg1], writes=[bg0])

            def rows(d):
                return slice(0, 4) if d == 0 else slice(32, 36)

            def dirview(a, d, off, ln):
                r = a[rows(d), off:off + ln]
                if d == 0:
                    return r
                return AP(r.tensor, r.offset + ln - 1, [list(r.ap[0]), [-1, ln]])

            def scan(dst, bd, src, bs, op1, init):
                for (off, ln) in seqs:
                    for d in range(2):
                        ini = init if not isinstance(init, tuple) else init[0][rows(d), init[1]:init[1] + 1]
                        P.op("dve", (lambda d, off, ln, ini: lambda e: e.tensor_tensor_scan(
                            out=dirview(dst, d, off, ln), data0=apx(ones32[rows(d), 0:1], [[0, ln]]),
                            data1=dirview(src, d, off, ln), initial=ini, op0=ALU.mult, op1=op1))(d, off, ln, ini),
                            reads=[bs, b_gb], writes=[bd])
            scan(g1, bg1, g0, bg0, ALU.add, 0.0)
            P.op("dve", lambda e: e.tensor_tensor(out=g2[R36, :], in0=g3[R36, :], in1=g1[R36, :], op=ALU.subtract),
                 reads=[bg3, bg1], writes=[bg2])
            scan(g0, bg0, g2, bg2, ALU.max, 0.0 if isP else (gb, 3))

            def Rv(d, off, j0, n, bc):
                col = off + (127 if d == 0 else 0) + 128 * j0
                base = g0[rows(d), col:col + 1]
                dims = [[128, n]] + ([[0, bc]] if bc else [])
                return apx(base, dims)

            def cv(a, d, off, j0, n):
                r = a[rows(d), off + 128 * j0:off + 128 * (j0 + n)]
                return r.rearrange("p (c t) -> p c t", t=128)
            chk('ml_d' + grp)
            m0s = 0.0
            for (off, ln) in seqs:
                nC = ln // 128
                gc0 = off // 128
                for d in range(2):
                    P.op("dve", (lambda d, off, nC: lambda e: e.tensor_tensor(out=cv(g3, d, off, 0, nC), in0=cv(g2, d, off, 0, nC),
                                                                               in1=Rv(d, off, 0, nC, 128), op=ALU.subtract))(d, off, nC),
                         reads=[bg2, bg0], writes=[bg3])
                    P.op("dve", (lambda d, off, nC: lambda e: e.tensor_tensor(out=cv(g2, d, off, 0, nC), in0=Rv(d, off, 0, nC, 128),
                                                                               in1=cv(g0, d, off, 0, nC), op=ALU.subtract))(d, off, nC),
                         reads=[bg0, bg3], writes=[bg2])
                    if d == 0:
                        jo, jp, jfirst = 1, 0, 0
                    else:
                        jo, jp, jfirst = 0, 1, nC - 1
                    if nC > 1:
                        P.op("dve", (lambda d, off, nC, jo, jp: lambda e: e.tensor_tensor(
                            out=cv(g4, d, off, jo, nC - 1), in0=Rv(d, off, jp, nC - 1, 128), in1=cv(g0, d, off, jo, nC - 1), op=ALU.subtract))(d, off, nC, jo, jp),
                            reads=[b_ginit, bg0], writes=[bg4])
                        P.op("dve", (lambda d, off, nC, jo, jp, gc0: lambda e: e.tensor_tensor(
                            out=DEC[rows(d), gc0 + jo:gc0 + jo + nC - 1], in0=Rv(d, off, jp, nC - 1, 0), in1=Rv(d, off, jo, nC - 1, 0), op=ALU.subtract))(d, off, nC, jo, jp, gc0),
                            reads=[b_ginit, bg0], writes=[b_DEC])
                    m0 = 0.0 if isP else gb[rows(d), 3:4]
                    P.op("dve", (lambda d, off, jfirst, m0: lambda e: e.tensor_scalar(
                        out=g4[rows(d), off + 128 * jfirst:off + 128 * (jfirst + 1)], in0=g0[rows(d), off + 128 * jfirst:off + 128 * (jfirst + 1)],
                        scalar1=m0, scalar2=-1.0, op0=ALU.subtract, op1=ALU.mult))(d, off, jfirst, m0),
                        reads=[b_ginit, bg0, b_gb], writes=[bg4])
                    P.op("dve", (lambda d, off, jfirst, m0, gc0: lambda e: e.tensor_scalar(
                        out=DEC[rows(d), gc0 + jfirst:gc0 + jfirst + 1], in0=Rv(d, off, jfirst, 1, 0),
                        scalar1=m0, scalar2=-1.0, op0=ALU.subtract, op1=ALU.mult))(d, off, jfirst, m0, gc0),
                        reads=[b_ginit, bg0, b_gb], writes=[b_DEC])
            if isP:
                for si, (off, ln) in enumerate(seqs):
                    for d in range(2):
                        col = off + (ln - 1 if d == 0 else 0)
                        P.op("dve", (lambda d, col, si: lambda e: e.tensor_tensor(out=mfin[rows(d), si:si + 1], in0=g1[rows(d), col:col + 1],
                                                                                    in1=g0[rows(d), col:col + 1], op=ALU.add))(d, col, si),
                             reads=[bg1, bg0], writes=[b_mfin])
                with nc.allow_non_contiguous_dma(reason="tiny state output"):
                    fl = []
                    for si in range(2):
                        for d in range(2):
                            dstp = O["o_m"][si, l, d, :]
                            dstp = AP(dstp.tensor, dstp.offset, [[1, 4], [1, 1]])
                            fl.append((lambda si, d, dstp: lambda e: e.dma_start(out=dstp, in_=mfin[rows(d), si:si + 1]))(si, d, dstp))
                    P.dma("sp", fl, reads=[b_mfin], ds=b_mfin.dsem, store=True)
            P.op("dve", lambda e: e.tensor_tensor(out=g1[R36, :], in0=g1[R36, :], in1=g0[R36, :], op=ALU.add),
                 reads=[bg1, bg0], writes=[bg1])
            P.op("act", lambda e: e.activation(out=g3[R36, :], in_=g3[R36, :], func=AF.Exp), reads=[bg3], writes=[bg3])
            P.op("act", lambda e: e.activation(out=g2[R36, :], in_=g2[R36, :], func=AF.Exp), reads=[bg2], writes=[bg2])
            P.op("act", lambda e: e.activation(out=g4[R36, :], in_=g4[R36, :], func=AF.Exp), reads=[bg4], writes=[bg4])
            P.op("act", lambda e: e.activation(out=g1[R36, :], in_=g1[R36, :], func=AF.Exp, scale=-1.0), reads=[bg1], writes=[bg1])
            P.op("act", lambda e: e.activation(out=DEC[R36, 0:nch], in_=DEC[R36, 0:nch], func=AF.Exp), reads=[b_DEC], writes=[b_DEC])
            chk('ml_e' + grp)
            for tcl in range(nch):
                for d in range(2):
                    ps, bp = PS()

                    def trs(e, ps=ps, tcl=tcl, d=d):
                        r0 = 0 if d == 0 else 32
                        for qi, gt in enumerate((g3, g2, g4, g1)):
                            r = e.matmul(ps[:, qi * 4:qi * 4 + 4], lhsT=gt[r0:r0 + 4, tcl * 128:(tcl + 1) * 128],
                                         rhs=ident32[r0:r0 + 4, r0:r0 + 4], start=True, stop=True)
                        return r
                    P.op("pe", trs, reads=[bg1, bg2, bg3, bg4, b_c32], writes=[bp])
                    P.op("act", (lambda ps, tcl, d: lambda e: e.activation(
                        out=apx(tks[:, tcl, d * 4:d * 4 + 1], [[8, 4], [1, 4]]), in_=ps[:, 0:16].rearrange("p (q h) -> p q h", q=4), func=AF.Copy))(ps, tcl, d),
                        reads=[bp], writes=[b_tks])
            for d in range(2):
                r0 = 0 if d == 0 else 32
                P.op("dve", (lambda d, r0: lambda e: e.tensor_tensor(
                    out=dgt[rows(d), 0:nch, :], in0=apx(ident32[r0:r0 + 4, r0:r0 + 1], [[0, nch], [1, 4]]),
                    in1=apx(DEC[rows(d), 0:1], [[1, nch], [0, 4]]), op=ALU.mult))(d, r0), reads=[b_DEC, b_c32], writes=[b_dgt])

            for d in range(2):
                psd_, bpd_ = PS()
                r0 = 0 if d == 0 else 32
                P.op("pe", (lambda psd_, r0: lambda e: e.matmul(psd_[:, 0:nch * 4], lhsT=ones32[r0:r0 + 4, :],
                                                                rhs=dgt[r0:r0 + 4, 0:nch, :].rearrange("p c h -> p (c h)"), start=True, stop=True))(psd_, r0),
                     reads=[b_dgt], writes=[bpd_])
                P.op("act", (lambda d, psd_: lambda e: e.activation(out=decsb[:, d, 0:nch, :].rearrange("p c h -> p (c h)"),
                                                                    in_=psd_[:, 0:nch * 4], func=AF.Copy))(d, psd_),
                     reads=[bpd_], writes=[b_decsb])

            chk('ml_f' + grp)
            hh_c = [(h % 2, h // 2) for h in range(4)]

            def tk(tcl, q, d):
                c = (q * 2 + d) * 4
                return tks[:, tcl, c:c + 4]

            def chain(si, off, ln, d, ci, wsets):
                nC = ln // 128
                order = list(range(nC)) if d == 0 else list(range(nC - 1, -1, -1))
                (ct, b_ct), (ctb, b_ctb) = CT[ci], CTb[ci]
                def ct_to_bf():
                    for hh in range(2):
                        pr = slice(hh * 64, hh * 64 + 64)
                        P.op("act", (lambda pr, hh: lambda e: e.activation(
                            out=apx(ctb[pr, hh, 0:1], [[130, 2], [1, 65]]), in_=ct[pr, :, :], func=AF.Copy))(pr, hh),
                            reads=[b_ct], writes=[b_ctb])
                P.op("dve", lambda e: e.memset(ctb[:, :, :], 0.0), writes=[b_ctb])
                if isP:
                    P.op("dve", lambda e: e.memset(ct[:, :, :], 0.0), writes=[b_ct])
                else:
                    for c in range(2):
                        ps, bp = PS()
                        P.op("pe", (lambda ps, c: lambda e: e.transpose(
                            ps[:, 0:64], in_=C0sb[0:64, d, 2 * c:2 * c + 2, :].rearrange("p h e -> p (h e)"), identity=ident32[0:64, 0:64]))(ps, c),
                            reads=[b_C0, b_c32], writes=[bp])
                        P.op("act", (lambda ps, c: lambda e: e.activation(out=ct[:, c, 0:64], in_=ps[:, 0:64], func=AF.Copy))(ps, c),
                             reads=[bp], writes=[b_ct])
                    with nc.allow_non_contiguous_dma(reason="tiny state load"):
                        srcn = I["stn"][l, d, :, :].rearrange("(c hh) e -> (hh e) c", hh=2)
                        P.dma("sp", lambda e: e.dma_start(out=ct[:, :, 64:65].rearrange("p c o -> p (c o)"), in_=srcn),
                              writes=[b_ct], ds=b_ct.dsem)
                    ct_to_bf()
                chk('ci_a')
                yield
                for step, j in enumerate(order):
                    tcl = off // 128 + j
                    sl = slice(tcl * 128, (tcl + 1) * 128)
                    w = wsets[step % len(wsets)]
                    (Pt, bPt), (Vp, bVp), (A, bA), (Bq, bBq), (sm, bsm) = (w["Pt"], w["Vp"], w["A"], w["Bq"], w["sm"])
                    pss, bps_ = PS()

                    def smm(e, pss=pss, sl=sl):
                        for h, (hh, c) in enumerate(hh_c):
                            r = e.matmul(pss[:, h * 128:(h + 1) * 128], lhsT=kT[:, h, sl], rhs=qT[:, c, sl], start=True, stop=True)
                        return r
                    P.op("pe", smm, reads=[b_kT, b_qT], writes=[bps_])
                    yield
                    chk('ci_b')
                    mk = maskFb if d == 0 else maskBb
                    P.op("dve", (lambda Pt, pss, mk: lambda e: e.tensor_tensor(
                        out=Pt, in0=pss[:, :].rearrange("p (h t) -> p h t", h=4), in1=apx(mk, [[0, 4], [1, 128]]), op=ALU.mult))(Pt, pss, mk),
                        reads=[bps_, b_cb], writes=[bPt])
                    chk('ci_c')
                    P.op("dve", (lambda Vp, tcl: lambda e: e.tensor_tensor(
                        out=Vp[:, :, 0:64], in0=vtok[:, tcl, :].rearrange("p (h x) -> p h x", h=4),
                        in1=apx(tk(tcl, 0, d), [[1, 4], [0, 64]]), op=ALU.mult))(Vp, tcl), reads=[b_vtok, b_tks], writes=[bVp])
                    chk('ci_d')
                    P.op("act", (lambda Vp, tcl: lambda e: e.activation(out=Vp[:, :, 64:65].rearrange("p h o -> p (h o)"),
                                                                        in_=tk(tcl, 0, d), func=AF.Copy))(Vp, tcl),
                         reads=[b_tks], writes=[bVp])
                    chk('ch_a')
                    yield
                    psi, bpi = PS()
                    psj, bpj = PS()

                    def imm(e, psi=psi, psj=psj, Pt=Pt, Vp=Vp, sl=sl):
                        for h, (hh, c) in enumerate(hh_c):
                            e.matmul(psi[:, h * 65:(h + 1) * 65], lhsT=Pt[:, h, :], rhs=Vp[:, h, :], start=True, stop=True)
                        for h, (hh, c) in enumerate(hh_c):
                            r = e.matmul(psj[:, h * 65:(h + 1) * 65], lhsT=qT[:, c, sl], rhs=ctb[:, h, :], start=True, stop=True)
                        return r
                    P.op("pe", imm, reads=[bPt, bVp, b_qT, b_ctb], writes=[bpi, bpj])
                    yield
                    chk('ch_b')
                    v3 = lambda ps_: ps_[:, 0:260].rearrange("p (h x) -> p h x", h=4)
                    P.op("dve", (lambda A, psi, tcl: lambda e: e.tensor_tensor(out=A, in0=v3(psi), in1=apx(tk(tcl, 1, d), [[1, 4], [0, 65]]), op=ALU.mult))(A, psi, tcl),
                         reads=[bpi, b_tks], writes=[bA])
                    P.op("dve", (lambda Bq, psj, tcl: lambda e: e.tensor_tensor(out=Bq, in0=v3(psj), in1=apx(tk(tcl, 2, d), [[1, 4], [0, 65]]), op=ALU.mult))(Bq, psj, tcl),
                         reads=[bpj, b_tks], writes=[bBq])
                    P.op("dve", (lambda A, Bq: lambda e: e.tensor_tensor(out=A, in0=A, in1=Bq, op=ALU.add))(A, Bq), reads=[bA, bBq], writes=[bA])
                    yield
                    den = sm[:, 0:4]
                    P.op("act", (lambda A, den: lambda e: e.activation(out=den, in_=A[:, :, 64:65].rearrange("p h o -> p (h o)"), func=AF.Abs))(A, den),
                         reads=[bA], writes=[bsm])
                    P.op("dve", (lambda den, tcl: lambda e: e.tensor_tensor(out=den, in0=den, in1=tk(tcl, 3, d), op=ALU.max))(den, tcl),
                         reads=[bsm, b_tks], writes=[bsm])
                    P.op("dve", (lambda den: lambda e: e.reciprocal(out=den, in_=den))(den), reads=[bsm], writes=[bsm])
                    yield
                    chk('ch_c')
                    hdst, b_hdst = (Hf, b_Hf) if d == 0 else (Hb, b_Hb)
                    hfv = hdst[:, tcl, :].rearrange("p (h x) -> p h x", h=4)
                    rd3 = apx(den, [[1, 4], [0, 64]])
                    P.op("dve", (lambda A, hfv, rd3: lambda e: e.tensor_tensor(out=hfv, in0=A[:, :, 0:64], in1=rd3, op=ALU.mult))(A, hfv, rd3),
                         reads=[bA, bsm], writes=[b_hdst])
                    chk('ch_d')
                    last = step == len(order) - 1
                    if last and not isP:
                        yield
                        continue
                    psu, bpu = PS()

                    def umm(e, psu=psu, Vp=Vp, tcl=tcl):
                        for h in range(4):
                            if h % 2 == 0:
                                r = e.matmul(psu[0:64, h * 65:(h + 1) * 65], lhsT=ktok[:, tcl, h * 64:(h + 1) * 64], rhs=Vp[:, h, :], start=True, stop=True)
                            else:
                                r = e.matmul(psu[:, h * 65:(h + 1) * 65], lhsT=ktok[:, tcl, (h - 1) * 64:(h + 1) * 64], rhs=Vp[:, h, :], start=True, stop=True)
                        return r
                    P.op("pe", umm, reads=[b_ktok, bVp], writes=[bpu])
                    yield
                    for hh in range(2):
                        pr = slice(hh * 64, (hh + 1) * 64)
                        dcv = apx(decsb[pr, d, tcl, hh:hh + 1], [[2, 2], [0, 65]])
                        P.op("dve", (lambda pr, dcv: lambda e: e.tensor_tensor(out=ct[pr, :, :], in0=ct[pr, :, :], in1=dcv, op=ALU.mult))(pr, dcv),
                             reads=[b_ct, b_decsb], writes=[b_ct])
                        psv = apx(psu[pr, hh * 65:hh * 65 + 1], [[130, 2], [1, 65]])
                        P.op("dve", (lambda pr, psv: lambda e: e.tensor_tensor(out=ct[pr, :, :], in0=psv, in1=ct[pr, :, :], op=ALU.add))(pr, psv),
                             reads=[bpu, b_ct], writes=[b_ct])
                    ct_to_bf()
                    yield
                chk('ch_e')
                if isP:
                    cst, b_cst = Cst[si]
                    for c in range(2):
                        ps, bp = PS()
                        P.op("pe", (lambda ps, c: lambda e: e.transpose(ps[0:64, 0:128], in_=ct[:, c, 0:64], identity=ident32))(ps, c),
                             reads=[b_ct, b_c32], writes=[bp])
                        P.op("act", (lambda ps, c: lambda e: e.activation(out=cst[0:64, d, 2 * c:2 * c + 2, :].rearrange("p h e -> p (h e)"),
                                                                          in_=ps[0:64, 0:128], func=AF.Copy))(ps, c),
                             reads=[bp], writes=[b_cst])
                    with nc.allow_non_contiguous_dma(reason="state outputs"):
                        P.dma("sp", lambda e: e.dma_start(out=O["o_C"][si, l, d].rearrange("h d e -> d h e"), in_=cst[0:64, d, :, :]),
                              reads=[b_cst], ds=b_cst.dsem, store=True)
                        dstn = O["o_n"][si, l, d, :, :].rearrange("(c hh) e -> (hh e) c", hh=2)
                        P.dma("sp", lambda e: e.dma_start(out=dstn, in_=ct[:, :, 64:65].rearrange("p c o -> p (c o)")),
                              reads=[b_ct], ds=b_ct.dsem, store=True)

            if not isP:
                P.dma("sp", lambda e: e.dma_start(out=C0sb[0:64, :, :, :], in_=I["stC"][l].rearrange("r h d e -> d r h e")),
                      writes=[b_C0], ds=b_C0.dsem)
            def finalize(tcl, fs):
                (sm, bsm), (Hs, bHs), (sq, bsq), (yml, byml) = fs["sm"], fs["Hs"], fs["sq"], fs["yml"]
                sl = slice(tcl * 128, (tcl + 1) * 128)
                Hs2 = Hs.rearrange("p h x -> p (h x)")
                P.op("dve", lambda e: e.tensor_tensor(out=Hs2, in0=Hf[:, tcl, :], in1=Hb[:, tcl, :], op=ALU.add), reads=[b_Hf, b_Hb], writes=[bHs])
                yield
                s1, s2, mu, va = sm[:, 4:8], sm[:, 8:12], sm[:, 12:16], sm[:, 16:20]
                P.op("dve", lambda e: e.tensor_reduce(out=s1, in_=Hs, axis=AX.X, op=ALU.add), reads=[bHs], writes=[bsm])
                yield
                P.op("act", lambda e: e.activation(out=sq, in_=Hs, func=AF.Square), reads=[bHs], writes=[bsq])
                yield
                P.op("dve", lambda e: e.tensor_reduce(out=s2, in_=sq, axis=AX.X, op=ALU.add), reads=[bsq], writes=[bsm])
                yield
                P.op("dve", lambda e: e.tensor_scalar(out=mu, in0=s1, scalar1=1.0 / 64, scalar2=None, op0=ALU.mult), reads=[bsm], writes=[bsm])
                yield
                P.op("dve", lambda e: e.tensor_tensor(out=va, in0=mu, in1=mu, op=ALU.mult), reads=[bsm], writes=[bsm])
                yield
                P.op("dve", lambda e: e.scalar_tensor_tensor(out=va, in0=s2, scalar=1.0 / 64, in1=va, op0=ALU.mult, op1=ALU.subtract), reads=[bsm], writes=[bsm])
                yield
                P.op("act", lambda e: e.activation(out=va, in_=va, func=AF.Sqrt, bias=small[:, 0:1]), reads=[bsm, b_small], writes=[bsm])
                yield
                P.op("dve", lambda e: e.reciprocal(out=va, in_=va), reads=[bsm], writes=[bsm])
                yield
                P.op("dve", lambda e: e.tensor_tensor(out=Hs, in0=Hs, in1=apx(mu, [[1, 4], [0, 64]]), op=ALU.subtract), reads=[bHs, bsm], writes=[bHs])
                yield
                P.op("dve", lambda e: e.tensor_tensor(out=Hs, in0=Hs, in1=apx(va, [[1, 4], [0, 64]]), op=ALU.mult), reads=[bHs, bsm], writes=[bHs])
                yield
                P.op("dve", lambda e: e.tensor_tensor(out=Hs2, in0=Hs2, in1=ngbc, op=ALU.mult), reads=[bHs, b_ngbc], writes=[bHs])
                yield
                P.op("dve", lambda e: e.tensor_tensor(out=yml, in0=Hs2, in1=ogtok[:, tcl, :], op=ALU.mult), reads=[bHs, b_og], writes=[byml])
                yield
                pst, bpt = PS()
                pstb = pst[:, :].bitcast(BF16)

                def ytr(e):
                    e.transpose(pstb[:, 0:128], in_=yml[:, 0:128], identity=identb)
                    return e.transpose(pstb[:, 128:256], in_=yml[:, 128:256], identity=identb)
                P.op("pe", ytr, reads=[byml, b_cb], writes=[bpt])
                yield
                P.op("act", lambda e: e.activation(out=mixT[:, 0:2, sl], in_=pstb[:, 0:256].rearrange("p (c t) -> p c t", c=2), func=AF.Copy),
                     reads=[bpt], writes=[b_mix[tcl]])
                yield

            gens = []
            ci = 0
            for si, (off, ln) in enumerate(seqs):
                for d in range(2):
                    wsets = [W[ci]] if isP else W[2 * ci:2 * ci + 2]
                    gens.append(chain(si, off, ln, d, ci, wsets))
                    ci += 1
            while gens:
                for g_ in list(gens):
                    try:
                        next(g_)
                    except StopIteration:
                        gens.remove(g_)
            def run_rr(gl):
                gl = list(gl)
                while gl:
                    for g_ in list(gl):
                        try:
                            next(g_)
                        except StopIteration:
                            gl.remove(g_)
            nf = len(FS)
            for t0_ in range(0, nch, nf):
                run_rr([finalize(tcl, FS[tcl % nf]) for tcl in range(t0_, min(nch, t0_ + nf))])
            print("mlstm arena bytes", grp, AR.off) if dbg is not None and "arena_dbg" in dbg else None

        def evac_dup_k(ps, bp, nt, kTd, b_kTd, bias_col, src=None):
            for g_, (r0, dsts) in enumerate(((0, (0, 64)), (64, (64, 0)))):
                for dp in dsts:
                    P.op("act", (lambda r0, dp, g_: lambda e: e.activation(
                        out=kTd[dp:dp + 64, 2 * g_ + dp // 64, nt * 512:(nt + 1) * 512], in_=(src if src is not None else ps)[r0:r0 + 64, 0:512], func=AF.Identity,
                        bias=(binT[r0:r0 + 64, bias_col:bias_col + 1] if bias_col is not None else 0.0)))(r0, dp, g_),
                        reads=[bp, b_binT], writes=[b_kTd])

        def swa_phase(l, grp):
            isP = grp == "P"
            T = 512 if isP else 1024
            nch = T // 128
            AR.new_phase()
            qTs, b_qTs = AR.alloc("s_qT", [3, T], BF16)
            kTd, b_kTd = AR.alloc("s_kTd", [4, T], BF16)
            P.op("dve", lambda e: e.memset(apx(kTd[64:128, 0, 0:1], [[2 * kTd.shape[-1], kTd.shape[1] // 2], [1, kTd.shape[-1]]]), 0.0), writes=[b_kTd])
            P.op("dve", lambda e: e.memset(apx(kTd[0:64, 1, 0:1], [[2 * kTd.shape[-1], kTd.shape[1] // 2], [1, kTd.shape[-1]]]), 0.0), writes=[b_kTd])
            vd, b_vd = AR.alloc("s_vd", [nch, 256], BF16)
            bbc, b_bbc = load_bias_bc("s_bbc", l, 1424, 256)
            pts = [alloc_pt(7, "a"), alloc_pt(7, "b")]
            P.dma("sp", lambda e: e.dma_start(out=small[:, 1:7], in_=dram_bc(I["sink"][l, :], 6)), writes=[b_small], ds=b_small.dsem)
            P.op("act", lambda e: e.activation(out=small[:, 1:7], in_=small[:, 1:7], func=AF.Exp), reads=[b_small], writes=[b_small])
            if isP:
                kst = [AR.alloc("s_kst%d" % i, [128], F32, DS("s_kst%d" % i)) for i in range(2)]
                vst = [AR.alloc("s_vst%d" % i, [128], F32, DS("s_vst%d" % i)) for i in range(2)]
            else:
                rope, b_rope = AR.alloc("s_rope", [2048], F32, DS("s_rope"))
                P.dma("sp", lambda e: e.dma_start(out=rope, in_=I["rope"][:, :]), writes=[b_rope], ds=b_rope.dsem)
                qf = [AR.alloc("s_qf%d" % i, [512], F32) for i in range(4)]
                ckb, b_ckb = AR.alloc("s_ckb", [4, 2, 2, 64], BF16, DS("s_ckb"))
                ckT, b_ckT = AR.alloc("s_ckT", [4, 512], BF16)
                P.op("dve", lambda e: e.memset(apx(ckT[64:128, 0, 0:1], [[2 * ckT.shape[-1], ckT.shape[1] // 2], [1, ckT.shape[-1]]]), 0.0), writes=[b_ckT])
                P.op("dve", lambda e: e.memset(apx(ckT[0:64, 1, 0:1], [[2 * ckT.shape[-1], ckT.shape[1] // 2], [1, ckT.shape[-1]]]), 0.0), writes=[b_ckT])
                cvd, b_cvd = AR.alloc("s_cvd", [4, 2, 2, 64], BF16, DS("s_cvd"))
                fl = []
                for dup in range(2):
                    for g_ in range(2):
                        fl.append((lambda dup, g_: lambda e: e.dma_start(out=ckb[:, :, g_, dup, :], in_=I["ck_swa"][l, g_].rearrange("(c p) d -> p c d", p=128)))(dup, g_))
                P.dma("pool", fl, writes=[b_ckb], ds=b_ckb.dsem)
                fl = []
                for dup in range(2):
                    for g_ in range(2):
                        fl.append((lambda dup, g_: lambda e: e.dma_start(out=cvd[:, :, g_, dup, :], in_=I["cv_swa"][l, g_].rearrange("(c p) d -> p c d", p=128)))(dup, g_))
                P.dma("pool", fl, writes=[b_cvd], ds=b_cvd.dsem)
                for g_ in range(2):
                    ps, bp = PS()
                    psb = ps[:, :].bitcast(BF16)

                    def ctr(e, psb=psb, g_=g_):
                        for ch in range(4):
                            r = e.transpose(psb[:, ch * 128:(ch + 1) * 128], in_=ckb[:, ch, g_, :, :].rearrange("p u d -> p (u d)"), identity=identb)
                        return r
                    P.op("pe", ctr, reads=[b_ckb, b_cb], writes=[bp])
                    for hh in range(2):
                        pr = slice(hh * 64, hh * 64 + 64)
                        P.op("act", (lambda psb, g_, pr, hh: lambda e: e.activation(out=ckT[pr, 2 * g_ + hh, :], in_=psb[pr, 0:512], func=AF.Copy))(psb, g_, pr, hh),
                             reads=[bp], writes=[b_ckT])
            slot, bsl = ring_load(wpiece(I["w_in"], l, 1040, 512), 512)
            for c in range(4):
                def ev(ps, bp, nt, c=c):
                    if isP:
                        if c < 3:
                            P.op("act", lambda e: e.activation(out=qTs[:, c, nt * 512:(nt + 1) * 512], in_=ps[:, :], func=AF.Identity,
                                                               bias=binT[:, 4 + c:5 + c]), reads=[bp, b_binT], writes=[b_qTs])
                        else:
                            evac_dup_k(ps, bp, nt, kTd, b_kTd, 7)
                        return
                    (q0, bq0), (q1, bq1) = qf[2 * ((2 * c + nt) % 2):2 * ((2 * c + nt) % 2) + 2]
                    P.op("act", lambda e: e.activation(out=q0, in_=ps[:, :], func=AF.Identity, bias=binT[:, 4 + c:5 + c]),
                         reads=[bp, b_binT], writes=[bq0])
                    psw, bpw = PS()
                    P.op("pe", lambda e: e.matmul(psw[:, :], lhsT=c32[:, C_PM:C_PM + 128], rhs=q0, start=True, stop=True),
                         reads=[bq0, b_c32], writes=[bpw])
                    tsl = slice(nt * 512, (nt + 1) * 512)
                    P.op("dve", lambda e: e.tensor_tensor(out=q1, in0=psw[:, :], in1=rope[:, 1024 + nt * 512:1024 + (nt + 1) * 512], op=ALU.mult),
                         reads=[bpw, b_rope], writes=[bq1])
                    P.op("dve", lambda e: e.tensor_tensor(out=q0, in0=q0, in1=rope[:, tsl], op=ALU.mult), reads=[bq0, b_rope], writes=[bq0])
                    if c < 3:
                        P.op("dve", lambda e: e.tensor_tensor(out=qTs[:, c, tsl], in0=q0, in1=q1, op=ALU.add), reads=[bq0, bq1], writes=[b_qTs])
                    else:
                        P.op("dve", lambda e: e.tensor_tensor(out=q0, in0=q0, in1=q1, op=ALU.add), reads=[bq0, bq1], writes=[bq0])
                        evac_dup_k(None, bq0, nt, kTd, b_kTd, None, src=q0)
                fm_proj(slot, bsl, c * 128, T, ev)
            if isP:
                for tcl in range(nch):
                    def evk(ps, bp, tcl=tcl):
                        st_, bst = kst[tcl % 2]
                        P.op("dve", lambda e: e.tensor_tensor(out=st_, in0=ps[:, 0:128], in1=bbc[:, 0:128], op=ALU.add), reads=[bp, b_bbc], writes=[bst])
                        si, t0 = tcl // 2, (tcl % 2) * 128
                        P.dma("sp", lambda e: e.dma_start(out=O["o_swa_k"][si, l, :, t0:t0 + 128, :].rearrange("g t d -> t g d"),
                                                          in_=st_.rearrange("p (g d) -> p g d", g=2)), reads=[bst], ds=bst.dsem, store=True)
                    tm_proj(slot, bsl, 384, 128, tcl, evk)
            slot, bsl = ring_load(wpiece(I["w_in"], l, 1552, 128), 128)
            for tcl in range(nch):
                def evv(ps, bp, tcl=tcl):
                    o4 = vd[:, tcl, :].rearrange("p (g u d) -> p g u d", g=2, u=2)
                    i4 = apx(ps[:, 0:1], [[64, 2], [0, 2], [1, 64]])
                    b4 = apx(bbc[:, 128:129], [[64, 2], [0, 2], [1, 64]])
                    P.op("dve", lambda e: e.tensor_tensor(out=o4, in0=i4, in1=b4, op=ALU.add), reads=[bp, b_bbc], writes=[b_vd])
                    if isP:
                        st_, bst = vst[tcl % 2]
                        P.op("dve", lambda e: e.tensor_tensor(out=st_, in0=ps[:, 0:128], in1=bbc[:, 128:256], op=ALU.add), reads=[bp, b_bbc], writes=[bst])
                        si, t0 = tcl // 2, (tcl % 2) * 128
                        P.dma("sp", lambda e: e.dma_start(out=O["o_swa_v"][si, l, :, t0:t0 + 128, :].rearrange("g t d -> t g d"),
                                                          in_=st_.rearrange("p (g d) -> p g d", g=2)), reads=[bst], ds=bst.dsem, store=True)
                tm_proj(slot, bsl, 0, 128, tcl, evv)
            jobn = [0]
            jobs = []

            def mkheads(g_, q0, nq):
                hs = []
                for h in range(3 * g_, 3 * g_ + 3):
                    p0 = (h % 2) * 64
                    tcl = q0 // 128
                    hs.append(dict(q=qTs[:, h // 2, q0:q0 + nq], half=h % 2,
                                   dst=mixT[p0:p0 + 64, 2 + h // 2, q0:q0 + nq], dstb=b_mix[tcl],
                                   sink=small[:, 1 + 3 * g_:4 + 3 * g_]))
                return hs

            def vsel(vt, tcl, g_, h):
                return vt[:, tcl, 2 * g_ * 64:2 * g_ * 64 + 64] if h % 2 == 0 else vt[:, tcl, 2 * g_ * 64:2 * g_ * 64 + 128]
            for g_ in range(2):
                for n in range(nch):
                    q0 = n * 128
                    heads = mkheads(g_, q0, 128)
                    tiles = []
                    if isP:
                        kcs = [(n // 2) * 2, (n // 2) * 2 + 1]
                        msk = [None, None]
                    else:
                        kcs = [j for j in (n - 1, n, n + 1) if 0 <= j < nch]
                        msk = [maskBb if j == n - 1 else (maskFb if j == n + 1 else None) for j in kcs]
                    for j, m in zip(kcs, msk):
                        tiles.append(dict(nk=128, kT=[kTd[:, 2 * g_ + h % 2, j * 128:(j + 1) * 128] for h in range(3 * g_, 3 * g_ + 3)],
                                          v=[vsel(vd, j, g_, h) for h in range(3 * g_, 3 * g_ + 3)], mask=m))
                    reads = [b_qTs, b_kTd, b_vd]
                    if not isP:
                        cvv = cvd.rearrange("p c g u d -> p c (g u d)")
                        for ch in range(4):
                            tiles.append(dict(nk=128, kT=[ckT[:, 2 * g_ + h % 2, ch * 128:(ch + 1) * 128] for h in range(3 * g_, 3 * g_ + 3)],
                                              v=[vsel(cvv, ch, g_, h) for h in range(3 * g_, 3 * g_ + 3)]))
                        reads += [b_ckT, b_cvd]
                    jobs.append(attn_job(heads, 128, tiles, reads, pts[jobn[0] % 2]))
                    jobn[0] += 1
            run_pipelined(jobs)

        def nat_phase(l, grp):
            isP = grp == "P"
            T = 512 if isP else 1024
            nch = T // 128
            AR.new_phase()
            qTn, b_qTn = AR.alloc("n_qT", [3, T], BF16)
            kTn, b_kTn = AR.alloc("n_kT", [6, T], BF16)
            P.op("dve", lambda e: e.memset(apx(kTn[64:128, 0, 0:1], [[2 * kTn.shape[-1], kTn.shape[1] // 2], [1, kTn.shape[-1]]]), 0.0), writes=[b_kTn])
            P.op("dve", lambda e: e.memset(apx(kTn[0:64, 1, 0:1], [[2 * kTn.shape[-1], kTn.shape[1] // 2], [1, kTn.shape[-1]]]), 0.0), writes=[b_kTn])
            vn, b_vn = AR.alloc("n_v", [nch, 384], BF16)
            bbc, b_bbc = load_bias_bc("n_bbc", l, 2064, 768)
            pts = [alloc_pt(9, "a"), alloc_pt(9, "b")]
            if isP:
                kst = [AR.alloc("n_kst%d" % i, [384], F32, DS("n_kst%d" % i)) for i in range(2)]
                vst = [AR.alloc("n_vst%d" % i, [384], F32, DS("n_vst%d" % i)) for i in range(2)]
            else:
                ckb, b_ckb = AR.alloc("n_ckb", [4, 384], BF16, DS("n_ckb"))
                cknT, b_cknT = AR.alloc("n_cknT", [6, 512], BF16)
                P.op("dve", lambda e: e.memset(apx(cknT[64:128, 0, 0:1], [[2 * cknT.shape[-1], cknT.shape[1] // 2], [1, cknT.shape[-1]]]), 0.0), writes=[b_cknT])
                P.op("dve", lambda e: e.memset(apx(cknT[0:64, 1, 0:1], [[2 * cknT.shape[-1], cknT.shape[1] // 2], [1, cknT.shape[-1]]]), 0.0), writes=[b_cknT])
                cvn, b_cvn = AR.alloc("n_cvn", [4, 384], BF16, DS("n_cvn"))
                P.dma("pool", [(lambda h: lambda e: e.dma_start(out=ckb[:, :, h * 64:(h + 1) * 64],
                                                                  in_=I["ck_nat"][l, h].rearrange("(c p) d -> p c d", p=128)))(h) for h in range(6)],
                      writes=[b_ckb], ds=b_ckb.dsem)
                P.dma("pool", [(lambda h: lambda e: e.dma_start(out=cvn[:, :, h * 64:(h + 1) * 64],
                                                                  in_=I["cv_nat"][l, h].rearrange("(c p) d -> p c d", p=128)))(h) for h in range(6)],
                      writes=[b_cvn], ds=b_cvn.dsem)
                for pr_ in range(3):
                    ps, bp = PS()
                    psb = ps[:, :].bitcast(BF16)

                    def ctr(e, psb=psb, pr_=pr_):
                        for ch in range(4):
                            r = e.transpose(psb[:, ch * 128:(ch + 1) * 128], in_=ckb[:, ch, pr_ * 128:(pr_ + 1) * 128], identity=identb)
                        return r
                    P.op("pe", ctr, reads=[b_ckb, b_cb], writes=[bp])
                    for hh in range(2):
                        prr = slice(hh * 64, hh * 64 + 64)
                        P.op("act", (lambda psb, pr_, prr, hh: lambda e: e.activation(out=cknT[prr, 2 * pr_ + hh, :], in_=psb[prr, 0:512], func=AF.Copy))(psb, pr_, prr, hh),
                             reads=[bp], writes=[b_cknT])
                tab, b_tab = AR.alloc("n_tab", [6, 16, 64], BF16)
                rsb, b_rsb = AR.alloc("n_rsb", [31], F32, DS("n_rsb"))
                RT, b_RT = AR.alloc("n_RT", [90], F32)
                P.dma("sp", lambda e: e.dma_start(out=rsb[0:90, :], in_=I["rpb"][l].rearrange("h i j -> (h i) j")), writes=[b_rsb], ds=b_rsb.dsem)
                ps, bp = PS()
                P.op("pe", (lambda ps: lambda e: e.transpose(ps[0:31, 0:90], in_=rsb[0:90, :], identity=ident32[0:90, 0:90]))(ps), reads=[b_rsb, b_c32], writes=[bp])
                P.op("act", (lambda ps: lambda e: e.activation(out=RT[0:31, :], in_=ps[0:31, 0:90], func=AF.Copy, scale=8.0))(ps), reads=[bp], writes=[b_RT])
                b_tab0 = P.buf("n_tab_init", init=dict(b_tab.rs))
                AR.cur.append(b_tab0)
                P.op("dve", lambda e: e.memset(tab[:, :, :, :], 0.0), writes=[b_tab0])
                for c0 in range(0, 64, 5):
                    ncg = min(5, 64 - c0)
                    ps, bp = PS()

                    def tmm(e, ps=ps, c0=c0, ncg=ncg):
                        for ci in range(ncg):
                            c = c0 + ci
                            r = e.matmul(ps[:, ci * 90:(ci + 1) * 90], lhsT=c32[0:31, C_MSH + 63 - c:C_MSH + 63 - c + 128], rhs=RT[0:31, :],
                                         start=True, stop=True)
                        return r
                    P.op("pe", tmm, reads=[b_RT, b_c32], writes=[bp])
                    for half in range(2):
                        pr = slice(half * 64, (half + 1) * 64)
                        ii0 = 1 - half
                        o_ = apx(tab[pr, 0, ii0, c0:c0 + 1], [[16 * 64, 6], [64, 15], [1, ncg]])
                        i_ = apx(ps[pr, 0:1], [[15, 6], [1, 15], [90, ncg]])
                        m_ = apx(c32[pr, C_MN + c0:C_MN + c0 + 1], [[0, 6], [0, 15], [1, ncg]])
                        P.op("dve", (lambda o_, i_, m_: lambda e: e.tensor_tensor(out=o_, in0=i_, in1=m_, op=ALU.add))(o_, i_, m_),
                             reads=[bp, b_c32, b_tab0], writes=[b_tab])
            if (not isP) and dbg and "cknT" in dbg and l == 0:
                P.dma("pool", lambda e: e.dma_start(out=O["cknT"][:, :], in_=cknT.rearrange("p h k -> p (h k)")), reads=[b_cknT], ds=DS("dbgck"), store=True)
            if (not isP) and dbg and "tab" in dbg and l == 0:
                P.dma("pool", lambda e: e.dma_start(out=O["tab"][:, :], in_=tab.rearrange("p h i c -> p (h i c)")), reads=[b_tab], ds=DS("dbgtab"), store=True)
            slot, bsl = ring_load(wpiece(I["w_in"], l, 1680, 384), 384)
            for c in range(3):
                fm_proj(slot, bsl, c * 128, T, lambda ps, bp, nt, c=c: P.op(
                    "act", lambda e: e.activation(out=qTn[:, c, nt * 512:(nt + 1) * 512], in_=ps[:, :], func=AF.Identity,
                                                  bias=binT[:, 8 + c:9 + c]), reads=[bp, b_binT], writes=[b_qTn]))
            slot, bsl = ring_load(wpiece(I["w_in"], l, 2064, 384), 384)
            for c in range(3):
                def evkn(ps, bp, nt, c=c):
                    for hh in range(2):
                        prr = slice(hh * 64, hh * 64 + 64)
                        P.op("act", (lambda prr, hh: lambda e: e.activation(
                            out=kTn[prr, 2 * c + hh, nt * 512:(nt + 1) * 512], in_=ps[prr, :], func=AF.Identity,
                            bias=binT[prr, 11 + c:12 + c]))(prr, hh), reads=[bp, b_binT], writes=[b_kTn])
                fm_proj(slot, bsl, c * 128, T, evkn)
            if isP:
                for tcl in range(nch):
                    def evk(ps, bp, tcl=tcl):
                        st_, bst = kst[tcl % 2]
                        P.op("dve", lambda e: e.tensor_tensor(out=st_, in0=ps[:, 0:384], in1=bbc[:, 0:384], op=ALU.add), reads=[bp, b_bbc], writes=[bst])
                        si, t0 = tcl // 2, (tcl % 2) * 128
                        P.dma("sp", lambda e: e.dma_start(out=O["o_nat_k"][si, l, :, t0:t0 + 128, :].rearrange("h t d -> t h d"),
                                                          in_=st_.rearrange("p (h d) -> p h d", h=6)), reads=[bst], ds=bst.dsem, store=True)
                    tm_proj(slot, bsl, 0, 384, tcl, evk)
            slot, bsl = ring_load(wpiece(I["w_in"], l, 2448, 384), 384)
            for tcl in range(nch):
                def evv(ps, bp, tcl=tcl):
                    P.op("dve", lambda e: e.tensor_tensor(out=vn[:, tcl, :], in0=ps[:, 0:384], in1=bbc[:, 384:768], op=ALU.add), reads=[bp, b_bbc], writes=[b_vn])
                    if isP:
                        st_, bst = vst[tcl % 2]
                        P.op("dve", lambda e: e.tensor_tensor(out=st_, in0=ps[:, 0:384], in1=bbc[:, 384:768], op=ALU.add), reads=[bp, b_bbc], writes=[bst])
                        si, t0 = tcl // 2, (tcl % 2) * 128
                        P.dma("sp", lambda e: e.dma_start(out=O["o_nat_v"][si, l, :, t0:t0 + 128, :].rearrange("h t d -> t h d"),
                                                          in_=st_.rearrange("p (h d) -> p h d", h=6)), reads=[bst], ds=bst.dsem, store=True)
                tm_proj(slot, bsl, 0, 384, tcl, evv)
            def vsel(vt, tcl, h):
                return vt[:, tcl, h * 64:(h + 1) * 64] if h % 2 == 0 else vt[:, tcl, (h - 1) * 64:(h + 1) * 64]

            def hp(h):
                return slice((h % 2) * 64, (h % 2) * 64 + 64)
            jn = 0
            jobs = []
            if isP:
                for si in range(2):
                    for hb in range(2):
                        hl = list(range(3 * hb, 3 * hb + 3))
                        for n in range(2):
                            q0 = si * 256 + n * 128
                            heads = [dict(q=qTn[:, h // 2, q0:q0 + 128], half=h % 2, dst=mixT[hp(h), 5 + h // 2, q0:q0 + 128],
                                          dstb=b_mix[q0 // 128]) for h in hl]
                            tiles = [dict(nk=128, kT=[kTn[:, h, j * 128:(j + 1) * 128] for h in hl], v=[vsel(vn, j, h) for h in hl])
                                     for j in (si * 2, si * 2 + 1)]
                            jobs.append(attn_job(heads, 128, tiles, [b_qTn, b_kTn, b_vn], pts[jn % 2]))
                            jn += 1
            else:
                hl = list(range(6))
                for r in range(16):
                    q0 = r * 64
                    r0 = min(max(r - 4, 0), 8)
                    heads = [dict(q=qTn[:, h // 2, q0:q0 + 64], half=h % 2, dst=mixT[hp(h), 5 + h // 2, q0:q0 + 64],
                                  dstb=b_mix[q0 // 128], dbgflag=(r == 0 and l == 0)) for h in hl]
                    tiles = []
                    for jj in range(r0 // 2, (r0 + 7) // 2 + 1):
                        ev_ok = r0 <= 2 * jj <= r0 + 7
                        od_ok = r0 <= 2 * jj + 1 <= r0 + 7
                        vr = (0 if ev_ok else 64, 128 if od_ok else 64)
                        ii = 2 * jj - r + 8
                        ii = min(max(ii, 0), 15)
                        tiles.append(dict(nk=128, kT=[kTn[:, h, jj * 128:(jj + 1) * 128] for h in hl], v=[vsel(vn, jj, h) for h in hl],
                                          bias=tab[:, :, ii, :], vrows=vr))
                    for ch in range(4):
                        tiles.append(dict(nk=128, kT=[cknT[:, h, ch * 128:(ch + 1) * 128] for h in hl], v=[vsel(cvn, ch, h) for h in hl]))
                    jobs.append(attn_job(heads, 64, tiles, [b_qTn, b_kTn, b_vn, b_cknT, b_cvn, b_tab], pts[jn % 2]))
                    jn += 1
            run_pipelined(jobs)

        def make_bc_from_mod(name, l, which, v):
            t, b = AR.alloc(name, [D], F32)
            for half in range(2):
                ps, bp = PS()

                def mm(e, ps=ps, half=half):
                    for kk in range(4):
                        k = half * 4 + kk
                        r = e.matmul(ps[:, kk * 128:(kk + 1) * 128], lhsT=apx(modT[:, l, which, k, v:v + 1], [[0, 128]]), rhs=ident32,
                                     start=True, stop=True)
                    return r
                P.op("pe", mm, reads=[b_modl[l], b_c32], writes=[bp])
                P.op("act", (lambda ps, half: lambda e: e.activation(out=t[:, half * 512:(half + 1) * 512], in_=ps[:, :], func=AF.Copy))(ps, half),
                     reads=[bp], writes=[b])
            return t, b

        def load_ln_bc(name, src, l, scale):
            t, b = AR.alloc(name, [D], F32, DS(name))
            P.dma("sp", lambda e: e.dma_start(out=t, in_=dram_bc(src[l, :], D)), writes=[b], ds=b.dsem)
            if scale != 1.0:
                P.op("act", lambda e: e.activation(out=t, in_=t, func=AF.Copy, scale=scale), reads=[b], writes=[b])
            return t, b

        def acc_into_xa(tc, half, ps, bp, g_bc, bg, tmp):
            t_, bt = tmp
            hs = slice(half * 512, (half + 1) * 512)
            P.op("dve", lambda e: e.tensor_tensor(out=t_, in0=ps[:, :], in1=g_bc[:, hs], op=ALU.mult), reads=[bp, bg], writes=[bt])
            P.op("dve", lambda e: e.tensor_tensor(out=xa[:, tc, hs], in0=xa[:, tc, hs], in1=t_, op=ALU.add), reads=[bt, b_xa[tc]], writes=[b_xa[tc]])

        def ln_inplace(tc, lg, blg, lb, blb, wk, out_dram=None):
            (nh_, bnh), (sm, bsm) = wk
            y = xa[:, tc, :]
            by = b_xa[tc]
            st6 = sm[:, 0:12].rearrange("p (a b) -> p a b", a=2)
            for half in range(2):
                P.op("dve", (lambda half: lambda e: e.bn_stats(out=st6[:, half, :], in_=y[:, half * 512:(half + 1) * 512]))(half), reads=[by], writes=[bsm])
            mv = sm[:, 12:14]
            P.op("dve", lambda e: e.bn_aggr(out=mv, in_=sm[:, 0:12]), reads=[bsm], writes=[bsm])
            rs_, nm_ = sm[:, 14:15], sm[:, 15:16]
            P.op("act", lambda e: e.activation(out=rs_, in_=sm[:, 13:14], func=AF.Sqrt, bias=small[:, 0:1]), reads=[bsm, b_small], writes=[bsm])
            P.op("dve", lambda e: e.reciprocal(out=rs_, in_=rs_), reads=[bsm], writes=[bsm])
            P.op("dve", lambda e: e.scalar_tensor_tensor(out=nm_, in0=sm[:, 12:13], scalar=-1.0, in1=rs_, op0=ALU.mult, op1=ALU.mult), reads=[bsm], writes=[bsm])
            P.op("act", lambda e: e.activation(out=nh_, in_=y, func=AF.Identity, scale=rs_, bias=nm_), reads=[by, bsm], writes=[bnh])
            P.op("dve", lambda e: e.tensor_tensor(out=nh_, in0=nh_, in1=lg, op=ALU.mult), reads=[bnh, blg], writes=[bnh])
            P.op("dve", lambda e: e.tensor_tensor(out=xa[:, tc, :], in0=nh_, in1=lb, op=ALU.add), reads=[bnh, blb], writes=[b_xa[tc]])
            if out_dram is not None:
                P.dma("sp", lambda e: e.dma_start(out=out_dram, in_=xa[:, tc, :]), reads=[b_xa[tc]], ds=b_xa[tc].dsem, store=True)

        def ln_work(tag):
            return [(AR.alloc("nh%s%d" % (tag, i), [D], F32), AR.alloc("lsm%s%d" % (tag, i), [16], F32)) for i in range(2)]

        def wout_phase(l, grp):
            isP = grp == "P"
            v = 0 if isP else 1
            chunks = list(range(0, 4)) if isP else list(range(4, 12))
            AR.new_phase()
            ga, bga = make_bc_from_mod("ga_bc", l, 2, v)
            lg, blg = load_ln_bc("ln1g", I["ln1_g"], l, ALPHA)
            lb, blb = load_ln_bc("ln1b", I["ln1_b"], l, ALPHA)
            wk = ln_work("a")
            tmps = [AR.alloc("atmp%d" % i, [512], F32) for i in range(2)]
            slots = [ring_load(wpiece(I["w_out"], l, nh * 512, 512), 512) for nh in range(2)]
            for i, tc in enumerate(chunks):
                for nh in range(2):
                    ps, bp = PS()
                    slot, bsl = slots[nh]

                    def mm(e, ps=ps, slot=slot, i=i):
                        for k in range(KC):
                            r = e.matmul(ps[:, :], lhsT=mixT[:, k, i * 128:(i + 1) * 128], rhs=slot[:, k, :], start=(k == 0), stop=(k == KC - 1))
                        return r
                    P.op("pe", mm, reads=[bsl, b_mix[i]], writes=[bp])
                    acc_into_xa(tc, nh, ps, bp, ga, bga, tmps[nh])
                ln_inplace(tc, lg, blg, lb, blb, wk[i % 2])

        def mlp_phase(l, last):
            a2 = 1.0 if last else ALPHA
            NT = 768
            AR.new_phase()
            hid, b_hid = AR.alloc("hid", [32, NT], BF16)
            gms = {v: make_bc_from_mod("gm_bc%d" % v, l, 5, v) for v in (0, 1)}
            lg, blg = load_ln_bc("ln2g", I["ln2_g"], l, a2)
            lb, blb = load_ln_bc("ln2b", I["ln2_b"], l, a2)
            wk = ln_work("m")
            rl = [AR.alloc("rl%d" % i, [NT], F32) for i in range(2)]
            tmps = [(r_[:, 0:512], br) for (r_, br) in rl]
            rows_m = alloc_rows("m") if l + 1 < nlayers else None
            groups = [list(range(0, 6)), list(range(6, 12))]

            def do_hT(gi):
                if gi == 0:
                    transposes_to_hT(l, 2, 3, [0, 1, 2, 3], 0, 0)
                    transposes_to_hT(l, 2, 3, [4, 5], 1, 4)
                else:
                    transposes_to_hT(l, 2, 3, [6, 7, 8, 9], 1, 0)
                    transposes_to_hT(l, 2, 3, [10, 11], 1, 4)

            def do_ln(tc, i):
                od = None
                if last:
                    od = O["y_p"][tc * 128:(tc + 1) * 128, :] if tc < 4 else O["y_s"][(tc - 4) * 128:(tc - 3) * 128, :]
                ln_inplace(tc, lg, blg, lb, blb, wk[i % 2], out_dram=od)

            def do_S1(gi, after_piece):
                for pc in range(8):
                    slot, bsl = ring_load(wpiece(I["w_mlp1"], l, pc * 512, 512), 512)
                    for fq in range(4):
                        fc = pc * 4 + fq
                        psA, bpA = PS()
                        psB, bpB = PS()

                        def mm(e, psA=psA, psB=psB, slot=slot, fq=fq):
                            for k in range(KC):
                                e.matmul(psA[:, :], lhsT=slot[:, k, fq * 128:(fq + 1) * 128], rhs=hT[:, k, 0:512], start=(k == 0), stop=(k == KC - 1))
                            for k in range(KC):
                                r = e.matmul(psB[:, 0:256], lhsT=slot[:, k, fq * 128:(fq + 1) * 128], rhs=hT[:, k, 512:768], start=(k == 0), stop=(k == KC - 1))
                            return r
                        P.op("pe", mm, reads=[bsl] + b_hT[0:6], writes=[bpA, bpB])
                        r_, br = rl[fc % 2]
                        P.op("act", (lambda psA, r_: lambda e: e.activation(out=r_[:, 0:512], in_=psA[:, :], func=AF.Relu))(psA, r_), reads=[bpA], writes=[br])
                        P.op("act", (lambda psB, r_: lambda e: e.activation(out=r_[:, 512:768], in_=psB[:, 0:256], func=AF.Relu))(psB, r_), reads=[bpB], writes=[br])
                        P.op("dve", (lambda r_, fc: lambda e: e.tensor_tensor(out=hid[:, fc, :], in0=r_[:, 0:NT], in1=r_[:, 0:NT], op=ALU.mult))(r_, fc),
                             reads=[br], writes=[b_hid])
                    after_piece(pc)

            def do_S2(gi):
                chunks = groups[gi]
                for nh in range(2):
                    accs = [PS() for _ in range(6)]
                    for pc in range(4):
                        slot, bsl = ring_load(wpiece(I["w_mlp2"], l, nh * 512, 512, r0=pc * 1024), 512)

                        def mm(e, slot=slot, pc=pc, accs=accs):
                            for i in range(6):
                                for k in range(KC):
                                    r = e.matmul(accs[i][0][:, :], lhsT=hid[:, pc * 8 + k, i * 128:(i + 1) * 128], rhs=slot[:, k, :],
                                                 start=(pc == 0 and k == 0), stop=(pc == 3 and k == KC - 1))
                            return r
                        P.op("pe", mm, reads=[bsl, b_hid], writes=[a[1] for a in accs])
                    for i, tc in enumerate(chunks):
                        gm, bgm = gms[0 if tc < 4 else 1]
                        acc_into_xa(tc, nh, accs[i][0], accs[i][1], gm, bgm, tmps[i % 2])

            def ada_cb(base):
                def cb(pc):
                    ja = base + pc
                    if rows_m is not None and ja < 12:
                        ada_piece(l + 1, ja, rows_m)
                return cb

            do_hT(0)
            do_S1(0, ada_cb(0))
            do_S2(0)
            do_hT(1)
            cb1 = ada_cb(8)

            def cbB(pc):
                cb1(pc)
                if pc < 6:
                    do_ln(groups[0][pc], pc)
            do_S1(1, cbB)
            do_S2(1)
            for i, tc in enumerate(groups[1]):
                do_ln(tc, i)

        dbg_ds = P.new_dsem("dbg")

        def dump_mix(name, ncols):
            if dbg and name in dbg:
                P.dma("pool", lambda e: e.dma_start(out=O[name][:, :, :], in_=mixT[:, :, 0:ncols]), reads=b_mix, ds=dbg_ds, store=True)

        def dump_xa(name):
            if dbg and name in dbg:
                P.dma("sp", lambda e: e.dma_start(out=O[name][:, :, :], in_=xa[:, :, :]), reads=b_xa, ds=dbg_ds, store=True)

        def chk(name):
            if stop == name:
                raise _Stop()

        try:
          for l in range(nlayers):
              chk('ada')
              with nc.allow_non_contiguous_dma(reason="per-partition bias layout"):
                  fl = []
                  for (c0, n, col) in ((0, 4, 0), (1040, 4, 4), (1680, 6, 8)):
                      srcb = I["b_in"][l, c0:c0 + n * 128].rearrange("(c p) -> p c", p=128)
                      fl.append((lambda srcb, n, col: lambda e: e.dma_start(out=binT[:, col:col + n], in_=srcb))(srcb, n, col))
                  P.dma("sp", fl, writes=[b_binT], ds=b_binT.dsem)
                  fl = []
                  for (c0, col) in ((1024, 0), (1032, 32), (1028, 64), (1036, 96)):
                      srcw = I["w_in"][l, :, c0:c0 + 4].rearrange("(k p) n -> p k n", p=128)
                      fl.append((lambda srcw, col: lambda e: e.dma_start(out=gw[:, :, col:col + 4], in_=srcw))(srcw, col))
                  P.dma("pool", fl, writes=[b_gw], ds=b_gw.dsem)
              P.op("act", lambda e: e.activation(out=binT[:, 2:4], in_=binT[:, 2:4], func=AF.Copy, scale=0.125), reads=[b_binT], writes=[b_binT])
              for (dst_i, src_i) in ((0, 1), (2, 4)):
                  P.op("dve", (lambda dst_i, src_i, l=l: lambda e: e.tensor_scalar(out=pv[:, dst_i, :, :], in0=modT[:, l, src_i, :, :], scalar1=1.0,
                                                                              scalar2=1.0 / ALPHA, op0=ALU.add, op1=ALU.mult))(dst_i, src_i),
                       reads=[b_modl[l]], writes=[b_pv])
              for grp in ("P", "S"):
                  chunks = list(range(0, 4)) if grp == "P" else list(range(4, 12))
                  transposes_to_hT(l, 0, 0, chunks, 0 if grp == "P" else 1)
                  chk('hT' + grp)
                  mlstm_phase(l, grp)
                  chk('ml' + grp)
                  swa_phase(l, grp)
                  chk('swa' + grp)
                  nat_phase(l, grp)
                  chk('nat' + grp)
                  if l == 0:
                      dump_mix("mix" + grp, 512 if grp == "P" else 1024)
                  wout_phase(l, grp)
                  chk('wout' + grp)
              if l == 0:
                  dump_xa("xa_ln1")
              mlp_phase(l, l == nlayers - 1)
              if l == 0:
                  dump_xa("xa_l0")

        except _Stop:
            pass

        fin = {}
        for s_, v_ in P.stores:
            fin[s_] = max(fin.get(s_, 0), v_)
        P._wait(P.q["sp"], fin)
        P.n_items = {k: len(q.items) for k, q in P.q.items()}
        with nc.Block() as block:
            @block.tensor
            def _(e):
                P.replay("pe", e)

            @block.scalar
            def _(e):
                P.replay("act", e)

            @block.vector
            def _(e):
                P.replay("dve", e)

            @block.gpsimd
            def _(e):
                with nc.allow_non_contiguous_dma(reason="small strided loads"):
                    P.replay("pool", e)

            @block.sync
            def _(e):
                with nc.allow_non_contiguous_dma(reason="small strided loads"):
                    P.replay("sp", e)
        nc._prog_stats = (P.n_items, P.nsem)
    return nc


def shard_inputs(inputs, ncores=8):
    consts, rope = make_consts()
    f = lambda a: np.ascontiguousarray(np.asarray(a, dtype=np.float32))
    maps = []
    for c in range(ncores):
        m = {
            "xp": f(inputs["x_prompt"][2 * c:2 * c + 2]).reshape(512, D),
            "xs": f(inputs["x_sample"][c]),
            "ck_swa": f(inputs["cache_swa_k"][c]), "cv_swa": f(inputs["cache_swa_v"][c]),
            "ck_nat": f(inputs["cache_nat_k"][c]), "cv_nat": f(inputs["cache_nat_v"][c]),
            "stC": f(inputs["state_mlstm_C"][c]), "stn": f(inputs["state_mlstm_n"][c]), "stm": f(inputs["state_mlstm_m"][c]),
            "cvec": f(np.stack([np.asarray(inputs["c_ctx"]), np.asarray(inputs["c"][c])], 0)),
            "w_ada": f(inputs["w_ada"]), "b_ada": f(inputs["b_ada"]), "w_in": f(inputs["w_in"]), "b_in": f(inputs["b_in"]),
            "fbias": f(inputs["mlstm_fbias"]), "norm_g": f(inputs["mlstm_norm_g"]), "sink": f(inputs["swa_sink"]), "rpb": f(inputs["nat_rpb"]),
            "w_out": f(inputs["w_out"]), "ln1_g": f(inputs["ln1_g"]), "ln1_b": f(inputs["ln1_b"]),
            "w_mlp1": f(inputs["w_mlp1"]), "w_mlp2": f(inputs["w_mlp2"]), "ln2_g": f(inputs["ln2_g"]), "ln2_b": f(inputs["ln2_b"]),
            "consts": consts, "rope": rope,
        }
        maps.append(m)
    return maps


_NC_CACHE = {}


def kernel(**inputs):
    if "nc" not in _NC_CACHE:
        _NC_CACHE["nc"] = build_program()
    nc = _NC_CACHE["nc"]
    maps = shard_inputs(inputs, 8)
    res = run_bass_kernel_spmd(nc, maps, core_ids=list(range(8)))
    R = res.results
    cat = lambda k: np.concatenate([np.asarray(r[k]) for r in R], axis=0)
    y_p = cat("y_p").reshape(16, 256, D)
    y_s = np.stack([np.asarray(r["y_s"]) for r in R], 0)
    outs = [y_p, y_s]
    for k in ("o_swa_k", "o_swa_v", "o_nat_k", "o_nat_v", "o_C", "o_n", "o_m"):
        outs.append(cat(k))
    return tuple(np.ascontiguousarray(o.astype(np.float32)) for o in outs)
```

```python
import contextlib
import numpy as np
import concourse.bass as bass
import concourse.mybir as mybir
from concourse.bass_utils import run_bass_kernel_spmd

F32 = mybir.dt.float32
BF16 = mybir.dt.bfloat16
ALU = mybir.AluOpType
AF = mybir.ActivationFunctionType
AX = mybir.AxisListType
AP = bass.AP

NL = 4
D = 1024
KC = 8
IN_DIM = 2832
DFF = 4096
ALPHA = (2 * NL) ** 0.25
EPS = 1e-5
NEG = -10000.0
SLOTW = 576
NSLOT = 4
ARENA_BYTES = 85 * 1024

C_ID, C_MF, C_MB, C_PM, C_MSH, C_MN = 0, 128, 256, 384, 512, 704
NCONST = 768


def make_consts():
    c = np.zeros((128, NCONST), np.float32)
    c[:, C_ID:C_ID + 128] = np.eye(128, dtype=np.float32)
    s = np.arange(128)[:, None]
    t = np.arange(128)[None, :]
    c[:, C_MF:C_MF + 128] = (t >= s)
    c[:, C_MB:C_MB + 128] = (t <= s)
    pm = np.zeros((128, 128), np.float32)
    for m in range(128):
        d = m % 64
        partner = d + 16 if (d % 32) < 16 else d - 16
        pm[(m // 64) * 64 + partner, m] = 1.0
    c[:, C_PM:C_PM + 128] = pm
    for j in range(31):
        c[j, C_MSH + j + 48] = 1.0
        c[j, C_MSH + j + 112] = 1.0
    kc = np.arange(64)[:, None]
    cc = np.arange(64)[None, :]
    cs = np.clip(cc - 8, 0, 48)
    ok = (kc >= cs) & (kc < cs + 16)
    mn = np.where(ok, 0.0, NEG * 8.0).astype(np.float32)
    c[0:64, C_MN:C_MN + 64] = mn
    c[64:128, C_MN:C_MN + 64] = mn
    tt = np.arange(1024)
    rope = np.zeros((128, 2048), np.float32)
    fr = 10000.0 ** (-np.arange(16, dtype=np.float32) / 16.0)
    for p in range(128):
        d = p % 64
        pos = (tt // 64) if d < 32 else (tt % 64)
        i = d % 16
        ang = pos.astype(np.float32) * fr[i]
        rope[p, :1024] = np.cos(ang)
        sn = np.sin(ang)
        rope[p, 1024:] = -sn if (d % 32) < 16 else sn
    return c, rope


class DSem:
    def __init__(self, sem):
        self.sem = sem
        self.count = 0


class Buf:
    __slots__ = ("name", "w", "rs", "dsem")

    def __init__(self, name, dsem=None, init=None):
        self.name = name
        self.w = None
        self.rs = dict(init) if init else {}
        self.dsem = dsem


class EngQ:
    def __init__(self, name):
        self.name = name
        self.items = []
        self.sem = None
        self.count = 0
        self.waited = {}


class Prog:
    def __init__(self, nc, stack):
        self.nc = nc
        self.stack = stack
        self.q = {n: EngQ(n) for n in ("pe", "act", "dve", "pool", "sp")}
        self.nsem = 0
        for n in ("pe", "act", "dve"):
            self.q[n].sem = self.new_sem("e_" + n)
        self.stores = []
        self.all_bufs = []

    def new_sem(self, name):
        self.nsem += 1
        return self.stack.enter_context(self.nc.semaphore(name + "_%d" % self.nsem))

    def new_dsem(self, name):
        return DSem(self.new_sem("d_" + name))

    def buf(self, name, dsem=None, init=None):
        b = Buf(name, dsem, init)
        self.all_bufs.append(b)
        return b

    def _deps(self, reads, writes, own_sem=None):
        deps = {}

        def add(tok):
            s, v = tok
            if deps.get(s, 0) < v:
                deps[s] = v
        for b in reads:
            if b.w is not None:
                add(b.w)
        rset = set(id(b) for b in reads)
        for b in writes:
            if b.w is not None and not (own_sem is not None and b.w[0] is own_sem and id(b) not in rset):
                add(b.w)
            for s, v in b.rs.items():
                add((s, v))
        return deps

    def _wait(self, q, deps):
        for s, v in deps.items():
            if q.waited.get(s, 0) < v:
                q.waited[s] = v
                q.items.append(("wait", s, v))

    def _commit(self, tok, reads, writes):
        s, v = tok
        for b in reads:
            if b.rs.get(s, 0) < v:
                b.rs[s] = v
        for b in writes:
            b.w = tok
            b.rs = {}

    def op(self, e, fn, reads=(), writes=()):
        q = self.q[e]
        self._wait(q, self._deps(reads, writes, own_sem=q.sem))
        if q.count >= 60000:
            q.sem = self.new_sem("e_" + e)
            q.count = 0
        q.count += 1
        tok = (q.sem, q.count)
        q.items.append(("op", fn, q.sem, 1))
        self._commit(tok, reads, writes)
        return tok

    def dma(self, e, fns, reads=(), writes=(), ds=None, store=False):
        q = self.q[e]
        self._wait(q, self._deps(reads, writes))
        if not isinstance(fns, (list, tuple)):
            fns = [fns]
        for fn in fns:
            ds.count += 16
            q.items.append(("op", fn, ds.sem, 16))
        tok = (ds.sem, ds.count)
        self._commit(tok, reads, writes)
        if store:
            self.stores.append(tok)
        return tok

    def retire(self, bufs):
        deps = {}
        for b in bufs:
            toks = list(b.rs.items())
            if b.w is not None:
                toks.append(b.w)
            for s, v in toks:
                if deps.get(s, 0) < v:
                    deps[s] = v
        return deps

    def replay(self, name, eng):
        for it in self.q[name].items:
            if it[0] == "wait":
                eng.wait_ge(it[1], it[2])
            else:
                ins = it[1](eng)
                ins.then_inc(it[2], it[3])


class Arena:
    def __init__(self, P, tile):
        self.P = P
        self.tile = tile
        self.off = 0
        self.cur = []
        self.cur_rng = []
        self.old = []

    def new_phase(self):
        known = set(id(b) for _, _, bl in self.cur_rng for b in bl)
        extra = [b for b in self.cur if id(b) not in known]
        if extra and self.cur_rng:
            lo = min(o for o, _, _ in self.cur_rng)
            hi = max(o + n for o, n, _ in self.cur_rng)
            self.cur_rng.append((lo, hi - lo, extra))
        self.old = self.old + self.cur_rng
        self.cur_rng = []
        self.cur = []
        self.off = 0

    def alloc(self, name, shape, dt, dsem=None):
        n = int(np.prod(shape))
        esz = 4 if dt == F32 else 2
        nb = (n * esz + 31) // 32 * 32
        assert self.off + nb <= ARENA_BYTES, (name, self.off, nb)
        a = self.tile[:, self.off // 2:(self.off + n * esz) // 2]
        if dt == F32:
            a = a.bitcast(F32)
        if len(shape) > 1:
            dims, stp = [], 1
            for sz in reversed(shape):
                dims.insert(0, [stp, sz])
                stp *= sz
            a = AP(a.tensor, a.offset, [list(a.ap[0])] + dims)
        lo, hi = self.off, self.off + nb
        ov = [bl for (o, n_, bl) in self.old if o < hi and o + n_ > lo]
        init = self.P.retire([b for bl in ov for b in bl])
        self.old = [(o, n_, bl) for (o, n_, bl) in self.old if not (o >= lo and o + n_ <= hi)]
        self.off += nb
        b = self.P.buf(name, dsem, init=init)
        self.cur.append(b)
        self.cur_rng.append((lo, nb, [b]))
        return a, b


def apx(a, dims, off=0):
    return AP(a.tensor, a.offset + off, [list(a.ap[0])] + [list(d) for d in dims])


IN_SPECS = [
    ("xp", [512, D]), ("xs", [1024, D]),
    ("ck_swa", [NL, 2, 512, 64]), ("cv_swa", [NL, 2, 512, 64]),
    ("ck_nat", [NL, 6, 512, 64]), ("cv_nat", [NL, 6, 512, 64]),
    ("stC", [NL, 2, 4, 64, 64]), ("stn", [NL, 2, 4, 64]), ("stm", [NL, 2, 4]),
    ("cvec", [2, D]),
    ("w_ada", [NL, D, 6 * D]), ("b_ada", [NL, 6 * D]), ("w_in", [NL, D, IN_DIM]), ("b_in", [NL, IN_DIM]),
    ("fbias", [NL, 2, 4]), ("norm_g", [NL, 256]), ("sink", [NL, 6]), ("rpb", [NL, 6, 15, 31]),
    ("w_out", [NL, D, D]), ("ln1_g", [NL, D]), ("ln1_b", [NL, D]),
    ("w_mlp1", [NL, D, DFF]), ("w_mlp2", [NL, DFF, D]), ("ln2_g", [NL, D]), ("ln2_b", [NL, D]),
    ("consts", [128, NCONST]), ("rope", [128, 2048]),
]
OUT_SPECS = [
    ("y_p", [512, D]), ("y_s", [1024, D]),
    ("o_swa_k", [2, NL, 2, 256, 64]), ("o_swa_v", [2, NL, 2, 256, 64]),
    ("o_nat_k", [2, NL, 6, 256, 64]), ("o_nat_v", [2, NL, 6, 256, 64]),
    ("o_C", [2, NL, 2, 4, 64, 64]), ("o_n", [2, NL, 2, 4, 64]), ("o_m", [2, NL, 2, 4]),
]


class _Stop(Exception):
    pass


def build_program(nlayers=NL, dbg=None, stop=None):
    nc = bass.Bass("TRN2", target_bir_lowering=False)
    I = {n: nc.dram_tensor(n, s, F32, kind="ExternalInput").ap() for n, s in IN_SPECS}
    O = {n: nc.dram_tensor(n, s, F32, kind="ExternalOutput").ap() for n, s in OUT_SPECS}
    if dbg:
        for n, s in dbg.items():
            O[n] = nc.dram_tensor(n, s, F32, kind="ExternalOutput").ap()
    with contextlib.ExitStack() as st:
        P = Prog(nc, st)

        def SB(name, shape, dt):
            return st.enter_context(nc.sbuf_tensor(name, shape, dt))

        c32 = SB("c32", [128, NCONST], F32); b_c32 = P.buf("c32", P.new_dsem("c32"))
        cb = SB("cb", [128, 3, 128], BF16); b_cb = P.buf("cb")
        onesb = SB("onesb", [128, 128], BF16); b_ones = P.buf("ones")
        ones32 = SB("ones32", [128, 128], F32)
        xa = SB("xa", [128, 12, D], F32)
        b_xa = [P.buf("xa%d" % i, P.new_dsem("xa%d" % i)) for i in range(12)]
        ring = SB("ring", [128, NSLOT, KC, 512], BF16)
        b_ring = [P.buf("ring%d" % i, P.new_dsem("ring%d" % i)) for i in range(NSLOT)]
        hT = SB("hT", [128, KC, 1024], BF16)
        b_hT = [P.buf("hT%d" % i) for i in range(8)]
        mixT = SB("mixT", [128, KC, 1024], BF16)
        b_mix = [P.buf("mix%d" % i) for i in range(8)]
        modT = SB("modT", [128, NL, 6, KC, 2], F32)
        b_modl = [P.buf("modT%d" % i) for i in range(NL)]
        badaT = SB("badaT", [128, NL, 48], F32); b_badaT = P.buf("badaT", P.new_dsem("badaT"))
        pv = SB("pv", [128, 6, KC, 2], F32); b_pv = P.buf("pv")
        siluT = SB("siluT", [128, KC, 2], BF16); b_silu = P.buf("silu")
        gw = SB("gw", [128, KC, 128], BF16); b_gw = P.buf("gw", P.new_dsem("gw"))
        small = SB("small", [128, 64], F32); b_small = P.buf("small", P.new_dsem("small"))
        binT = SB("binT", [128, 16], F32); b_binT = P.buf("binT", P.new_dsem("binT"))
        arena_t = SB("arena", [128, ARENA_BYTES // 2], BF16)
        AR = Arena(P, arena_t)
        psum = [st.enter_context(nc.psum_tensor("ps%d" % i, [128, 512], F32)) for i in range(8)]
        b_ps = [P.buf("ps%d" % i) for i in range(8)]
        ps_rr = [0]

        def PS():
            i = ps_rr[0] % 8
            ps_rr[0] += 1
            return psum[i], b_ps[i]

        ident32 = c32[:, C_ID:C_ID + 128]
        identb = cb[:, 0, :]
        maskFb = cb[:, 1, :]
        maskBb = cb[:, 2, :]

        ring_rr = [0]

        def ring_load(src_ap, ncols, nk=KC):
            i = ring_rr[0] % NSLOT
            ring_rr[0] += 1
            dst = ring[:, i, 0:nk, 0:ncols]
            P.dma("pool", lambda e: e.dma_start(out=dst, in_=src_ap), writes=[b_ring[i]], ds=b_ring[i].dsem)
            return ring[:, i], b_ring[i]

        def wpiece(w, l, c0, ncols, r0=0, nk=KC):
            return w[l, r0:r0 + nk * 128, c0:c0 + ncols].rearrange("(k p) n -> p k n", p=128)

        P.dma("sp", lambda e: e.dma_start(out=c32[:], in_=I["consts"][:, :]), writes=[b_c32], ds=b_c32.dsem)

        def setup_consts(e):
            e.tensor_copy(out=cb[:, 0, :], in_=c32[:, C_ID:C_ID + 128])
            e.tensor_copy(out=cb[:, 1, :], in_=c32[:, C_MF:C_MF + 128])
            e.tensor_copy(out=cb[:, 2, :], in_=c32[:, C_MB:C_MB + 128])
            e.memset(onesb[:], 1.0)
            e.memset(ones32[:], 1.0)
            e.memset(small[:], EPS)
            return e.memset(gw[:], 0.0)
        P.op("dve", setup_consts, reads=[b_c32], writes=[b_cb, b_ones, b_gw, b_small])

        for i in range(12):
            src = I["xp"][i * 128:(i + 1) * 128, :] if i < 4 else I["xs"][(i - 4) * 128:(i - 3) * 128, :]
            P.dma("sp", (lambda i, src: lambda e: e.dma_start(out=xa[:, i, :], in_=src))(i, src),
                  writes=[b_xa[i]], ds=b_xa[i].dsem)
            P.op("act", (lambda i: lambda e: e.activation(out=xa[:, i, :], in_=xa[:, i, :], func=AF.Copy, scale=ALPHA))(i),
                 reads=[b_xa[i]], writes=[b_xa[i]])

        AR.new_phase()
        cT, b_cT = AR.alloc("cT", [KC, 2], F32, P.new_dsem("cT"))
        sg, b_sg = AR.alloc("sg", [KC, 2], F32)
        with nc.allow_non_contiguous_dma(reason="tiny transposed vector load"):
            P.dma("sp", [(lambda v_: lambda e: e.dma_start(out=cT[:, :, v_], in_=I["cvec"][v_, :].rearrange("(k p) -> p k", p=128)))(v_) for v_ in range(2)],
                  writes=[b_cT], ds=b_cT.dsem)
        P.op("act", lambda e: e.activation(out=sg, in_=cT, func=AF.Sigmoid), reads=[b_cT], writes=[b_sg])
        P.op("dve", lambda e: e.tensor_tensor(out=siluT[:], in0=cT, in1=sg, op=ALU.mult), reads=[b_cT, b_sg], writes=[b_silu])

        with nc.allow_non_contiguous_dma(reason="per-partition bias layout"):
            P.dma("sp", [(lambda l_: lambda e: e.dma_start(out=badaT[:, l_, :], in_=I["b_ada"][l_, :].rearrange("(c p) -> p c", p=128)))(l_)
                         for l_ in range(nlayers)], writes=[b_badaT], ds=b_badaT.dsem)

        def ada_piece(l, j, rows2):
            slot, bsl = ring_load(wpiece(I["w_ada"], l, j * 512, 512), 512)
            ps, bp = PS()

            def ada_mm(e):
                for k in range(KC):
                    r = e.matmul(ps[0:2, :], lhsT=siluT[:, k, :], rhs=slot[:, k, :], start=(k == 0), stop=(k == KC - 1))
                return r
            P.op("pe", ada_mm, reads=[b_silu, bsl], writes=[bp])
            rw, brw = rows2[j % 2]
            P.op("act", lambda e: e.activation(out=rw[0:2, :], in_=ps[0:2, :], func=AF.Copy), reads=[bp], writes=[brw])
            ps2, bp2 = PS()

            def ada_tr(e):
                for qd in range(4):
                    r = e.matmul(ps2[:, qd * 2:qd * 2 + 2], lhsT=rw[0:2, qd * 128:(qd + 1) * 128],
                                 rhs=ident32[0:2, 0:2], start=True, stop=True)
                return r
            P.op("pe", ada_tr, reads=[brw, b_c32], writes=[bp2])
            k0 = (j % 2) * 4
            dstm = modT[:, l, j // 2, k0:k0 + 4, :]
            bsrc = apx(badaT[:, l, (j // 2) * 8 + k0:(j // 2) * 8 + k0 + 1], [[1, 4], [0, 2]])
            P.op("dve", lambda e: e.tensor_tensor(out=dstm, in0=ps2[:, 0:8].rearrange("p (a b) -> p a b", a=4), in1=bsrc, op=ALU.add),
                 reads=[bp2, b_badaT], writes=[b_modl[l]])

        def alloc_rows(tag):
            r_, b_r = AR.alloc("rows" + tag, [2, 512], F32)
            b_r1 = P.buf("rows1" + tag)
            AR.cur.append(b_r1)
            return [(r_[:, 0, :], b_r), (r_[:, 1, :], b_r1)]

        rows_up = alloc_rows("u")
        for j in range(12):
            ada_piece(0, j, rows_up)

        def chk(name):
            if stop == name:
                raise _Stop()

        dsems = {}

        def DS(name):
            if name not in dsems:
                dsems[name] = P.new_dsem(name)
            return dsems[name]

        def dram_bc(src1d, n):
            return AP(src1d.tensor, src1d.offset, [[0, 128], [1, n]])

        def transposes_to_hT(l, s_idx, b_idx, xchunks, v, j_off=0):
            for j0 in range(0, len(xchunks), 4):
                grp = xchunks[j0:j0 + 4]
                for k in range(KC):
                    ps, bp = PS()

                    def tr(e, ps=ps, grp=grp, k=k):
                        for jj, tc in enumerate(grp):
                            r = e.transpose(ps[:, jj * 128:(jj + 1) * 128], in_=xa[:, tc, k * 128:(k + 1) * 128], identity=ident32)
                        return r
                    P.op("pe", tr, reads=[b_xa[tc] for tc in grp] + [b_c32], writes=[bp])
                    n = len(grp) * 128
                    dst = hT[:, k, (j_off + j0) * 128:(j_off + j0) * 128 + n]
                    P.op("act", (lambda dst, ps, n, k: lambda e: e.activation(
                        out=dst, in_=ps[:, 0:n], func=AF.Identity,
                        scale=pv[:, s_idx, k, v:v + 1], bias=modT[:, l, b_idx, k, v:v + 1]))(dst, ps, n, k),
                        reads=[bp, b_pv, b_modl[l]], writes=b_hT[j_off + j0:j_off + j0 + len(grp)])

        def fm_proj(slot, bsl, c0, T, evac):
            for nt in range(T // 512):
                ps, bp = PS()

                def mm(e, ps=ps, nt=nt):
                    for k in range(KC):
                        r = e.matmul(ps[:, :], lhsT=slot[:, k, c0:c0 + 128], rhs=hT[:, k, nt * 512:(nt + 1) * 512],
                                     start=(k == 0), stop=(k == KC - 1))
                    return r
                P.op("pe", mm, reads=[bsl] + b_hT[nt * 4:(nt + 1) * 4], writes=[bp])
                evac(ps, bp, nt)

        def tm_proj(slot, bsl, c0, ncols, tcl, evac):
            ps, bp = PS()

            def mm(e, ps=ps):
                for k in range(KC):
                    r = e.matmul(ps[:, 0:ncols], lhsT=hT[:, k, tcl * 128:(tcl + 1) * 128], rhs=slot[:, k, c0:c0 + ncols],
                                 start=(k == 0), stop=(k == KC - 1))
                return r
            P.op("pe", mm, reads=[bsl, b_hT[tcl]], writes=[bp])
            evac(ps, bp)

        def load_bias_bc(name, l, c0, n):
            t, b = AR.alloc(name, [n], F32, DS(name))
            P.dma("sp", lambda e: e.dma_start(out=t, in_=dram_bc(I["b_in"][l, c0:c0 + n], n)), writes=[b], ds=b.dsem)
            return t, b

        def attn_job(heads, nq, tiles, reads, pt_tiles):
            nh = len(heads)
            W = nh * nq
            pts = []
            for ti, tl in enumerate(tiles):
                nk = tl["nk"]
                ps, bp = PS()

                def qk(e, ps=ps, tl=tl, nk=nk):
                    for hi, h in enumerate(heads):
                        r = e.matmul(ps[0:nk, hi * nq:(hi + 1) * nq], lhsT=tl["kT"][hi], rhs=h["q"], start=True, stop=True)
                    return r
                P.op("pe", qk, reads=reads, writes=[bp])
                pt, bpt = pt_tiles[ti]
                src = ps[0:nk, 0:W]
                if tl.get("bias") is not None:
                    P.op("dve", (lambda src, tl: lambda e: e.tensor_tensor(
                        out=src.rearrange("p (h q) -> p h q", h=nh), in0=src.rearrange("p (h q) -> p h q", h=nh),
                        in1=tl["bias"], op=ALU.add))(src, tl),
                        reads=[bp] + reads, writes=[bp])
                    P.op("act", (lambda pt, src, nk: lambda e: e.activation(out=pt[0:nk, 0:W], in_=src, func=AF.Exp, scale=0.125))(pt, src, nk),
                         reads=[bp], writes=[bpt])
                else:
                    P.op("act", (lambda pt, src, nk: lambda e: e.activation(out=pt[0:nk, 0:W], in_=src, func=AF.Exp, scale=0.125))(pt, src, nk),
                         reads=[bp], writes=[bpt])
                if tl.get("mask") is not None:
                    pv3 = pt[0:nk, 0:W].rearrange("p (h q) -> p h q", h=nh)
                    mk = tl["mask"]
                    mk3 = apx(mk, [[0, nh], [1, nq]])
                    P.op("dve", (lambda pv3, mk3: lambda e: e.tensor_tensor(out=pv3, in0=pv3, in1=mk3, op=ALU.mult))(pv3, mk3),
                         reads=[bpt, b_cb], writes=[bpt])
                pts.append((pt, bpt))
            yield
            pso, bpo = PS()
            psd, bpd = PS()

            def pvmm(e):
                for hi, h in enumerate(heads):
                    for ti, tl in enumerate(tiles):
                        r0, r1 = tl.get("vrows", (0, tl["nk"]))
                        v = tl["v"][hi][r0:r1, :]
                        m = v.shape[-1]
                        e.matmul(pso[0:m, hi * nq:(hi + 1) * nq], lhsT=v, rhs=pts[ti][0][r0:r1, hi * nq:(hi + 1) * nq],
                                 start=(ti == 0), stop=(ti == len(tiles) - 1))
                for ti, tl in enumerate(tiles):
                    r0, r1 = tl.get("vrows", (0, tl["nk"]))
                    r = e.matmul(psd[:, 0:W], lhsT=onesb[r0:r1, :], rhs=pts[ti][0][r0:r1, 0:W],
                                 start=(ti == 0), stop=(ti == len(tiles) - 1))
                return r
            P.op("pe", pvmm, reads=[b for _, b in pts] + reads + [b_ones], writes=[bpo, bpd])
            rd, brd = pt_tiles[-2]
            rdv = rd.bitcast(F32)[:, 0:W]
            if heads[0].get("sink") is not None:
                sk = heads[0]["sink"]
                for hi in range(nh):
                    P.op("act", (lambda hi: lambda e: e.activation(out=rdv[:, hi * nq:(hi + 1) * nq], in_=psd[:, hi * nq:(hi + 1) * nq],
                                                                   func=AF.Ln, bias=sk[:, hi:hi + 1]))(hi),
                         reads=[bpd, b_small], writes=[brd])
            else:
                P.op("act", lambda e: e.activation(out=rdv, in_=psd[:, 0:W], func=AF.Ln), reads=[bpd], writes=[brd])
            P.op("act", lambda e: e.activation(out=rdv, in_=rdv, func=AF.Exp, scale=-1.0), reads=[brd], writes=[brd])
            if dbg and "aj_pt" in dbg and heads[0].get("dbgflag"):
                P.dma("pool", lambda e: e.dma_start(out=O["aj_pt"][:, :], in_=pts[0][0][:, 0:384]), reads=[pts[0][1]], ds=DS("dbgaj1"), store=True)
                P.dma("pool", lambda e: e.dma_start(out=O["aj_pt4"][:, :], in_=pts[4][0][:, 0:384]), reads=[pts[4][1]], ds=DS("dbgaj3"), store=True)
                P.dma("sp", lambda e: e.dma_start(out=O["aj_rd"][:, :], in_=rdv), reads=[brd], ds=DS("dbgaj2"), store=True)
                osb, bosb = pt_tiles[-1]
                osbv = osb.bitcast(F32)[:, 0:W]
                P.op("act", lambda e: e.activation(out=osbv, in_=pso[:, 0:W], func=AF.Copy), reads=[bpo], writes=[bosb])
                P.dma("sp", lambda e: e.dma_start(out=O["aj_o"][:, :], in_=osbv), reads=[bosb], ds=DS("dbgaj4"), store=True)
            for hf in range(2):
                his = [hi for hi, h in enumerate(heads) if h["half"] == hf]
                if not his:
                    continue
                assert all(b - a == 2 for a, b in zip(his, his[1:]))
                p0 = hf * 64
                n_ = len(his)
                d0 = heads[his[0]]["dst"]
                o_ = apx(d0, [[1024, n_], [1, nq]])
                i0_ = apx(pso[p0:p0 + 64, his[0] * nq:his[0] * nq + 1], [[2 * nq, n_], [1, nq]])
                i1_ = apx(rdv[p0:p0 + 64, his[0] * nq:his[0] * nq + 1], [[2 * nq, n_], [1, nq]])
                P.op("dve", (lambda o_, i0_, i1_: lambda e: e.tensor_tensor(out=o_, in0=i0_, in1=i1_, op=ALU.mult))(o_, i0_, i1_),
                     reads=[bpo, brd], writes=[heads[his[0]]["dstb"]])

        def run_pipelined(jobs):
            prev = None
            for j in jobs:
                next(j)
                if prev is not None:
                    for _ in prev:
                        pass
                prev = j
            if prev is not None:
                for _ in prev:
                    pass

        def alloc_pt(n, tag):
            out = []
            for i in range(n):
                out.append(AR.alloc("pt%s%d" % (tag, i), [384], BF16))
            out.append(AR.alloc("ptrd" + tag, [768], BF16))
            out.append(AR.alloc("pttmp" + tag, [768], BF16))
            return out

        def mlstm_phase(l, grp):
            isP = grp == "P"
            T = 512 if isP else 1024
            nch = T // 128
            seqs = [(0, 256), (256, 256)] if isP else [(0, 1024)]
            xbase = 0 if isP else 4
            AR.new_phase()
            qT, b_qT = AR.alloc("m_qT", [2, T], BF16)
            kT, b_kT = AR.alloc("m_kT", [4, T], BF16)
            P.op("dve", lambda e: e.memset(kT[:, :, :], 0.0), writes=[b_kT])
            ktok, b_ktok = AR.alloc("m_ktok", [nch, 256], BF16)
            vtok, b_vtok = AR.alloc("m_vtok", [nch, 256], BF16)
            ogtok, b_og = AR.alloc("m_og", [nch, 256], BF16)
            Hf, b_Hf = AR.alloc("m_Hf", [nch, 256], BF16)
            g = []
            for i in range(5):
                g.append(AR.alloc("m_g%d" % i, [T], F32))
            tks, b_tks = AR.alloc("m_tks", [nch, 32], F32)
            bbc, b_bbc = load_bias_bc("m_bbc", l, 256, 768)
            ngbc, b_ngbc = AR.alloc("m_ngbc", [256], F32, DS("m_ngbc"))
            P.dma("sp", lambda e: e.dma_start(out=ngbc, in_=dram_bc(I["norm_g"][l, :], 256)), writes=[b_ngbc], ds=b_ngbc.dsem)
            gb, b_gb = AR.alloc("m_gb", [4], F32, DS("m_gb"))
            DEC, b_DEC = AR.alloc("m_DEC", [8], F32)
            dgt, b_dgt = AR.alloc("m_dgt", [8, 4], F32)
            decsb, b_decsb = AR.alloc("m_decsb", [2, 8, 4], F32)
            nchains = 4 if isP else 2
            CT = [AR.alloc("m_CT%d" % d, [2, 65], F32, DS("m_CT%d" % d)) for d in range(nchains)]
            CTb = [AR.alloc("m_CTb%d" % d, [4, 65], BF16) for d in range(nchains)]
            Hb, b_Hb = AR.alloc("m_Hb", [nch, 256], BF16)
            W = []
            for i in range(4):
                W.append(dict(
                    Pt=AR.alloc("m_Pt%d" % i, [4, 128], BF16), Vp=AR.alloc("m_Vp%d" % i, [4, 65], BF16),
                    A=AR.alloc("m_A%d" % i, [4, 65], F32), Bq=AR.alloc("m_Bq%d" % i, [4, 65], F32),
                    sm=AR.alloc("m_sm%d" % i, [32], F32)))
            FS = []
            for i in range(4 if isP else 2):
                FS.append(dict(sm=AR.alloc("m_fsm%d" % i, [32], F32), Hs=AR.alloc("m_Hs%d" % i, [4, 64], F32),
                               sq=AR.alloc("m_sq%d" % i, [4, 64], F32), yml=AR.alloc("m_yml%d" % i, [256], BF16)))
            ogts = [AR.alloc("m_ogt%d" % i, [256], F32) for i in range(2)]
            if isP:
                Cst = [AR.alloc("m_Cst%d" % i, [2, 4, 64], F32, DS("m_Cst%d" % i)) for i in range(2)]
                mfin, b_mfin = AR.alloc("m_mfin", [2], F32, DS("m_mfin"))
            else:
                C0sb, b_C0 = AR.alloc("m_C0", [2, 4, 64], F32, DS("m_C0"))

            b_ginit = P.buf("m_g4dec_init", init=dict(g[4][1].rs))
            AR.cur.append(b_ginit)
            P.op("dve", lambda e: e.memset(g[4][0][0:36, :], 0.0), writes=[b_ginit])
            P.op("dve", lambda e: e.memset(DEC[0:36, :], 0.0), writes=[b_ginit])
            P.op("dve", lambda e: e.memset(gb[:, :], 0.0), writes=[b_gb])
            with nc.allow_non_contiguous_dma(reason="tiny vector loads"):
                fl = []
                for (r0, col, src) in ((0, 0, I["b_in"][l, 1024:1028]), (32, 0, I["b_in"][l, 1032:1036]),
                                       (0, 1, I["b_in"][l, 1028:1032]), (32, 1, I["b_in"][l, 1036:1040]),
                                       (0, 2, I["fbias"][l, 0, :]), (32, 2, I["fbias"][l, 1, :])) + \
                        (() if isP else ((0, 3, I["stm"][l, 0, :]), (32, 3, I["stm"][l, 1, :]))):
                    srcp = AP(src.tensor, src.offset, [[1, 4], [1, 1]])
                    fl.append((lambda r0, col, srcp: lambda e: e.dma_start(out=gb[r0:r0 + 4, col:col + 1], in_=srcp))(r0, col, srcp))
                P.dma("sp", fl, writes=[b_gb], ds=b_gb.dsem)
            P.op("dve", lambda e: e.tensor_tensor(out=gb[0:36, 1:2], in0=gb[0:36, 1:2], in1=gb[0:36, 2:3], op=ALU.add),
                 reads=[b_gb], writes=[b_gb])

            chk('ml_a' + grp)
            slot, bsl = ring_load(wpiece(I["w_in"], l, 0, 512), 512)
            for c in range(2):
                fm_proj(slot, bsl, c * 128, T, lambda ps, bp, nt, c=c: P.op(
                    "act", lambda e: e.activation(out=qT[:, c, nt * 512:(nt + 1) * 512], in_=ps[:, :], func=AF.Identity,
                                                  bias=binT[:, c:c + 1]), reads=[bp, b_binT], writes=[b_qT]))
                def evk_(ps, bp, nt, c=c):
                    for hh in range(2):
                        pr = slice(hh * 64, hh * 64 + 64)
                        P.op("act", (lambda pr, hh: lambda e: e.activation(
                            out=kT[pr, 2 * c + hh, nt * 512:(nt + 1) * 512], in_=ps[pr, :], func=AF.Identity,
                            bias=binT[pr, 2 + c:3 + c], scale=0.125))(pr, hh), reads=[bp, b_binT], writes=[b_kT])
                fm_proj(slot, bsl, 256 + c * 128, T, evk_)
            for tcl in range(nch):
                tm_proj(slot, bsl, 256, 256, tcl, lambda ps, bp, tcl=tcl: P.op(
                    "dve", lambda e: e.scalar_tensor_tensor(out=ktok[:, tcl, :], in0=ps[:, 0:256], scalar=1.0, in1=bbc[:, 0:256],
                                                            op0=ALU.mult, op1=ALU.add), reads=[bp, b_bbc], writes=[b_ktok]))
            P.op("act", lambda e: e.activation(out=ktok[:, :, :], in_=ktok[:, :, :], func=AF.Copy, scale=0.125),
                 reads=[b_ktok], writes=[b_ktok])
            slot, bsl = ring_load(wpiece(I["w_in"], l, 512, 512), 512)
            for tcl in range(nch):
                def ev(ps, bp, tcl=tcl):
                    ogt, b_ogt = ogts[tcl % 2]
                    P.op("dve", lambda e: e.tensor_tensor(out=vtok[:, tcl, :], in0=ps[:, 0:256], in1=bbc[:, 256:512], op=ALU.add),
                         reads=[bp, b_bbc], writes=[b_vtok])
                    P.op("dve", lambda e: e.tensor_tensor(out=ogt, in0=ps[:, 256:512], in1=bbc[:, 512:768], op=ALU.add),
                         reads=[bp, b_bbc], writes=[b_ogt])
                    P.op("act", lambda e: e.activation(out=ogtok[:, tcl, :], in_=ogt, func=AF.Sigmoid), reads=[b_ogt], writes=[b_og])
                tm_proj(slot, bsl, 0, 512, tcl, ev)
            chk('ml_b' + grp)
            (g0, bg0), (g1, bg1), (g2, bg2), (g3, bg3), (g4, bg4) = g
            for (c0, dst, bdst, bcol) in ((0, g3, bg3, 0), (64, g0, bg0, 1)):
                for nt in range(T // 512):
                    ps, bp = PS()

                    def mm(e, ps=ps, nt=nt, c0=c0):
                        for k in range(KC):
                            r = e.matmul(ps[0:64, :], lhsT=gw[:, k, c0:c0 + 64], rhs=hT[:, k, nt * 512:(nt + 1) * 512],
                                         start=(k == 0), stop=(k == KC - 1))
                        return r
                    P.op("pe", mm, reads=[b_gw] + b_hT[nt * 4:(nt + 1) * 4], writes=[bp])
                    P.op("act", (lambda ps, nt, dst, bcol: lambda e: e.activation(
                        out=dst[0:36, nt * 512:(nt + 1) * 512], in_=ps[0:36, :], func=AF.Identity, bias=gb[0:36, bcol:bcol + 1]))(ps, nt, dst, bcol),
                        reads=[bp, b_gb], writes=[bdst])
            chk('ml_c' + grp)
            R36 = slice(0, 36)
            P.op("act", lambda e: e.activation(out=g1[R36, :], in_=g0[R36, :], func=AF.Abs), reads=[bg0], writes=[bg1])
            P.op("act", lambda e: e.activation(out=g1[R36, :], in_=g1[R36, :], func=AF.Exp, scale=-1.0), reads=[bg1], writes=[bg1])
            P.op("act", lambda e: e.activation(out=g1[R36, :], in_=g1[R36, :], func=AF.Ln, bias=1.0), reads=[bg1], writes=[bg1])
            P.op("dve", lambda e: e.tensor_single_scalar(out=g0[R36, :], in_=g0[R36, :], scalar=0.0, op=ALU.min), reads=[bg0], writes=[bg0])
            P.op("dve", lambda e: e.tensor_tensor(out=g0[R36, :], in0=g0[R36, :], in1=g1[R36, :], op=ALU.subtract),
                 reads=[bg0, bg1], writes=[bg0])

            def rows(d):
                return slice(0, 4) if d == 0 else slice(32, 36)

            def dirview(a, d, off, ln):
                r = a[rows(d), off:off + ln]
                if d == 0:
                    return r
                return AP(r.tensor, r.offset + ln - 1, [list(r.ap[0]), [-1, ln]])

            def scan(dst, bd, src, bs, op1, init):
                for (off, ln) in seqs:
                    for d in range(2):
                        ini = init if not isinstance(init, tuple) else init[0][rows(d), init[1]:init[1] + 1]
                        P.op("dve", (lambda d, off, ln, ini: lambda e: e.tensor_tensor_scan(
                            out=dirview(dst, d, off, ln), data0=apx(ones32[rows(d), 0:1], [[0, ln]]),
                            data1=dirview(src, d, off, ln), initial=ini, op0=ALU.mult, op1=op1))(d, off, ln, ini),
                            reads=[bs, b_gb], writes=[bd])
            scan(g1, bg1, g0, bg0, ALU.add, 0.0)
            P.op("dve", lambda e: e.tensor_tensor(out=g2[R36, :], in0=g3[R36, :], in1=g1[R36, :], op=ALU.subtract),
                 reads=[bg3, bg1], writes=[bg2])
            scan(g0, bg0, g2, bg2, ALU.max, 0.0 if isP else (gb, 3))

            def Rv(d, off, j0, n, bc):
                col = off + (127 if d == 0 else 0) + 128 * j0
                base = g0[rows(d), col:col + 1]
                dims = [[128, n]] + ([[0, bc]] if bc else [])
                return apx(base, dims)

            def cv(a, d, off, j0, n):
                r = a[rows(d), off + 128 * j0:off + 128 * (j0 + n)]
                return r.rearrange("p (c t) -> p c t", t=128)
            chk('ml_d' + grp)
            m0s = 0.0
            for (off, ln) in seqs:
                nC = ln // 128
                gc0 = off // 128
                for d in range(2):
                    P.op("dve", (lambda d, off, nC: lambda e: e.tensor_tensor(out=cv(g3, d, off, 0, nC), in0=cv(g2, d, off, 0, nC),
                                                                               in1=Rv(d, off, 0, nC, 128), op=ALU.subtract))(d, off, nC),
                         reads=[bg2, bg0], writes=[bg3])
                    P.op("dve", (lambda d, off, nC: lambda e: e.tensor_tensor(out=cv(g2, d, off, 0, nC), in0=Rv(d, off, 0, nC, 128),
                                                                               in1=cv(g0, d, off, 0, nC), op=ALU.subtract))(d, off, nC),
                         reads=[bg0, bg3], writes=[bg2])
                    if d == 0:
                        jo, jp, jfirst = 1, 0, 0
                    else:
                        jo, jp, jfirst = 0, 1, nC - 1
                    if nC > 1:
                        P.op("dve", (lambda d, off, nC, jo, jp: lambda e: e.tensor_tensor(
                            out=cv(g4, d, off, jo, nC - 1), in0=Rv(d, off, jp, nC - 1, 128), in1=cv(g0, d, off, jo, nC - 1), op=ALU.subtract))(d, off, nC, jo, jp),
                            reads=[b_ginit, bg0], writes=[bg4])
                        P.op("dve", (lambda d, off, nC, jo, jp, gc0: lambda e: e.tensor_tensor(
                            out=DEC[rows(d), gc0 + jo:gc0 + jo + nC - 1], in0=Rv(d, off, jp, nC - 1, 0), in1=Rv(d, off, jo, nC - 1, 0), op=ALU.subtract))(d, off, nC, jo, jp, gc0),
                            reads=[b_ginit, bg0], writes=[b_DEC])
                    m0 = 0.0 if isP else gb[rows(d), 3:4]
                    P.op("dve", (lambda d, off, jfirst, m0: lambda e: e.tensor_scalar(
                        out=g4[rows(d), off + 128 * jfirst:off + 128 * (jfirst + 1)], in0=g0[rows(d), off + 128 * jfirst:off + 128 * (jfirst + 1)],
                        scalar1=m0, scalar2=-1.0, op0=ALU.subtract, op1=ALU.mult))(d, off, jfirst, m0),
                        reads=[b_ginit, bg0, b_gb], writes=[bg4])
                    P.op("dve", (lambda d, off, jfirst, m0, gc0: lambda e: e.tensor_scalar(
                        out=DEC[rows(d), gc0 + jfirst:gc0 + jfirst + 1], in0=Rv(d, off, jfirst, 1, 0),
                        scalar1=m0, scalar2=-1.0, op0=ALU.subtract, op1=ALU.mult))(d, off, jfirst, m0, gc0),
                        reads=[b_ginit, bg0, b_gb], writes=[b_DEC])
            if isP:
                for si, (off, ln) in enumerate(seqs):
                    for d in range(2):
                        col = off + (ln - 1 if d == 0 else 0)
                        P.op("dve", (lambda d, col, si: lambda e: e.tensor_tensor(out=mfin[rows(d), si:si + 1], in0=g1[rows(d), col:col + 1],
                                                                                    in1=g0[rows(d), col:col + 1], op=ALU.add))(d, col, si),
                             reads=[bg1, bg0], writes=[b_mfin])
                with nc.allow_non_contiguous_dma(reason="tiny state output"):
                    fl = []
                    for si in range(2):
                        for d in range(2):
                            dstp = O["o_m"][si, l, d, :]
                            dstp = AP(dstp.tensor, dstp.offset, [[1, 4], [1, 1]])
                            fl.append((lambda si, d, dstp: lambda e: e.dma_start(out=dstp, in_=mfin[rows(d), si:si + 1]))(si, d, dstp))
                    P.dma("sp", fl, reads=[b_mfin], ds=b_mfin.dsem, store=True)
            P.op("dve", lambda e: e.tensor_tensor(out=g1[R36, :], in0=g1[R36, :], in1=g0[R36, :], op=ALU.add),
                 reads=[bg1, bg0], writes=[bg1])
            P.op("act", lambda e: e.activation(out=g3[R36, :], in_=g3[R36, :], func=AF.Exp), reads=[bg3], writes=[bg3])
            P.op("act", lambda e: e.activation(out=g2[R36, :], in_=g2[R36, :], func=AF.Exp), reads=[bg2], writes=[bg2])
            P.op("act", lambda e: e.activation(out=g4[R36, :], in_=g4[R36, :], func=AF.Exp), reads=[bg4], writes=[bg4])
            P.op("act", lambda e: e.activation(out=g1[R36, :], in_=g1[R36, :], func=AF.Exp, scale=-1.0), reads=[bg1], writes=[bg1])
            P.op("act", lambda e: e.activation(out=DEC[R36, 0:nch], in_=DEC[R36, 0:nch], func=AF.Exp), reads=[b_DEC], writes=[b_DEC])
            chk('ml_e' + grp)
            for tcl in range(nch):
                for d in range(2):
                    ps, bp = PS()

                    def trs(e, ps=ps, tcl=tcl, d=d):
                        r0 = 0 if d == 0 else 32
                        for qi, gt in enumerate((g3, g2, g4, g1)):
                            r = e.matmul(ps[:, qi * 4:qi * 4 + 4], lhsT=gt[r0:r0 + 4, tcl * 128:(tcl + 1) * 128],
                                         rhs=ident32[r0:r0 + 4, r0:r0 + 4], start=True, stop=True)
                        return r
                    P.op("pe", trs, reads=[bg1, bg2, bg3, bg4, b_c32], writes=[bp])
                    P.op("act", (lambda ps, tcl, d: lambda e: e.activation(
                        out=apx(tks[:, tcl, d * 4:d * 4 + 1], [[8, 4], [1, 4]]), in_=ps[:, 0:16].rearrange("p (q h) -> p q h", q=4), func=AF.Copy))(ps, tcl, d),
                        reads=[bp], writes=[b_tks])
            for d in range(2):
                r0 = 0 if d == 0 else 32
                P.op("dve", (lambda d, r0: lambda e: e.tensor_tensor(
                    out=dgt[rows(d), 0:nch, :], in0=apx(ident32[r0:r0 + 4, r0:r0 + 1], [[0, nch], [1, 4]]),
                    in1=apx(DEC[rows(d), 0:1], [[1, nch], [0, 4]]), op=ALU.mult))(d, r0), reads=[b_DEC, b_c32], writes=[b_dgt])

            for d in range(2):
                psd_, bpd_ = PS()
                r0 = 0 if d == 0 else 32
                P.op("pe", (lambda psd_, r0: lambda e: e.matmul(psd_[:, 0:nch * 4], lhsT=ones32[r0:r0 + 4, :],
                                                                rhs=dgt[r0:r0 + 4, 0:nch, :].rearrange("p c h -> p (c h)"), start=True, stop=True))(psd_, r0),
                     reads=[b_dgt], writes=[bpd_])
                P.op("act", (lambda d, psd_: lambda e: e.activation(out=decsb[:, d, 0:nch, :].rearrange("p c h -> p (c h)"),
                                                                    in_=psd_[:, 0:nch * 4], func=AF.Copy))(d, psd_),
                     reads=[bpd_], writes=[b_decsb])

            chk('ml_f' + grp)
            hh_c = [(h % 2, h // 2) for h in range(4)]

            def tk(tcl, q, d):
                c = (q * 2 + d) * 4
                return tks[:, tcl, c:c + 4]

            def chain(si, off, ln, d, ci, wsets):
                nC = ln // 128
                order = list(range(nC)) if d == 0 else list(range(nC - 1, -1, -1))
                (ct, b_ct), (ctb, b_ctb) = CT[ci], CTb[ci]
                def ct_to_bf():
                    for hh in range(2):
                        pr = slice(hh * 64, hh * 64 + 64)
                        P.op("act", (lambda pr, hh: lambda e: e.activation(
                            out=apx(ctb[pr, hh, 0:1], [[130, 2], [1, 65]]), in_=ct[pr, :, :], func=AF.Copy))(pr, hh),
                            reads=[b_ct], writes=[b_ctb])
                P.op("dve", lambda e: e.memset(ctb[:, :, :], 0.0), writes=[b_ctb])
                if isP:
                    P.op("dve", lambda e: e.memset(ct[:, :, :], 0.0), writes=[b_ct])
                else:
                    for c in range(2):
                        ps, bp = PS()
                        P.op("pe", (lambda ps, c: lambda e: e.transpose(
                            ps[:, 0:64], in_=C0sb[0:64, d, 2 * c:2 * c + 2, :].rearrange("p h e -> p (h e)"), identity=ident32[0:64, 0:64]))(ps, c),
                            reads=[b_C0, b_c32], writes=[bp])
                        P.op("act", (lambda ps, c: lambda e: e.activation(out=ct[:, c, 0:64], in_=ps[:, 0:64], func=AF.Copy))(ps, c),
                             reads=[bp], writes=[b_ct])
                    with nc.allow_non_contiguous_dma(reason="tiny state load"):
                        srcn = I["stn"][l, d, :, :].rearrange("(c hh) e -> (hh e) c", hh=2)
                        P.dma("sp", lambda e: e.dma_start(out=ct[:, :, 64:65].rearrange("p c o -> p (c o)"), in_=srcn),
                              writes=[b_ct], ds=b_ct.dsem)
                    ct_to_bf()
                chk('ci_a')
                yield
                for step, j in enumerate(order):
                    tcl = off // 128 + j
                    sl = slice(tcl * 128, (tcl + 1) * 128)
                    w = wsets[step % len(wsets)]
                    (Pt, bPt), (Vp, bVp), (A, bA), (Bq, bBq), (sm, bsm) = (w["Pt"], w["Vp"], w["A"], w["Bq"], w["sm"])
                    pss, bps_ = PS()

                    def smm(e, pss=pss, sl=sl):
                        for h, (hh, c) in enumerate(hh_c):
                            r = e.matmul(pss[:, h * 128:(h + 1) * 128], lhsT=kT[:, h, sl], rhs=qT[:, c, sl], start=True, stop=True)
                        return r
                    P.op("pe", smm, reads=[b_kT, b_qT], writes=[bps_])
                    yield
                    chk('ci_b')
                    mk = maskFb if d == 0 else maskBb
                    P.op("dve", (lambda Pt, pss, mk: lambda e: e.tensor_tensor(
                        out=Pt, in0=pss[:, :].rearrange("p (h t) -> p h t", h=4), in1=apx(mk, [[0, 4], [1, 128]]), op=ALU.mult))(Pt, pss, mk),
                        reads=[bps_, b_cb], writes=[bPt])
                    chk('ci_c')
                    P.op("dve", (lambda Vp, tcl: lambda e: e.tensor_tensor(
                        out=Vp[:, :, 0:64], in0=vtok[:, tcl, :].rearrange("p (h x) -> p h x", h=4),
                        in1=apx(tk(tcl, 0, d), [[1, 4], [0, 64]]), op=ALU.mult))(Vp, tcl), reads=[b_vtok, b_tks], writes=[bVp])
                    chk('ci_d')
                    P.op("act", (lambda Vp, tcl: lambda e: e.activation(out=Vp[:, :, 64:65].rearrange("p h o -> p (h o)"),
                                                                        in_=tk(tcl, 0, d), func=AF.Copy))(Vp, tcl),
                         reads=[b_tks], writes=[bVp])
                    chk('ch_a')
                    yield
                    psi, bpi = PS()
                    psj, bpj = PS()

                    def imm(e, psi=psi, psj=psj, Pt=Pt, Vp=Vp, sl=sl):
                        for h, (hh, c) in enumerate(hh_c):
                            e.matmul(psi[:, h * 65:(h + 1) * 65], lhsT=Pt[:, h, :], rhs=Vp[:, h, :], start=True, stop=True)
                        for h, (hh, c) in enumerate(hh_c):
                            r = e.matmul(psj[:, h * 65:(h + 1) * 65], lhsT=qT[:, c, sl], rhs=ctb[:, h, :], start=True, stop=True)
                        return r
                    P.op("pe", imm, reads=[bPt, bVp, b_qT, b_ctb], writes=[bpi, bpj])
                    yield
                    chk('ch_b')
                    v3 = lambda ps_: ps_[:, 0:260].rearrange("p (h x) -> p h x", h=4)
                    P.op("dve", (lambda A, psi, tcl: lambda e: e.tensor_tensor(out=A, in0=v3(psi), in1=apx(tk(tcl, 1, d), [[1, 4], [0, 65]]), op=ALU.mult))(A, psi, tcl),
                         reads=[bpi, b_tks], writes=[bA])
                    P.op("dve", (lambda Bq, psj, tcl: lambda e: e.tensor_tensor(out=Bq, in0=v3(psj), in1=apx(tk(tcl, 2, d), [[1, 4], [0, 65]]), op=ALU.mult))(Bq, psj, tcl),
                         reads=[bpj, b_tks], writes=[bBq])
                    P.op("dve", (lambda A, Bq: lambda e: e.tensor_tensor(out=A, in0=A, in1=Bq, op=ALU.add))(A, Bq), reads=[bA, bBq], writes=[bA])
                    yield
                    den = sm[:, 0:4]
                    P.op("act", (lambda A, den: lambda e: e.activation(out=den, in_=A[:, :, 64:65].rearrange("p h o -> p (h o)"), func=AF.Abs))(A, den),
                         reads=[bA], writes=[bsm])
                    P.op("dve", (lambda den, tcl: lambda e: e.tensor_tensor(out=den, in0=den, in1=tk(tcl, 3, d), op=ALU.max))(den, tcl),
                         reads=[bsm, b_tks], writes=[bsm])
                    P.op("dve", (lambda den: lambda e: e.reciprocal(out=den, in_=den))(den), reads=[bsm], writes=[bsm])
                    yield
                    chk('ch_c')
                    hdst, b_hdst = (Hf, b_Hf) if d == 0 else (Hb, b_Hb)
                    hfv = hdst[:, tcl, :].rearrange("p (h x) -> p h x", h=4)
                    rd3 = apx(den, [[1, 4], [0, 64]])
                    P.op("dve", (lambda A, hfv, rd3: lambda e: e.tensor_tensor(out=hfv, in0=A[:, :, 0:64], in1=rd3, op=ALU.mult))(A, hfv, rd3),
                         reads=[bA, bsm], writes=[b_hdst])
                    chk('ch_d')
                    last = step == len(order) - 1
                    if last and not isP:
                        yield
                        continue
                    psu, bpu = PS()

                    def umm(e, psu=psu, Vp=Vp, tcl=tcl):
                        for h in range(4):
                            if h % 2 == 0:
                                r = e.matmul(psu[0:64, h * 65:(h + 1) * 65], lhsT=ktok[:, tcl, h * 64:(h + 1) * 64], rhs=Vp[:, h, :], start=True, stop=True)
                            else:
                                r = e.matmul(psu[:, h * 65:(h + 1) * 65], lhsT=ktok[:, tcl, (h - 1) * 64:(h + 1) * 64], rhs=Vp[:, h, :], start=True, stop=True)
                        return r
                    P.op("pe", umm, reads=[b_ktok, bVp], writes=[bpu])
                    yield
                    for hh in range(2):
                        pr = slice(hh * 64, (hh + 1) * 64)
                        dcv = apx(decsb[pr, d, tcl, hh:hh + 1], [[2, 2], [0, 65]])
                        P.op("dve", (lambda pr, dcv: lambda e: e.tensor_tensor(out=ct[pr, :, :], in0=ct[pr, :, :], in1=dcv, op=ALU.mult))(pr, dcv),
                             reads=[b_ct, b_decsb], writes=[b_ct])
                        psv = apx(psu[pr, hh * 65:hh * 65 + 1], [[130, 2], [1, 65]])
                        P.op("dve", (lambda pr, psv: lambda e: e.tensor_tensor(out=ct[pr, :, :], in0=psv, in1=ct[pr, :, :], op=ALU.add))(pr, psv),
                             reads=[bpu, b_ct], writes=[b_ct])
                    ct_to_bf()
                    yield
                chk('ch_e')
                if isP:
                    cst, b_cst = Cst[si]
                    for c in range(2):
                        ps, bp = PS()
                        P.op("pe", (lambda ps, c: lambda e: e.transpose(ps[0:64, 0:128], in_=ct[:, c, 0:64], identity=ident32))(ps, c),
                             reads=[b_ct, b_c32], writes=[bp])
                        P.op("act", (lambda ps, c: lambda e: e.activation(out=cst[0:64, d, 2 * c:2 * c + 2, :].rearrange("p h e -> p (h e)"),
                                                                          in_=ps[0:64, 0:128], func=AF.Copy))(ps, c),
                             reads=[bp], writes=[b_cst])
                    with nc.allow_non_contiguous_dma(reason="state outputs"):
                        P.dma("sp", lambda e: e.dma_start(out=O["o_C"][si, l, d].rearrange("h d e -> d h e"), in_=cst[0:64, d, :, :]),
                              reads=[b_cst], ds=b_cst.dsem, store=True)
                        dstn = O["o_n"][si, l, d, :, :].rearrange("(c hh) e -> (hh e) c", hh=2)
                        P.dma("sp", lambda e: e.dma_start(out=dstn, in_=ct[:, :, 64:65].rearrange("p c o -> p (c o)")),
                              reads=[b_ct], ds=b_ct.dsem, store=True)

            if not isP:
                P.dma("sp", lambda e: e.dma_start(out=C0sb[0:64, :, :, :], in_=I["stC"][l].rearrange("r h d e -> d r h e")),
                      writes=[b_C0], ds=b_C0.dsem)
            def finalize(tcl, fs):
                (sm, bsm), (Hs, bHs), (sq, bsq), (yml, byml) = fs["sm"], fs["Hs"], fs["sq"], fs["yml"]
                sl = slice(tcl * 128, (tcl + 1) * 128)
                Hs2 = Hs.rearrange("p h x -> p (h x)")
                P.op("dve", lambda e: e.tensor_tensor(out=Hs2, in0=Hf[:, tcl, :], in1=Hb[:, tcl, :], op=ALU.add), reads=[b_Hf, b_Hb], writes=[bHs])
                yield
                s1, s2, mu, va = sm[:, 4:8], sm[:, 8:12], sm[:, 12:16], sm[:, 16:20]
                P.op("dve", lambda e: e.tensor_reduce(out=s1, in_=Hs, axis=AX.X, op=ALU.add), reads=[bHs], writes=[bsm])
                yield
                P.op("act", lambda e: e.activation(out=sq, in_=Hs, func=AF.Square), reads=[bHs], writes=[bsq])
                yield
                P.op("dve", lambda e: e.tensor_reduce(out=s2, in_=sq, axis=AX.X, op=ALU.add), reads=[bsq], writes=[bsm])
                yield
                P.op("dve", lambda e: e.tensor_scalar(out=mu, in0=s1, scalar1=1.0 / 64, scalar2=None, op0=ALU.mult), reads=[bsm], writes=[bsm])
                yield
                P.op("dve", lambda e: e.tensor_tensor(out=va, in0=mu, in1=mu, op=ALU.mult), reads=[bsm], writes=[bsm])
                yield
                P.op("dve", lambda e: e.scalar_tensor_tensor(out=va, in0=s2, scalar=1.0 / 64, in1=va, op0=ALU.mult, op1=ALU.subtract), reads=[bsm], writes=[bsm])
                yield
                P.op("act", lambda e: e.activation(out=va, in_=va, func=AF.Sqrt, bias=small[:, 0:1]), reads=[bsm, b_small], writes=[bsm])
                yield
                P.op("dve", lambda e: e.reciprocal(out=va, in_=va), reads=[bsm], writes=[bsm])
                yield
                P.op("dve", lambda e: e.tensor_tensor(out=Hs, in0=Hs, in1=apx(mu, [[1, 4], [0, 64]]), op=ALU.subtract), reads=[bHs, bsm], writes=[bHs])
                yield
                P.op("dve", lambda e: e.tensor_tensor(out=Hs, in0=Hs, in1=apx(va, [[1, 4], [0, 64]]), op=ALU.mult), reads=[bHs, bsm], writes=[bHs])
                yield
                P.op("dve", lambda e: e.tensor_tensor(out=Hs2, in0=Hs2, in1=ngbc, op=ALU.mult), reads=[bHs, b_ngbc], writes=[bHs])
                yield
                P.op("dve", lambda e: e.tensor_tensor(out=yml, in0=Hs2, in1=ogtok[:, tcl, :], op=ALU.mult), reads=[bHs, b_og], writes=[byml])
                yield
                pst, bpt = PS()
                pstb = pst[:, :].bitcast(BF16)

                def ytr(e):
                    e.transpose(pstb[:, 0:128], in_=yml[:, 0:128], identity=identb)
                    return e.transpose(pstb[:, 128:256], in_=yml[:, 128:256], identity=identb)
                P.op("pe", ytr, reads=[byml, b_cb], writes=[bpt])
                yield
                P.op("act", lambda e: e.activation(out=mixT[:, 0:2, sl], in_=pstb[:, 0:256].rearrange("p (c t) -> p c t", c=2), func=AF.Copy),
                     reads=[bpt], writes=[b_mix[tcl]])
                yield

            gens = []
            ci = 0
            for si, (off, ln) in enumerate(seqs):
                for d in range(2):
                    wsets = [W[ci]] if isP else W[2 * ci:2 * ci + 2]
                    gens.append(chain(si, off, ln, d, ci, wsets))
                    ci += 1
            while gens:
                for g_ in list(gens):
                    try:
                        next(g_)
                    except StopIteration:
                        gens.remove(g_)
            def run_rr(gl):
                gl = list(gl)
                while gl:
                    for g_ in list(gl):
                        try:
                            next(g_)
                        except StopIteration:
                            gl.remove(g_)
            nf = len(FS)
            for t0_ in range(0, nch, nf):
                run_rr([finalize(tcl, FS[tcl % nf]) for tcl in range(t0_, min(nch, t0_ + nf))])
            print("mlstm arena bytes", grp, AR.off) if dbg is not None and "arena_dbg" in dbg else None

        def evac_dup_k(ps, bp, nt, kTd, b_kTd, bias_col, src=None):
            for g_, (r0, dsts) in enumerate(((0, (0, 64)), (64, (64, 0)))):
                for dp in dsts:
                    P.op("act", (lambda r0, dp, g_: lambda e: e.activation(
                        out=kTd[dp:dp + 64, 2 * g_ + dp // 64, nt * 512:(nt + 1) * 512], in_=(src if src is not None else ps)[r0:r0 + 64, 0:512], func=AF.Identity,
                        bias=(binT[r0:r0 + 64, bias_col:bias_col + 1] if bias_col is not None else 0.0)))(r0, dp, g_),
                        reads=[bp, b_binT], writes=[b_kTd])

        def swa_phase(l, grp):
            isP = grp == "P"
            T = 512 if isP else 1024
            nch = T // 128
            AR.new_phase()
            qTs, b_qTs = AR.alloc("s_qT", [3, T], BF16)
            kTd, b_kTd = AR.alloc("s_kTd", [4, T], BF16)
            P.op("dve", lambda e: e.memset(kTd[:, :, :], 0.0), writes=[b_kTd])
            vd, b_vd = AR.alloc("s_vd", [nch, 256], BF16)
            bbc, b_bbc = load_bias_bc("s_bbc", l, 1424, 256)
            pts = [alloc_pt(7, "a"), alloc_pt(7, "b")]
            P.dma("sp", lambda e: e.dma_start(out=small[:, 1:7], in_=dram_bc(I["sink"][l, :], 6)), writes=[b_small], ds=b_small.dsem)
            P.op("act", lambda e: e.activation(out=small[:, 1:7], in_=small[:, 1:7], func=AF.Exp), reads=[b_small], writes=[b_small])
            if isP:
                kst = [AR.alloc("s_kst%d" % i, [128], F32, DS("s_kst%d" % i)) for i in range(2)]
                vst = [AR.alloc("s_vst%d" % i, [128], F32, DS("s_vst%d" % i)) for i in range(2)]
            else:
                rope, b_rope = AR.alloc("s_rope", [2048], F32, DS("s_rope"))
                P.dma("sp", lambda e: e.dma_start(out=rope, in_=I["rope"][:, :]), writes=[b_rope], ds=b_rope.dsem)
                qf = [AR.alloc("s_qf%d" % i, [512], F32) for i in range(4)]
                ckb, b_ckb = AR.alloc("s_ckb", [4, 2, 2, 64], BF16, DS("s_ckb"))
                ckT, b_ckT = AR.alloc("s_ckT", [4, 512], BF16)
                P.op("dve", lambda e: e.memset(ckT[:, :, :], 0.0), writes=[b_ckT])
                cvd, b_cvd = AR.alloc("s_cvd", [4, 2, 2, 64], BF16, DS("s_cvd"))
                fl = []
                for dup in range(2):
                    for g_ in range(2):
                        fl.append((lambda dup, g_: lambda e: e.dma_start(out=ckb[:, :, g_, dup, :], in_=I["ck_swa"][l, g_].rearrange("(c p) d -> p c d", p=128)))(dup, g_))
                P.dma("pool", fl, writes=[b_ckb], ds=b_ckb.dsem)
                fl = []
                for dup in range(2):
                    for g_ in range(2):
                        fl.append((lambda dup, g_: lambda e: e.dma_start(out=cvd[:, :, g_, dup, :], in_=I["cv_swa"][l, g_].rearrange("(c p) d -> p c d", p=128)))(dup, g_))
                P.dma("pool", fl, writes=[b_cvd], ds=b_cvd.dsem)
                for g_ in range(2):
                    ps, bp = PS()
                    psb = ps[:, :].bitcast(BF16)

                    def ctr(e, psb=psb, g_=g_):
                        for ch in range(4):
                            r = e.transpose(psb[:, ch * 128:(ch + 1) * 128], in_=ckb[:, ch, g_, :, :].rearrange("p u d -> p (u d)"), identity=identb)
                        return r
                    P.op("pe", ctr, reads=[b_ckb, b_cb], writes=[bp])
                    for hh in range(2):
                        pr = slice(hh * 64, hh * 64 + 64)
                        P.op("act", (lambda psb, g_, pr, hh: lambda e: e.activation(out=ckT[pr, 2 * g_ + hh, :], in_=psb[pr, 0:512], func=AF.Copy))(psb, g_, pr, hh),
                             reads=[bp], writes=[b_ckT])
            slot, bsl = ring_load(wpiece(I["w_in"], l, 1040, 512), 512)
            for c in range(4):
                def ev(ps, bp, nt, c=c):
                    if isP:
                        if c < 3:
                            P.op("act", lambda e: e.activation(out=qTs[:, c, nt * 512:(nt + 1) * 512], in_=ps[:, :], func=AF.Identity,
                                                               bias=binT[:, 4 + c:5 + c]), reads=[bp, b_binT], writes=[b_qTs])
                        else:
                            evac_dup_k(ps, bp, nt, kTd, b_kTd, 7)
                        return
                    (q0, bq0), (q1, bq1) = qf[2 * ((2 * c + nt) % 2):2 * ((2 * c + nt) % 2) + 2]
                    P.op("act", lambda e: e.activation(out=q0, in_=ps[:, :], func=AF.Identity, bias=binT[:, 4 + c:5 + c]),
                         reads=[bp, b_binT], writes=[bq0])
                    psw, bpw = PS()
                    P.op("pe", lambda e: e.matmul(psw[:, :], lhsT=c32[:, C_PM:C_PM + 128], rhs=q0, start=True, stop=True),
                         reads=[bq0, b_c32], writes=[bpw])
                    tsl = slice(nt * 512, (nt + 1) * 512)
                    P.op("dve", lambda e: e.tensor_tensor(out=q1, in0=psw[:, :], in1=rope[:, 1024 + nt * 512:1024 + (nt + 1) * 512], op=ALU.mult),
                         reads=[bpw, b_rope], writes=[bq1])
                    P.op("dve", lambda e: e.tensor_tensor(out=q0, in0=q0, in1=rope[:, tsl], op=ALU.mult), reads=[bq0, b_rope], writes=[bq0])
                    if c < 3:
                        P.op("dve", lambda e: e.tensor_tensor(out=qTs[:, c, tsl], in0=q0, in1=q1, op=ALU.add), reads=[bq0, bq1], writes=[b_qTs])
                    else:
                        P.op("dve", lambda e: e.tensor_tensor(out=q0, in0=q0, in1=q1, op=ALU.add), reads=[bq0, bq1], writes=[bq0])
                        evac_dup_k(None, bq0, nt, kTd, b_kTd, None, src=q0)
                fm_proj(slot, bsl, c * 128, T, ev)
            if isP:
                for tcl in range(nch):
                    def evk(ps, bp, tcl=tcl):
                        st_, bst = kst[tcl % 2]
                        P.op("dve", lambda e: e.tensor_tensor(out=st_, in0=ps[:, 0:128], in1=bbc[:, 0:128], op=ALU.add), reads=[bp, b_bbc], writes=[bst])
                        si, t0 = tcl // 2, (tcl % 2) * 128
                        P.dma("sp", lambda e: e.dma_start(out=O["o_swa_k"][si, l, :, t0:t0 + 128, :].rearrange("g t d -> t g d"),
                                                          in_=st_.rearrange("p (g d) -> p g d", g=2)), reads=[bst], ds=bst.dsem, store=True)
                    tm_proj(slot, bsl, 384, 128, tcl, evk)
            slot, bsl = ring_load(wpiece(I["w_in"], l, 1552, 128), 128)
            for tcl in range(nch):
                def evv(ps, bp, tcl=tcl):
                    o4 = vd[:, tcl, :].rearrange("p (g u d) -> p g u d", g=2, u=2)
                    i4 = apx(ps[:, 0:1], [[64, 2], [0, 2], [1, 64]])
                    b4 = apx(bbc[:, 128:129], [[64, 2], [0, 2], [1, 64]])
                    P.op("dve", lambda e: e.tensor_tensor(out=o4, in0=i4, in1=b4, op=ALU.add), reads=[bp, b_bbc], writes=[b_vd])
                    if isP:
                        st_, bst = vst[tcl % 2]
                        P.op("dve", lambda e: e.tensor_tensor(out=st_, in0=ps[:, 0:128], in1=bbc[:, 128:256], op=ALU.add), reads=[bp, b_bbc], writes=[bst])
                        si, t0 = tcl // 2, (tcl % 2) * 128
                        P.dma("sp", lambda e: e.dma_start(out=O["o_swa_v"][si, l, :, t0:t0 + 128, :].rearrange("g t d -> t g d"),
                                                          in_=st_.rearrange("p (g d) -> p g d", g=2)), reads=[bst], ds=bst.dsem, store=True)
                tm_proj(slot, bsl, 0, 128, tcl, evv)
            jobn = [0]
            jobs = []

            def mkheads(g_, q0, nq):
                hs = []
                for h in range(3 * g_, 3 * g_ + 3):
                    p0 = (h % 2) * 64
                    tcl = q0 // 128
                    hs.append(dict(q=qTs[:, h // 2, q0:q0 + nq], half=h % 2,
                                   dst=mixT[p0:p0 + 64, 2 + h // 2, q0:q0 + nq], dstb=b_mix[tcl],
                                   sink=small[:, 1 + 3 * g_:4 + 3 * g_]))
                return hs

            def vsel(vt, tcl, g_, h):
                return vt[:, tcl, 2 * g_ * 64:2 * g_ * 64 + 64] if h % 2 == 0 else vt[:, tcl, 2 * g_ * 64:2 * g_ * 64 + 128]
            for g_ in range(2):
                for n in range(nch):
                    q0 = n * 128
                    heads = mkheads(g_, q0, 128)
                    tiles = []
                    if isP:
                        kcs = [(n // 2) * 2, (n // 2) * 2 + 1]
                        msk = [None, None]
                    else:
                        kcs = [j for j in (n - 1, n, n + 1) if 0 <= j < nch]
                        msk = [maskBb if j == n - 1 else (maskFb if j == n + 1 else None) for j in kcs]
                    for j, m in zip(kcs, msk):
                        tiles.append(dict(nk=128, kT=[kTd[:, 2 * g_ + h % 2, j * 128:(j + 1) * 128] for h in range(3 * g_, 3 * g_ + 3)],
                                          v=[vsel(vd, j, g_, h) for h in range(3 * g_, 3 * g_ + 3)], mask=m))
                    reads = [b_qTs, b_kTd, b_vd]
                    if not isP:
                        cvv = cvd.rearrange("p c g u d -> p c (g u d)")
                        for ch in range(4):
                            tiles.append(dict(nk=128, kT=[ckT[:, 2 * g_ + h % 2, ch * 128:(ch + 1) * 128] for h in range(3 * g_, 3 * g_ + 3)],
                                              v=[vsel(cvv, ch, g_, h) for h in range(3 * g_, 3 * g_ + 3)]))
                        reads += [b_ckT, b_cvd]
                    jobs.append(attn_job(heads, 128, tiles, reads, pts[jobn[0] % 2]))
                    jobn[0] += 1
            run_pipelined(jobs)

        def nat_phase(l, grp):
            isP = grp == "P"
            T = 512 if isP else 1024
            nch = T // 128
            AR.new_phase()
            qTn, b_qTn = AR.alloc("n_qT", [3, T], BF16)
            kTn, b_kTn = AR.alloc("n_kT", [6, T], BF16)
            P.op("dve", lambda e: e.memset(kTn[:, :, :], 0.0), writes=[b_kTn])
            vn, b_vn = AR.alloc("n_v", [nch, 384], BF16)
            bbc, b_bbc = load_bias_bc("n_bbc", l, 2064, 768)
            pts = [alloc_pt(9, "a"), alloc_pt(9, "b")]
            if isP:
                kst = [AR.alloc("n_kst%d" % i, [384], F32, DS("n_kst%d" % i)) for i in range(2)]
                vst = [AR.alloc("n_vst%d" % i, [384], F32, DS("n_vst%d" % i)) for i in range(2)]
            else:
                ckb, b_ckb = AR.alloc("n_ckb", [4, 384], BF16, DS("n_ckb"))
                cknT, b_cknT = AR.alloc("n_cknT", [6, 512], BF16)
                P.op("dve", lambda e: e.memset(cknT[:, :, :], 0.0), writes=[b_cknT])
                cvn, b_cvn = AR.alloc("n_cvn", [4, 384], BF16, DS("n_cvn"))
                P.dma("pool", [(lambda h: lambda e: e.dma_start(out=ckb[:, :, h * 64:(h + 1) * 64],
                                                                  in_=I["ck_nat"][l, h].rearrange("(c p) d -> p c d", p=128)))(h) for h in range(6)],
                      writes=[b_ckb], ds=b_ckb.dsem)
                P.dma("pool", [(lambda h: lambda e: e.dma_start(out=cvn[:, :, h * 64:(h + 1) * 64],
                                                                  in_=I["cv_nat"][l, h].rearrange("(c p) d -> p c d", p=128)))(h) for h in range(6)],
                      writes=[b_cvn], ds=b_cvn.dsem)
                for pr_ in range(3):
                    ps, bp = PS()
                    psb = ps[:, :].bitcast(BF16)

                    def ctr(e, psb=psb, pr_=pr_):
                        for ch in range(4):
                            r = e.transpose(psb[:, ch * 128:(ch + 1) * 128], in_=ckb[:, ch, pr_ * 128:(pr_ + 1) * 128], identity=identb)
                        return r
                    P.op("pe", ctr, reads=[b_ckb, b_cb], writes=[bp])
                    for hh in range(2):
                        prr = slice(hh * 64, hh * 64 + 64)
                        P.op("act", (lambda psb, pr_, prr, hh: lambda e: e.activation(out=cknT[prr, 2 * pr_ + hh, :], in_=psb[prr, 0:512], func=AF.Copy))(psb, pr_, prr, hh),
                             reads=[bp], writes=[b_cknT])
                tab, b_tab = AR.alloc("n_tab", [6, 16, 64], BF16)
                rsb, b_rsb = AR.alloc("n_rsb", [31], F32, DS("n_rsb"))
                RT, b_RT = AR.alloc("n_RT", [90], F32)
                P.dma("sp", lambda e: e.dma_start(out=rsb[0:90, :], in_=I["rpb"][l].rearrange("h i j -> (h i) j")), writes=[b_rsb], ds=b_rsb.dsem)
                ps, bp = PS()
                P.op("pe", (lambda ps: lambda e: e.transpose(ps[0:31, 0:90], in_=rsb[0:90, :], identity=ident32[0:90, 0:90]))(ps), reads=[b_rsb, b_c32], writes=[bp])
                P.op("act", (lambda ps: lambda e: e.activation(out=RT[0:31, :], in_=ps[0:31, 0:90], func=AF.Copy, scale=8.0))(ps), reads=[bp], writes=[b_RT])
                b_tab0 = P.buf("n_tab_init", init=dict(b_tab.rs))
                AR.cur.append(b_tab0)
                P.op("dve", lambda e: e.memset(tab[:, :, :, :], 0.0), writes=[b_tab0])
                for c0 in range(0, 64, 5):
                    ncg = min(5, 64 - c0)
                    ps, bp = PS()

                    def tmm(e, ps=ps, c0=c0, ncg=ncg):
                        for ci in range(ncg):
                            c = c0 + ci
                            r = e.matmul(ps[:, ci * 90:(ci + 1) * 90], lhsT=c32[0:31, C_MSH + 63 - c:C_MSH + 63 - c + 128], rhs=RT[0:31, :],
                                         start=True, stop=True)
                        return r
                    P.op("pe", tmm, reads=[b_RT, b_c32], writes=[bp])
                    for half in range(2):
                        pr = slice(half * 64, (half + 1) * 64)
                        ii0 = 1 - half
                        o_ = apx(tab[pr, 0, ii0, c0:c0 + 1], [[16 * 64, 6], [64, 15], [1, ncg]])
                        i_ = apx(ps[pr, 0:1], [[15, 6], [1, 15], [90, ncg]])
                        m_ = apx(c32[pr, C_MN + c0:C_MN + c0 + 1], [[0, 6], [0, 15], [1, ncg]])
                        P.op("dve", (lambda o_, i_, m_: lambda e: e.tensor_tensor(out=o_, in0=i_, in1=m_, op=ALU.add))(o_, i_, m_),
                             reads=[bp, b_c32, b_tab0], writes=[b_tab])
            if (not isP) and dbg and "cknT" in dbg and l == 0:
                P.dma("pool", lambda e: e.dma_start(out=O["cknT"][:, :], in_=cknT.rearrange("p h k -> p (h k)")), reads=[b_cknT], ds=DS("dbgck"), store=True)
            if (not isP) and dbg and "tab" in dbg and l == 0:
                P.dma("pool", lambda e: e.dma_start(out=O["tab"][:, :], in_=tab.rearrange("p h i c -> p (h i c)")), reads=[b_tab], ds=DS("dbgtab"), store=True)
            slot, bsl = ring_load(wpiece(I["w_in"], l, 1680, 384), 384)
            for c in range(3):
                fm_proj(slot, bsl, c * 128, T, lambda ps, bp, nt, c=c: P.op(
                    "act", lambda e: e.activation(out=qTn[:, c, nt * 512:(nt + 1) * 512], in_=ps[:, :], func=AF.Identity,
                                                  bias=binT[:, 8 + c:9 + c]), reads=[bp, b_binT], writes=[b_qTn]))
            slot, bsl = ring_load(wpiece(I["w_in"], l, 2064, 384), 384)
            for c in range(3):
                def evkn(ps, bp, nt, c=c):
                    for hh in range(2):
                        prr = slice(hh * 64, hh * 64 + 64)
                        P.op("act", (lambda prr, hh: lambda e: e.activation(
                            out=kTn[prr, 2 * c + hh, nt * 512:(nt + 1) * 512], in_=ps[prr, :], func=AF.Identity,
                            bias=binT[prr, 11 + c:12 + c]))(prr, hh), reads=[bp, b_binT], writes=[b_kTn])
                fm_proj(slot, bsl, c * 128, T, evkn)
            if isP:
                for tcl in range(nch):
                    def evk(ps, bp, tcl=tcl):
                        st_, bst = kst[tcl % 2]
                        P.op("dve", lambda e: e.tensor_tensor(out=st_, in0=ps[:, 0:384], in1=bbc[:, 0:384], op=ALU.add), reads=[bp, b_bbc], writes=[bst])
                        si, t0 = tcl // 2, (tcl % 2) * 128
                        P.dma("sp", lambda e: e.dma_start(out=O["o_nat_k"][si, l, :, t0:t0 + 128, :].rearrange("h t d -> t h d"),
                                                          in_=st_.rearrange("p (h d) -> p h d", h=6)), reads=[bst], ds=bst.dsem, store=True)
                    tm_proj(slot, bsl, 0, 384, tcl, evk)
            slot, bsl = ring_load(wpiece(I["w_in"], l, 2448, 384), 384)
            for tcl in range(nch):
                def evv(ps, bp, tcl=tcl):
                    P.op("dve", lambda e: e.tensor_tensor(out=vn[:, tcl, :], in0=ps[:, 0:384], in1=bbc[:, 384:768], op=ALU.add), reads=[bp, b_bbc], writes=[b_vn])
                    if isP:
                        st_, bst = vst[tcl % 2]
                        P.op("dve", lambda e: e.tensor_tensor(out=st_, in0=ps[:, 0:384], in1=bbc[:, 384:768], op=ALU.add), reads=[bp, b_bbc], writes=[bst])
                        si, t0 = tcl // 2, (tcl % 2) * 128
                        P.dma("sp", lambda e: e.dma_start(out=O["o_nat_v"][si, l, :, t0:t0 + 128, :].rearrange("h t d -> t h d"),
                                                          in_=st_.rearrange("p (h d) -> p h d", h=6)), reads=[bst], ds=bst.dsem, store=True)
                tm_proj(slot, bsl, 0, 384, tcl, evv)
            def vsel(vt, tcl, h):
                return vt[:, tcl, h * 64:(h + 1) * 64] if h % 2 == 0 else vt[:, tcl, (h - 1) * 64:(h + 1) * 64]

            def hp(h):
                return slice((h % 2) * 64, (h % 2) * 64 + 64)
            jn = 0
            jobs = []
            if isP:
                for si in range(2):
                    for hb in range(2):
                        hl = list(range(3 * hb, 3 * hb + 3))
                        for n in range(2):
                            q0 = si * 256 + n * 128
                            heads = [dict(q=qTn[:, h // 2, q0:q0 + 128], half=h % 2, dst=mixT[hp(h), 5 + h // 2, q0:q0 + 128],
                                          dstb=b_mix[q0 // 128]) for h in hl]
                            tiles = [dict(nk=128, kT=[kTn[:, h, j * 128:(j + 1) * 128] for h in hl], v=[vsel(vn, j, h) for h in hl])
                                     for j in (si * 2, si * 2 + 1)]
                            jobs.append(attn_job(heads, 128, tiles, [b_qTn, b_kTn, b_vn], pts[jn % 2]))
                            jn += 1
            else:
                hl = list(range(6))
                for r in range(16):
                    q0 = r * 64
                    r0 = min(max(r - 4, 0), 8)
                    heads = [dict(q=qTn[:, h // 2, q0:q0 + 64], half=h % 2, dst=mixT[hp(h), 5 + h // 2, q0:q0 + 64],
                                  dstb=b_mix[q0 // 128], dbgflag=(r == 0 and l == 0)) for h in hl]
                    tiles = []
                    for jj in range(r0 // 2, (r0 + 7) // 2 + 1):
                        ev_ok = r0 <= 2 * jj <= r0 + 7
                        od_ok = r0 <= 2 * jj + 1 <= r0 + 7
                        vr = (0 if ev_ok else 64, 128 if od_ok else 64)
                        ii = 2 * jj - r + 8
                        ii = min(max(ii, 0), 15)
                        tiles.append(dict(nk=128, kT=[kTn[:, h, jj * 128:(jj + 1) * 128] for h in hl], v=[vsel(vn, jj, h) for h in hl],
                                          bias=tab[:, :, ii, :], vrows=vr))
                    for ch in range(4):
                        tiles.append(dict(nk=128, kT=[cknT[:, h, ch * 128:(ch + 1) * 128] for h in hl], v=[vsel(cvn, ch, h) for h in hl]))
                    jobs.append(attn_job(heads, 64, tiles, [b_qTn, b_kTn, b_vn, b_cknT, b_cvn, b_tab], pts[jn % 2]))
                    jn += 1
            run_pipelined(jobs)

        def make_bc_from_mod(name, l, which, v):
            t, b = AR.alloc(name, [D], F32)
            for half in range(2):
                ps, bp = PS()

                def mm(e, ps=ps, half=half):
                    for kk in range(4):
                        k = half * 4 + kk
                        r = e.matmul(ps[:, kk * 128:(kk + 1) * 128], lhsT=apx(modT[:, l, which, k, v:v + 1], [[0, 128]]), rhs=ident32,
                                     start=True, stop=True)
                    return r
                P.op("pe", mm, reads=[b_modl[l], b_c32], writes=[bp])
                P.op("act", (lambda ps, half: lambda e: e.activation(out=t[:, half * 512:(half + 1) * 512], in_=ps[:, :], func=AF.Copy))(ps, half),
                     reads=[bp], writes=[b])
            return t, b

        def load_ln_bc(name, src, l, scale):
            t, b = AR.alloc(name, [D], F32, DS(name))
            P.dma("sp", lambda e: e.dma_start(out=t, in_=dram_bc(src[l, :], D)), writes=[b], ds=b.dsem)
            if scale != 1.0:
                P.op("act", lambda e: e.activation(out=t, in_=t, func=AF.Copy, scale=scale), reads=[b], writes=[b])
            return t, b

        def acc_into_xa(tc, half, ps, bp, g_bc, bg, tmp):
            t_, bt = tmp
            hs = slice(half * 512, (half + 1) * 512)
            P.op("dve", lambda e: e.tensor_tensor(out=t_, in0=ps[:, :], in1=g_bc[:, hs], op=ALU.mult), reads=[bp, bg], writes=[bt])
            P.op("dve", lambda e: e.tensor_tensor(out=xa[:, tc, hs], in0=xa[:, tc, hs], in1=t_, op=ALU.add), reads=[bt, b_xa[tc]], writes=[b_xa[tc]])

        def ln_inplace(tc, lg, blg, lb, blb, wk, out_dram=None):
            (nh_, bnh), (sm, bsm) = wk
            y = xa[:, tc, :]
            by = b_xa[tc]
            st6 = sm[:, 0:12].rearrange("p (a b) -> p a b", a=2)
            for half in range(2):
                P.op("dve", (lambda half: lambda e: e.bn_stats(out=st6[:, half, :], in_=y[:, half * 512:(half + 1) * 512]))(half), reads=[by], writes=[bsm])
            mv = sm[:, 12:14]
            P.op("dve", lambda e: e.bn_aggr(out=mv, in_=sm[:, 0:12]), reads=[bsm], writes=[bsm])
            rs_, nm_ = sm[:, 14:15], sm[:, 15:16]
            P.op("act", lambda e: e.activation(out=rs_, in_=sm[:, 13:14], func=AF.Sqrt, bias=small[:, 0:1]), reads=[bsm, b_small], writes=[bsm])
            P.op("dve", lambda e: e.reciprocal(out=rs_, in_=rs_), reads=[bsm], writes=[bsm])
            P.op("dve", lambda e: e.scalar_tensor_tensor(out=nm_, in0=sm[:, 12:13], scalar=-1.0, in1=rs_, op0=ALU.mult, op1=ALU.mult), reads=[bsm], writes=[bsm])
            P.op("act", lambda e: e.activation(out=nh_, in_=y, func=AF.Identity, scale=rs_, bias=nm_), reads=[by, bsm], writes=[bnh])
            P.op("dve", lambda e: e.tensor_tensor(out=nh_, in0=nh_, in1=lg, op=ALU.mult), reads=[bnh, blg], writes=[bnh])
            P.op("dve", lambda e: e.tensor_tensor(out=xa[:, tc, :], in0=nh_, in1=lb, op=ALU.add), reads=[bnh, blb], writes=[b_xa[tc]])
            if out_dram is not None:
                P.dma("sp", lambda e: e.dma_start(out=out_dram, in_=xa[:, tc, :]), reads=[b_xa[tc]], ds=b_xa[tc].dsem, store=True)

        def ln_work(tag):
            return [(AR.alloc("nh%s%d" % (tag, i), [D], F32), AR.alloc("lsm%s%d" % (tag, i), [16], F32)) for i in range(2)]

        def wout_phase(l, grp):
            isP = grp == "P"
            v = 0 if isP else 1
            chunks = list(range(0, 4)) if isP else list(range(4, 12))
            AR.new_phase()
            ga, bga = make_bc_from_mod("ga_bc", l, 2, v)
            lg, blg = load_ln_bc("ln1g", I["ln1_g"], l, ALPHA)
            lb, blb = load_ln_bc("ln1b", I["ln1_b"], l, ALPHA)
            wk = ln_work("a")
            tmps = [AR.alloc("atmp%d" % i, [512], F32) for i in range(2)]
            slots = [ring_load(wpiece(I["w_out"], l, nh * 512, 512), 512) for nh in range(2)]
            for i, tc in enumerate(chunks):
                for nh in range(2):
                    ps, bp = PS()
                    slot, bsl = slots[nh]

                    def mm(e, ps=ps, slot=slot, i=i):
                        for k in range(KC):
                            r = e.matmul(ps[:, :], lhsT=mixT[:, k, i * 128:(i + 1) * 128], rhs=slot[:, k, :], start=(k == 0), stop=(k == KC - 1))
                        return r
                    P.op("pe", mm, reads=[bsl, b_mix[i]], writes=[bp])
                    acc_into_xa(tc, nh, ps, bp, ga, bga, tmps[nh])
                ln_inplace(tc, lg, blg, lb, blb, wk[i % 2])

        def mlp_phase(l, last):
            a2 = 1.0 if last else ALPHA
            NT = 768
            AR.new_phase()
            hid, b_hid = AR.alloc("hid", [32, NT], BF16)
            gms = {v: make_bc_from_mod("gm_bc%d" % v, l, 5, v) for v in (0, 1)}
            lg, blg = load_ln_bc("ln2g", I["ln2_g"], l, a2)
            lb, blb = load_ln_bc("ln2b", I["ln2_b"], l, a2)
            wk = ln_work("m")
            rl = [AR.alloc("rl%d" % i, [NT], F32) for i in range(2)]
            tmps = [(r_[:, 0:512], br) for (r_, br) in rl]
            rows_m = alloc_rows("m") if l + 1 < nlayers else None
            groups = [list(range(0, 6)), list(range(6, 12))]

            def do_hT(gi):
                if gi == 0:
                    transposes_to_hT(l, 2, 3, [0, 1, 2, 3], 0, 0)
                    transposes_to_hT(l, 2, 3, [4, 5], 1, 4)
                else:
                    transposes_to_hT(l, 2, 3, [6, 7, 8, 9], 1, 0)
                    transposes_to_hT(l, 2, 3, [10, 11], 1, 4)

            def do_ln(tc, i):
                od = None
                if last:
                    od = O["y_p"][tc * 128:(tc + 1) * 128, :] if tc < 4 else O["y_s"][(tc - 4) * 128:(tc - 3) * 128, :]
                ln_inplace(tc, lg, blg, lb, blb, wk[i % 2], out_dram=od)

            def do_S1(gi, after_piece):
                for pc in range(8):
                    slot, bsl = ring_load(wpiece(I["w_mlp1"], l, pc * 512, 512), 512)
                    for fq in range(4):
                        fc = pc * 4 + fq
                        psA, bpA = PS()
                        psB, bpB = PS()

                        def mm(e, psA=psA, psB=psB, slot=slot, fq=fq):
                            for k in range(KC):
                                e.matmul(psA[:, :], lhsT=slot[:, k, fq * 128:(fq + 1) * 128], rhs=hT[:, k, 0:512], start=(k == 0), stop=(k == KC - 1))
                            for k in range(KC):
                                r = e.matmul(psB[:, 0:256], lhsT=slot[:, k, fq * 128:(fq + 1) * 128], rhs=hT[:, k, 512:768], start=(k == 0), stop=(k == KC - 1))
                            return r
                        P.op("pe", mm, reads=[bsl] + b_hT[0:6], writes=[bpA, bpB])
                        r_, br = rl[fc % 2]
                        P.op("act", (lambda psA, r_: lambda e: e.activation(out=r_[:, 0:512], in_=psA[:, :], func=AF.Relu))(psA, r_), reads=[bpA], writes=[br])
                        P.op("act", (lambda psB, r_: lambda e: e.activation(out=r_[:, 512:768], in_=psB[:, 0:256], func=AF.Relu))(psB, r_), reads=[bpB], writes=[br])
                        P.op("dve", (lambda r_, fc: lambda e: e.tensor_tensor(out=hid[:, fc, :], in0=r_[:, 0:NT], in1=r_[:, 0:NT], op=ALU.mult))(r_, fc),
                             reads=[br], writes=[b_hid])
                    after_piece(pc)

            def do_S2(gi):
                chunks = groups[gi]
                for nh in range(2):
                    accs = [PS() for _ in range(6)]
                    for pc in range(4):
                        slot, bsl = ring_load(wpiece(I["w_mlp2"], l, nh * 512, 512, r0=pc * 1024), 512)

                        def mm(e, slot=slot, pc=pc, accs=accs):
                            for i in range(6):
                                for k in range(KC):
                                    r = e.matmul(accs[i][0][:, :], lhsT=hid[:, pc * 8 + k, i * 128:(i + 1) * 128], rhs=slot[:, k, :],
                                                 start=(pc == 0 and k == 0), stop=(pc == 3 and k == KC - 1))
                            return r
                        P.op("pe", mm, reads=[bsl, b_hid], writes=[a[1] for a in accs])
                    for i, tc in enumerate(chunks):
                        gm, bgm = gms[0 if tc < 4 else 1]
                        acc_into_xa(tc, nh, accs[i][0], accs[i][1], gm, bgm, tmps[i % 2])

            def ada_cb(base):
                def cb(pc):
                    ja = base + pc
                    if rows_m is not None and ja < 12:
                        ada_piece(l + 1, ja, rows_m)
                return cb

            do_hT(0)
            do_S1(0, ada_cb(0))
            do_S2(0)
            do_hT(1)
            cb1 = ada_cb(8)

            def cbB(pc):
                cb1(pc)
                if pc < 6:
                    do_ln(groups[0][pc], pc)
            do_S1(1, cbB)
            do_S2(1)
            for i, tc in enumerate(groups[1]):
                do_ln(tc, i)

        dbg_ds = P.new_dsem("dbg")

        def dump_mix(name, ncols):
            if dbg and name in dbg:
                P.dma("pool", lambda e: e.dma_start(out=O[name][:, :, :], in_=mixT[:, :, 0:ncols]), reads=b_mix, ds=dbg_ds, store=True)

        def dump_xa(name):
            if dbg and name in dbg:
                P.dma("sp", lambda e: e.dma_start(out=O[name][:, :, :], in_=xa[:, :, :]), reads=b_xa, ds=dbg_ds, store=True)

        def chk(name):
            if stop == name:
                raise _Stop()

        try:
          for l in range(nlayers):
              chk('ada')
              with nc.allow_non_contiguous_dma(reason="per-partition bias layout"):
                  fl = []
                  for (c0, n, col) in ((0, 4, 0), (1040, 4, 4), (1680, 6, 8)):
                      srcb = I["b_in"][l, c0:c0 + n * 128].rearrange("(c p) -> p c", p=128)
                      fl.append((lambda srcb, n, col: lambda e: e.dma_start(out=binT[:, col:col + n], in_=srcb))(srcb, n, col))
                  P.dma("sp", fl, writes=[b_binT], ds=b_binT.dsem)
                  fl = []
                  for (c0, col) in ((1024, 0), (1032, 32), (1028, 64), (1036, 96)):
                      srcw = I["w_in"][l, :, c0:c0 + 4].rearrange("(k p) n -> p k n", p=128)
                      fl.append((lambda srcw, col: lambda e: e.dma_start(out=gw[:, :, col:col + 4], in_=srcw))(srcw, col))
                  P.dma("pool", fl, writes=[b_gw], ds=b_gw.dsem)
              P.op("act", lambda e: e.activation(out=binT[:, 2:4], in_=binT[:, 2:4], func=AF.Copy, scale=0.125), reads=[b_binT], writes=[b_binT])
              for (dst_i, src_i) in ((0, 1), (2, 4)):
                  P.op("dve", (lambda dst_i, src_i, l=l: lambda e: e.tensor_scalar(out=pv[:, dst_i, :, :], in0=modT[:, l, src_i, :, :], scalar1=1.0,
                                                                              scalar2=1.0 / ALPHA, op0=ALU.add, op1=ALU.mult))(dst_i, src_i),
                       reads=[b_modl[l]], writes=[b_pv])
              for grp in ("P", "S"):
                  chunks = list(range(0, 4)) if grp == "P" else list(range(4, 12))
                  transposes_to_hT(l, 0, 0, chunks, 0 if grp == "P" else 1)
                  chk('hT' + grp)
                  mlstm_phase(l, grp)
                  chk('ml' + grp)
                  swa_phase(l, grp)
                  chk('swa' + grp)
                  nat_phase(l, grp)
                  chk('nat' + grp)
                  if l == 0:
                      dump_mix("mix" + grp, 512 if grp == "P" else 1024)
                  wout_phase(l, grp)
                  chk('wout' + grp)
              if l == 0:
                  dump_xa("xa_ln1")
              mlp_phase(l, l == nlayers - 1)
              if l == 0:
                  dump_xa("xa_l0")

        except _Stop:
            pass

        fin = {}
        for s_, v_ in P.stores:
            fin[s_] = max(fin.get(s_, 0), v_)
        P._wait(P.q["sp"], fin)
        P.n_items = {k: len(q.items) for k, q in P.q.items()}
        with nc.Block() as block:
            @block.tensor
            def _(e):
                P.replay("pe", e)

            @block.scalar
            def _(e):
                P.replay("act", e)

            @block.vector
            def _(e):
                P.replay("dve", e)

            @block.gpsimd
            def _(e):
                with nc.allow_non_contiguous_dma(reason="small strided loads"):
                    P.replay("pool", e)

            @block.sync
            def _(e):
                with nc.allow_non_contiguous_dma(reason="small strided loads"):
                    P.replay("sp", e)
        nc._prog_stats = (P.n_items, P.nsem)
    return nc


def shard_inputs(inputs, ncores=8):
    consts, rope = make_consts()
    f = lambda a: np.ascontiguousarray(np.asarray(a, dtype=np.float32))
    maps = []
    for c in range(ncores):
        m = {
            "xp": f(inputs["x_prompt"][2 * c:2 * c + 2]).reshape(512, D),
            "xs": f(inputs["x_sample"][c]),
            "ck_swa": f(inputs["cache_swa_k"][c]), "cv_swa": f(inputs["cache_swa_v"][c]),
            "ck_nat": f(inputs["cache_nat_k"][c]), "cv_nat": f(inputs["cache_nat_v"][c]),
            "stC": f(inputs["state_mlstm_C"][c]), "stn": f(inputs["state_mlstm_n"][c]), "stm": f(inputs["state_mlstm_m"][c]),
            "cvec": f(np.stack([np.asarray(inputs["c_ctx"]), np.asarray(inputs["c"][c])], 0)),
            "w_ada": f(inputs["w_ada"]), "b_ada": f(inputs["b_ada"]), "w_in": f(inputs["w_in"]), "b_in": f(inputs["b_in"]),
            "fbias": f(inputs["mlstm_fbias"]), "norm_g": f(inputs["mlstm_norm_g"]), "sink": f(inputs["swa_sink"]), "rpb": f(inputs["nat_rpb"]),
            "w_out": f(inputs["w_out"]), "ln1_g": f(inputs["ln1_g"]), "ln1_b": f(inputs["ln1_b"]),
            "w_mlp1": f(inputs["w_mlp1"]), "w_mlp2": f(inputs["w_mlp2"]), "ln2_g": f(inputs["ln2_g"]), "ln2_b": f(inputs["ln2_b"]),
            "consts": consts, "rope": rope,
        }
        maps.append(m)
    return maps


_NC_CACHE = {}


def kernel(**inputs):
    if "nc" not in _NC_CACHE:
        _NC_CACHE["nc"] = build_program()
    nc = _NC_CACHE["nc"]
    maps = shard_inputs(inputs, 8)
    res = run_bass_kernel_spmd(nc, maps, core_ids=list(range(8)))
    R = res.results
    cat = lambda k: np.concatenate([np.asarray(r[k]) for r in R], axis=0)
    y_p = cat("y_p").reshape(16, 256, D)
    y_s = np.stack([np.asarray(r["y_s"]) for r in R], 0)
    outs = [y_p, y_s]
    for k in ("o_swa_k", "o_swa_v", "o_nat_k", "o_nat_v", "o_C", "o_n", "o_m"):
        outs.append(cat(k))
    return tuple(np.ascontiguousarray(o.astype(np.float32)) for o in outs)
```
